# Optimizing a Trainium2 kernel written in Bass

```python
import math
import jax, jax.numpy as jnp
from jax import lax
import numpy as np

D_MODEL = 1024
BATCH = 4
SEQ = 8192
DEPTH = 2

N_EVEN = (DEPTH + 1) // 2
N_ODD = DEPTH // 2

LRU_WIDTH = D_MODEL
LRU_HEADS = 8
LRU_BLOCK = LRU_WIDTH // LRU_HEADS
CONV_WIDTH = 4
LRU_C = 8.0
SB_HEADS = 8
SB_HEAD_DIM = 128
SB_WIDTH = SB_HEADS * SB_HEAD_DIM
Q_BLOCK = 128
IN_EVEN = 2 * LRU_WIDTH + 4 * SB_WIDTH
OUT_EVEN = LRU_WIDTH + SB_WIDTH
S5_WIDTH = D_MODEL
S5_GROUP = 16
S5_GROUPS = S5_WIDTH // S5_GROUP
S5_STATE = 64
IN_ODD = 2 * S5_WIDTH

EPS = 1e-6

kernel_name = "hybrid_rglru_stickbreak_s5_trunk"


def rms_norm(x, g):
    x32 = x.astype(jnp.float32)
    y = x32 * lax.rsqrt(jnp.mean(x32 * x32, axis=-1, keepdims=True) + EPS)
    return (y * g.astype(jnp.float32)).astype(x.dtype)


def ada_modulate(x, c, g, w, b):
    mod = jax.nn.silu(c) @ w + b
    shift, scale, gate = jnp.split(mod, 3, axis=-1)
    h = rms_norm(x, g) * (1 + scale[:, None, :]) + shift[:, None, :]
    return h, gate[:, None, :]


def causal_depthwise_conv(x, w, b):
    width = x.shape[-1]
    y = lax.conv_general_dilated(
        x, w[:, None, :].astype(x.dtype), window_strides=(1,),
        padding=[(CONV_WIDTH - 1, 0)], dimension_numbers=("NWC", "WIO", "NWC"),
        feature_group_count=width)
    return y + b


def _linear_combine(e1, e2):
    a1, b1 = e1
    a2, b2 = e2
    return (a2 * a1, a2 * b1 + b2)


def rg_lru(x, wr, br, wi, bi, lam):
    bsz, slen, width = x.shape
    x32 = x.astype(jnp.float32)
    xh = x32.reshape(bsz, slen, LRU_HEADS, LRU_BLOCK)
    r = jax.nn.sigmoid(jnp.einsum("bshi,hij->bshj", xh, wr.astype(jnp.float32)).reshape(bsz, slen, width) + br)
    i = jax.nn.sigmoid(jnp.einsum("bshi,hij->bshj", xh, wi.astype(jnp.float32)).reshape(bsz, slen, width) + bi)
    log_a = LRU_C * r * jax.nn.log_sigmoid(lam.astype(jnp.float32))
    a = jnp.exp(log_a)
    b = jnp.sqrt(-jnp.expm1(2.0 * log_a)) * (i * x32)
    _, h = lax.associative_scan(_linear_combine, (a, b), axis=1)
    return h.astype(x.dtype)


def stick_breaking_attention(q, k, v):
    slen, dh = q.shape[1], q.shape[-1]
    q32 = q.astype(jnp.float32) * (dh ** -0.5)
    k32 = k.astype(jnp.float32)
    v32 = v.astype(jnp.float32)
    outs = []
    for blk in range(slen // Q_BLOCK):
        start = blk * Q_BLOCK
        end = start + Q_BLOCK
        qb = q32[:, start:end]
        kp = k32[:, :end]
        vp = v32[:, :end]
        z = jnp.einsum("bqhd,bkhd->bhqk", qb, kp)
        t_idx = start + jnp.arange(Q_BLOCK)[:, None]
        s_idx = jnp.arange(end)[None, :]
        mask = s_idx < t_idx
        log_keep = jnp.where(mask, jax.nn.log_sigmoid(-z), 0.0)
        later = lax.cumsum(log_keep, axis=3, reverse=True) - log_keep
        w = jnp.where(mask, jnp.exp(jax.nn.log_sigmoid(z) + later), 0.0)
        outs.append(jnp.einsum("bhqk,bkhd->bqhd", w, vp))
    return jnp.concatenate(outs, axis=1).astype(v.dtype)


def s5_ssm(u, lam_re, lam_im, log_dt, b_re, b_im, c_re, c_im, d_skip):
    bsz, slen, width = u.shape
    u32 = u.astype(jnp.float32)
    dt = jnp.exp(log_dt.astype(jnp.float32))[:, None]
    lam_re = lam_re.astype(jnp.float32)
    lam_im = lam_im.astype(jnp.float32)
    decay = jnp.exp(lam_re * dt)
    ang = lam_im * dt
    abar_re = decay * jnp.cos(ang)
    abar_im = decay * jnp.sin(ang)
    den = lam_re * lam_re + lam_im * lam_im
    num_re = abar_re - 1.0
    coef_re = (num_re * lam_re + abar_im * lam_im) / den
    coef_im = (abar_im * lam_re - num_re * lam_im) / den
    b_re = b_re.astype(jnp.float32)
    b_im = b_im.astype(jnp.float32)
    bbar_re = coef_re[..., None] * b_re - coef_im[..., None] * b_im
    bbar_im = coef_re[..., None] * b_im + coef_im[..., None] * b_re
    ug = u32.reshape(bsz, slen, S5_GROUPS, S5_GROUP)
    bu_re = jnp.einsum("bsgc,gpc->bsgp", ug, bbar_re)
    bu_im = jnp.einsum("bsgc,gpc->bsgp", ug, bbar_im)
    a_re = jnp.broadcast_to(abar_re, (1, slen, S5_GROUPS, S5_STATE))
    a_im = jnp.broadcast_to(abar_im, (1, slen, S5_GROUPS, S5_STATE))

    def complex_combine(e1, e2):
        a1r, a1i, b1r, b1i = e1
        a2r, a2i, b2r, b2i = e2
        return (a2r * a1r - a2i * a1i,
                a2r * a1i + a2i * a1r,
                a2r * b1r - a2i * b1i + b2r,
                a2r * b1i + a2i * b1r + b2i)

    _, _, h_re, h_im = lax.associative_scan(complex_combine, (a_re, a_im, bu_re, bu_im), axis=1)
    y = (jnp.einsum("bsgp,gcp->bsgc", h_re, c_re.astype(jnp.float32))
         - jnp.einsum("bsgp,gcp->bsgc", h_im, c_im.astype(jnp.float32)))
    y = y.reshape(bsz, slen, width) + d_skip.astype(jnp.float32) * u32
    return y.astype(u.dtype)


def even_mixer(h, w_in, conv_w, conv_b, wr, br, wi, bi, lam, q_g, k_g, w_out):
    bsz, slen, _ = h.shape
    proj = h @ w_in
    o1 = LRU_WIDTH
    o2 = 2 * LRU_WIDTH
    xa, ga, q, k, v, gb = jnp.split(
        proj, [o1, o2, o2 + SB_WIDTH, o2 + 2 * SB_WIDTH, o2 + 3 * SB_WIDTH], axis=-1)
    ya = rg_lru(causal_depthwise_conv(xa, conv_w, conv_b), wr, br, wi, bi, lam) * jax.nn.silu(ga)
    q = rms_norm(q.reshape(bsz, slen, SB_HEADS, SB_HEAD_DIM), q_g)
    k = rms_norm(k.reshape(bsz, slen, SB_HEADS, SB_HEAD_DIM), k_g)
    v = v.reshape(bsz, slen, SB_HEADS, SB_HEAD_DIM)
    yb = stick_breaking_attention(q, k, v).reshape(bsz, slen, SB_WIDTH) * jax.nn.silu(gb)
    return jnp.concatenate([ya, yb], axis=-1) @ w_out


def odd_mixer(h, w_in, lam_re, lam_im, log_dt, b_re, b_im, c_re, c_im, d_skip, glu_w, glu_b, w_out):
    u, g = jnp.split(h @ w_in, 2, axis=-1)
    y = jax.nn.gelu(s5_ssm(u, lam_re, lam_im, log_dt, b_re, b_im, c_re, c_im, d_skip))
    y = y * jax.nn.sigmoid(y @ glu_w + glu_b)
    return (y * jax.nn.silu(g)) @ w_out


def setup_inputs(seed: int = 0) -> dict:
    key = jax.random.key(seed)
    ks = jax.random.split(key, 32)
    f32 = jnp.float32
    nrm = lambda k, shape, s: jax.random.normal(k, shape, f32) * s
    a0 = jax.random.uniform(ks[12], (N_EVEN, LRU_WIDTH), f32, 0.9, 0.999)
    sig = a0 ** (1.0 / LRU_C)
    lru_lambda = jnp.log(sig) - jnp.log1p(-sig)
    n_idx = jnp.arange(S5_STATE, dtype=f32)
    return {
        "x": nrm(ks[0], (BATCH, SEQ, D_MODEL), 1.0),
        "c": nrm(ks[1], (BATCH, D_MODEL), 1.0),
        "norm_g": 1.0 + nrm(ks[2], (DEPTH, D_MODEL), 0.02),
        "ada_w": nrm(ks[3], (DEPTH, D_MODEL, 3 * D_MODEL), 0.5 * D_MODEL ** -0.5),
        "ada_b": nrm(ks[4], (DEPTH, 3 * D_MODEL), 0.01),
        "w_in_even": nrm(ks[5], (N_EVEN, D_MODEL, IN_EVEN), D_MODEL ** -0.5),
        "conv_w": nrm(ks[6], (N_EVEN, CONV_WIDTH, LRU_WIDTH), CONV_WIDTH ** -0.5),
        "conv_b": nrm(ks[7], (N_EVEN, LRU_WIDTH), 0.01),
        "lru_wr": nrm(ks[8], (N_EVEN, LRU_HEADS, LRU_BLOCK, LRU_BLOCK), LRU_BLOCK ** -0.5),
        "lru_br": nrm(ks[9], (N_EVEN, LRU_WIDTH), 0.01),
        "lru_wi": nrm(ks[10], (N_EVEN, LRU_HEADS, LRU_BLOCK, LRU_BLOCK), LRU_BLOCK ** -0.5),
        "lru_bi": nrm(ks[11], (N_EVEN, LRU_WIDTH), 0.01),
        "lru_lambda": lru_lambda,
        "q_norm_g": 1.0 + nrm(ks[13], (N_EVEN, SB_HEAD_DIM), 0.02),
        "k_norm_g": 1.0 + nrm(ks[14], (N_EVEN, SB_HEAD_DIM), 0.02),
        "w_out_even": nrm(ks[15], (N_EVEN, OUT_EVEN, D_MODEL), OUT_EVEN ** -0.5),
        "w_in_odd": nrm(ks[16], (N_ODD, D_MODEL, IN_ODD), D_MODEL ** -0.5),
        "s5_lambda_re": -0.5 + nrm(ks[17], (N_ODD, S5_GROUPS, S5_STATE), 0.01),
        "s5_lambda_im": math.pi * n_idx + nrm(ks[18], (N_ODD, S5_GROUPS, S5_STATE), 0.01),
        "s5_log_dt": jax.random.uniform(ks[19], (N_ODD, S5_GROUPS), f32, math.log(1e-3), math.log(1e-1)),
        "s5_b_re": nrm(ks[20], (N_ODD, S5_GROUPS, S5_STATE, S5_GROUP), (2 * S5_GROUP) ** -0.5),
        "s5_b_im": nrm(ks[21], (N_ODD, S5_GROUPS, S5_STATE, S5_GROUP), (2 * S5_GROUP) ** -0.5),
        "s5_c_re": nrm(ks[22], (N_ODD, S5_GROUPS, S5_GROUP, S5_STATE), S5_STATE ** -0.5),
        "s5_c_im": nrm(ks[23], (N_ODD, S5_GROUPS, S5_GROUP, S5_STATE), S5_STATE ** -0.5),
        "s5_d": nrm(ks[24], (N_ODD, S5_WIDTH), 1.0),
        "glu_w": nrm(ks[25], (N_ODD, S5_WIDTH, S5_WIDTH), S5_WIDTH ** -0.5),
        "glu_b": nrm(ks[26], (N_ODD, S5_WIDTH), 0.01),
        "w_out_odd": nrm(ks[27], (N_ODD, S5_WIDTH, D_MODEL), S5_WIDTH ** -0.5),
    }


def reference(x, c, norm_g, ada_w, ada_b, w_in_even, conv_w, conv_b, lru_wr, lru_br,
              lru_wi, lru_bi, lru_lambda, q_norm_g, k_norm_g, w_out_even, w_in_odd,
              s5_lambda_re, s5_lambda_im, s5_log_dt, s5_b_re, s5_b_im, s5_c_re, s5_c_im,
              s5_d, glu_w, glu_b, w_out_odd):
    for layer in range(DEPTH):
        h, gate = ada_modulate(x, c, norm_g[layer], ada_w[layer], ada_b[layer])
        j = layer // 2
        if layer % 2 == 0:
            out = even_mixer(h, w_in_even[j], conv_w[j], conv_b[j], lru_wr[j], lru_br[j],
                             lru_wi[j], lru_bi[j], lru_lambda[j], q_norm_g[j], k_norm_g[j],
                             w_out_even[j])
        else:
            out = odd_mixer(h, w_in_odd[j], s5_lambda_re[j], s5_lambda_im[j], s5_log_dt[j],
                            s5_b_re[j], s5_b_im[j], s5_c_re[j], s5_c_im[j], s5_d[j],
                            glu_w[j], glu_b[j], w_out_odd[j])
        x = x + gate * out
    return x
```

```python
import numpy as np
import ml_dtypes
from contextlib import ExitStack
import concourse.bass as bass
import concourse.mybir as mybir
from concourse.bass_utils import run_bass_kernel_spmd

F32 = mybir.dt.float32
BF16 = mybir.dt.bfloat16
AF = mybir.ActivationFunctionType
ALU = mybir.AluOpType

S = 8192
D = 1024
EPS = 1e-6


class Src:
    def __init__(self, sem):
        self.sem = sem


class Buf:
    def __init__(self, t):
        self.t = t
        self.wr = None
        self.rd = {}
        self.psum = False

    def __getitem__(self, idx):
        return self.t[idx]


class Q:
    def __init__(self, ctx, eng, name, is_pe=False):
        self.ctx = ctx
        self.eng = eng
        self.src = Src(ctx.sem("q_" + name))
        self.cnt = 0
        self.seen = {}
        self.is_pe = is_pe

    def wait(self, toks):
        for t in toks:
            if t is None:
                continue
            src, val = t
            if self.is_pe and src is self.src:
                continue
            if self.seen.get(src, 0) >= val:
                continue
            self.eng.wait_ge(src.sem, val)
            self.seen[src] = val

    def sig(self, inst):
        self.cnt += 1
        inst.then_inc(self.src.sem, 1)
        return (self.src, self.cnt)


class Ctx:
    def __init__(self, nc, es):
        self.nc = nc
        self.es = es
        self.n = 0
        self.PE = Q(self, nc.tensor, "pe", is_pe=True)
        self.ACT = Q(self, nc.scalar, "act")
        self.DVE = Q(self, nc.vector, "dve")
        self.POOL = Q(self, nc.gpsimd, "pool")
        self.SP = Q(self, nc.sync, "sp")
        self.queues = [self.PE, self.ACT, self.DVE, self.POOL, self.SP]
        self.dsem = {}
        for q, tag in ((self.SP, "s"), (self.POOL, "g"), (self.ACT, "a")):
            self.dsem[q] = [[Src(self.sem("d%s%d" % (tag, i))), 0] for i in range(12 if q is not self.ACT else 6)]
        self.di = {self.SP: 0, self.POOL: 0, self.ACT: 0}
        self.dram_toks = []
        self.ST = self.POOL

    def sem(self, name):
        return self.es.enter_context(self.nc.semaphore(name))

    def uid(self, p):
        self.n += 1
        return "%s_%d" % (p, self.n)

    def sb(self, shape, dt, name="sb", es=None):
        return Buf((es or self.es).enter_context(self.nc.sbuf_tensor(self.uid(name), list(shape), dt)))

    def ps(self, name="ps", es=None, shape=(128, 512), dt=F32):
        b = Buf((es or self.es).enter_context(self.nc.psum_tensor(self.uid(name), list(shape), dt)))
        b.psum = True
        return b

    def op(self, q, fn, reads=(), writes=(), extra=()):
        toks = list(extra)
        for b in reads:
            toks.append(b.wr)
            if b.psum:
                toks.extend(t for t in b.rd.items() if t[0] is not q.src)
        for b in writes:
            toks.append(b.wr)
            toks.extend(b.rd.items())
        q.wait(toks)
        tok = q.sig(fn())
        for b in writes:
            b.wr = tok
            b.rd = {}
        for b in reads:
            if b not in writes:
                b.rd[tok[0]] = max(b.rd.get(tok[0], 0), tok[1])
        return tok

    def dma(self, q, out, in_, reads=(), writes=(), extra=(), **kw):
        pool = self.dsem[q]
        i = self.di[q] % len(pool)
        self.di[q] += 1
        ent = pool[i]
        toks = list(extra) + [(ent[0], ent[1])]
        for b in reads:
            toks.append(b.wr)
        for b in writes:
            toks.append(b.wr)
            toks.extend(b.rd.items())
        q.wait(toks)
        ent[1] += 16
        q.eng.dma_start(out=out, in_=in_, **kw).then_inc(ent[0].sem, 16)
        tok = (ent[0], ent[1])
        for b in writes:
            b.wr = tok
            b.rd = {}
        for b in reads:
            b.rd[tok[0]] = max(b.rd.get(tok[0], 0), tok[1])
        if not writes:
            self.dram_toks.append(tok)
            if len(self.dram_toks) > 64:
                self.dram_toks = self.dram_toks[-64:]
        return tok

    def barrier(self):
        toks = [(q.src, q.cnt) for q in self.queues if q.cnt > 0]
        for q in (self.SP, self.POOL, self.ACT):
            for ent in self.dsem[q]:
                if ent[1] > 0:
                    toks.append((ent[0], ent[1]))
        for q in self.queues:
            q.wait(toks)
        self.dram_toks = []


def make_consts(c):
    nc = c.nc
    ones = c.sb([128, 128], F32, "ones")
    ident = c.sb([128, 128], F32, "ident")
    c.op(c.POOL, lambda: nc.gpsimd.memset(ones[:, :], 1.0), writes=[ones])
    c.op(c.POOL, lambda: nc.gpsimd.affine_select(out=ident[:, :], in_=ones[:, :], pattern=[[-1, 128]],
                                                 compare_op=ALU.is_equal, fill=0.0, base=0, channel_multiplier=1),
         reads=[ones], writes=[ident])
    return ones, ident


def load_cols(c, dram_ap, ncols, name, es=None):
    t = c.sb([128, ncols], F32, name, es)
    c.dma(c.SP, t[:, :], dram_ap.rearrange("(j p) -> p j", p=128), writes=[t], allow_slow_non_contiguous=True)
    return t


def ada_mod_bc(c, es, ones, cvec_ap, adaw_ap, adab_ap, col0, ncols, stage, name, pb=None):
    nc = c.nc
    cc = load_cols(c, cvec_ap, 8, "ccol", es)
    sc = c.sb([128, 8], F32, "scol", es)
    c.op(c.ACT, lambda: nc.scalar.activation(out=sc[:, :], in_=cc[:, :], func=AF.Silu), reads=[cc], writes=[sc])
    scb = c.sb([128, 8, 128], F32, "scb", es)
    for kt in range(8):
        c.op(c.DVE, lambda kt=kt: nc.vector.tensor_scalar(out=scb[:, kt, :], in0=ones[:, :], scalar1=sc[:, kt:kt + 1],
                                                          scalar2=None, op0=ALU.mult), reads=[ones, sc], writes=[scb])
    res = c.sb([128, ncols], F32, name, es)
    brow = c.sb([128, ncols], F32, name + "_b", es)
    c.dma(c.SP, brow[:, :], adab_ap[col0:col0 + ncols].partition_broadcast(128), writes=[brow])
    wv = adaw_ap.rearrange("(kt p) n -> p kt n", p=128)
    if pb is None:
        pb = c.ps("modps", es)
    for j in range(ncols // 512):
        c.dma(c.SP, stage[:, :, :], wv[:, :, col0 + j * 512: col0 + (j + 1) * 512], writes=[stage])
        for kt in range(8):
            c.op(c.PE, lambda kt=kt: nc.tensor.matmul(pb[:, :], lhsT=scb[:, kt, :], rhs=stage[:, kt, :],
                                                      start=(kt == 0), stop=(kt == 7)),
                 reads=[scb, stage], writes=[pb])
        c.op(c.DVE, lambda j=j: nc.vector.tensor_tensor(out=res[:, j * 512:(j + 1) * 512], in0=pb[:, :],
                                                        in1=brow[:, j * 512:(j + 1) * 512], op=ALU.add),
             reads=[pb, brow], writes=[res])
    return res


def load_w_bf16(c, w_ap, ktn, ncols, stage_bufs, name, es=None, dst=None):
    nc = c.nc
    wt = dst if dst is not None else c.sb([128, ktn, ncols], BF16, name, es)
    wv = w_ap.rearrange("(kt p) n -> p kt n", p=128)
    i = 0
    for k0 in range(0, ktn, 8):
        for j in range(ncols // 512):
            st = stage_bufs[i % len(stage_bufs)]
            c.dma(c.SP, st[:, :, :], wv[:, k0:k0 + 8, j * 512:(j + 1) * 512], writes=[st])
            q = c.POOL if i % 2 == 0 else c.DVE
            e = nc.gpsimd if i % 2 == 0 else nc.vector
            c.op(q, lambda e=e, st=st, k0=k0, j=j: e.tensor_copy(out=wt[:, k0:k0 + 8, j * 512:(j + 1) * 512], in_=st[:, :, :]),
                 reads=[st], writes=[wt])
            i += 1
    return wt


def norm_transpose_tile(c, x_ap_rows, xbufs, hbufs, i, gmod_bc, shift_bc, ident, trps, hT, col_off, small, part=0):
    nc = c.nc
    xt = xbufs[i % len(xbufs)]
    hb = hbufs[i % len(hbufs)]
    ssq, rstd = small[i % len(small)]
    if part != 2:
        _norm_part(c, nc, x_ap_rows, xt, hb, ssq, rstd, gmod_bc, shift_bc)
    if part != 1:
        _tr_part(c, nc, hb, ident, trps, hT, col_off)
    return xt


def _norm_part(c, nc, x_ap_rows, xt, hb, ssq, rstd, gmod_bc, shift_bc):
    if x_ap_rows is not None:
        c.dma(c.SP, xt[:, :], x_ap_rows, writes=[xt])
    c.op(c.DVE, lambda: nc.vector.scalar_tensor_tensor(out=hb[:, :], in0=xt[:, :], scalar=1.0, in1=xt[:, :],
                                                       op0=ALU.mult, op1=ALU.mult, accum_out=ssq[:, 0:1]),
         reads=[xt], writes=[hb, ssq])
    c.op(c.ACT, lambda: nc.scalar.activation(out=rstd[:, 0:1], in_=ssq[:, 0:1], func=AF.Sqrt, scale=1.0 / D, bias=EPS),
         reads=[ssq], writes=[rstd])
    c.op(c.DVE, lambda: nc.vector.reciprocal(out=rstd[:, 0:1], in_=rstd[:, 0:1]), reads=[rstd], writes=[rstd])
    c.op(c.DVE, lambda: nc.vector.scalar_tensor_tensor(out=hb[:, :], in0=xt[:, :], scalar=rstd[:, 0:1], in1=gmod_bc[:, :],
                                                       op0=ALU.mult, op1=ALU.mult),
         reads=[xt, rstd, gmod_bc], writes=[hb])
    c.op(c.POOL, lambda: nc.gpsimd.tensor_tensor(out=hb[:, :], in0=hb[:, :], in1=shift_bc[:, :], op=ALU.add),
         reads=[shift_bc], writes=[hb])


def _tr_part(c, nc, hb, ident, trps, hT, col_off):
    for g in range(2):
        tp = trps[g]
        for k4 in range(4):
            kt = g * 4 + k4
            c.op(c.PE, lambda kt=kt, k4=k4, tp=tp: nc.tensor.transpose(out=tp[:, k4 * 128:(k4 + 1) * 128],
                                                                       in_=hb[:, kt * 128:(kt + 1) * 128], identity=ident[:, :]),
                 reads=[hb, ident], writes=[tp])
        c.op(c.ACT, lambda g=g, tp=tp: nc.scalar.copy(out=hT[:, g * 4:(g + 1) * 4, col_off:col_off + 128],
                                                      in_=tp[:, :].rearrange("p (k t) -> p k t", k=4)),
             reads=[tp], writes=[hT])


L1_IN = [("x", [S, D], F32), ("cvec", [D], F32), ("adaw", [D, 2048], F32), ("adab", [2048], F32), ("ng", [D], F32),
         ("win", [D, 3072], F32), ("convw", [4, 512], F32), ("convb", [512], F32), ("wr", [4, 128, 128], F32),
         ("wi", [4, 128, 128], F32), ("br", [512], F32), ("bi", [512], F32), ("lam", [512], F32), ("qg", [128], F32), ("kg", [128], F32)]
L1_SCR = [("xaT", [512, S], F32), ("sgaT", [512, S], F32), ("sgbT", [512, S], F32), ("qT", [512, S], BF16), ("kT", [512, S], BF16),
          ("vS", [S, 512], BF16)]


def declare(nc, specs, kind):
    out = {}
    for name, shape, dt_ in specs:
        if kind == "Internal":
            out[name] = nc.dram_tensor(name, list(shape), dt_).ap()
        else:
            out[name] = nc.dram_tensor(name, list(shape), dt_, kind=kind).ap()
    return out


def l1_body(c, nc, ones, ident, A, yT, nqg=16, nchunks=16, do_attn=True, do_lru=True, stop=99, after_lru=None):
    x, cvec, adaw, adab, ng, win, convw, convb, wr, wi, br, bi, lam, qg_, kg_ = [A[k[0]] for k in L1_IN]
    xaT, sgaT, sgbT, qT, kT, vS = [A[k[0]] for k in L1_SCR]
    onesb = c.sb([128, 128], BF16, "onesb")
    c.op(c.POOL, lambda: nc.gpsimd.tensor_copy(out=onesb[:, :], in_=ones[:, :]), reads=[ones], writes=[onesb])
    with ExitStack() as ea:
        if stop <= 1:
            c.barrier()
            return
        stage = [c.sb([128, 8, 512], F32, "wst", ea) for _ in range(2)]
        ssps_l = [c.ps("ssps", ea) for _ in range(2)]
        modbc = ada_mod_bc(c, ea, ones, cvec, adaw, adab, 0, 2048, stage[0], "modbc", pb=ssps_l[1])
        if stop <= 2:
            c.barrier()
            return
        ngb = c.sb([128, D], F32, "ngb", ea)
        c.dma(c.SP, ngb[:, :], ng.partition_broadcast(128), writes=[ngb])
        gmod = c.sb([128, D], F32, "gmod", ea)
        c.op(c.DVE, lambda: nc.vector.scalar_tensor_tensor(out=gmod[:, :], in0=modbc[:, 1024:2048], scalar=1.0, in1=ngb[:, :],
                                                           op0=ALU.add, op1=ALU.mult), reads=[modbc, ngb], writes=[gmod])
        shift = c.sb([128, D], F32, "shift", ea)
        c.op(c.DVE, lambda: nc.vector.tensor_copy(out=shift[:, :], in_=modbc[:, 0:1024]), reads=[modbc], writes=[shift])
        W = load_w_bf16(c, win, 8, 3072, stage, "W", ea)
        gq = load_cols(c, qg_, 1, "gq", ea)
        gk = load_cols(c, kg_, 1, "gk", ea)
        c.op(c.DVE, lambda: nc.vector.tensor_scalar(out=gq[:, :], in0=gq[:, :], scalar1=128.0 ** -0.5, scalar2=None, op0=ALU.mult),
             writes=[gq])
        if stop <= 3:
            c.barrier()
            return
        xbufs = [c.sb([128, D], F32, "xt", ea) for _ in range(4)]
        hbufs = [c.sb([128, D], F32, "hb", ea) for _ in range(4)]
        small = [(c.sb([128, 1], F32, "ssq", ea), c.sb([128, 1], F32, "rstd", ea)) for _ in range(4)]
        hTs = [c.sb([128, 8, 512], BF16, "hT", ea) for _ in range(2)]
        trps = [c.ps("trps", ea) for _ in range(2)]
        mmps = [c.ps("mmps", ea) for _ in range(4)]
        outf = [c.sb([128, 512], F32, "outf", ea) for _ in range(4)]
        outb = [c.sb([128, 512], BF16, "outb", ea) for _ in range(4)]
        sqb = [c.sb([128, 512], BF16, "sqb", ea) for _ in range(4)]
        rsd = [c.sb([128, 512], F32, "rsd", ea) for _ in range(4)]
        qkf = [c.sb([128, 512], F32, "qkf", ea) for _ in range(4)]
        mi = 0
        fi = 0
        bi_ = 0
        pending = []
        def ntile(i_, part):
            norm_transpose_tile(c, x[i_ * 128:(i_ + 1) * 128, :], xbufs, hbufs, i_, gmod, shift, ident, trps, hTs[(i_ // 4) % 2], (i_ % 4) * 128, small, part)
        for i4 in range(4):
            ntile(i4, 1)
        for ch in range(nchunks):
            hT = hTs[ch % 2]
            for i4 in range(4):
                ntile(ch * 4 + i4, 2)
            pf = [0]

            def hook(ch=ch, pf=pf):
                pf[0] += 1
                if ch + 1 < nchunks and pf[0] in (3, 7, 11, 15):
                    ntile((ch + 1) * 4 + (pf[0] - 3) // 4, 1)
            if stop <= 4:
                c.barrier()
                return
            tsl = slice(ch * 512, (ch + 1) * 512)
            for nt in list(range(0, 16)) + list(range(20, 24)):
                if (stop == 5 and nt >= 4) or (stop == 6 and (nt >= 8)) or (stop in (65, 66) and 8 <= nt < 16):
                    continue
                pm = mmps[mi % 4]
                mi += 1
                for kt in range(8):
                    c.op(c.PE, lambda kt=kt, nt=nt, pm=pm: nc.tensor.matmul(pm[:, :], lhsT=W[:, kt, nt * 128:(nt + 1) * 128],
                                                                            rhs=hT[:, kt, :], start=(kt == 0), stop=(kt == 7)),
                         reads=[W, hT], writes=[pm])
                while pending:
                    pending.pop(0)()
                hook()
                if nt < 4:
                    of = outf[fi % 4]
                    fi += 1
                    c.op(c.DVE, lambda of=of, pm=pm: nc.vector.tensor_copy(out=of[:, :], in_=pm[:, :]), reads=[pm], writes=[of])
                    c.dma(c.ST, xaT[nt * 128:(nt + 1) * 128, tsl], of[:, :], reads=[of])
                elif nt < 8 or nt >= 20:
                    of = outf[fi % 4]
                    fi += 1
                    c.op(c.ACT, lambda of=of, pm=pm: nc.scalar.activation(out=of[:, :], in_=pm[:, :], func=AF.Silu), reads=[pm], writes=[of])
                    dst = sgaT if nt < 8 else sgbT
                    r0 = (nt - 4) * 128 if nt < 8 else (nt - 20) * 128
                    c.dma(c.ST, dst[r0:r0 + 128, tsl], of[:, :], reads=[of])
                else:
                    isq = nt < 12
                    g = gq if isq else gk
                    sq = sqb[bi_ % 4]
                    rs = rsd[bi_ % 4]
                    qf = qkf[bi_ % 4]
                    ob = outb[bi_ % 4]
                    bi_ += 1
                    c.op(c.DVE, lambda qf=qf, pm=pm: nc.vector.tensor_copy(out=qf[:, :], in_=pm[:, :]), reads=[pm], writes=[qf])
                    c.op(c.DVE, lambda sq=sq, qf=qf, pm=pm: nc.vector.tensor_tensor(out=sq[:, :], in0=qf[:, :], in1=pm[:, :], op=ALU.mult), reads=[qf, pm], writes=[sq])

                    def tail(sq=sq, rs=rs, qf=qf, ob=ob, g=g, isq=isq, nt=nt, tsl=tsl, ssps=ssps_l[bi_ % 2]):
                        c.op(c.PE, lambda: nc.tensor.matmul(ssps[:, :], lhsT=onesb[:, :], rhs=sq[:, :], start=True, stop=True),
                             reads=[onesb, sq], writes=[ssps])
                        c.op(c.ACT, lambda: nc.scalar.activation(out=rs[:, :], in_=ssps[:, :], func=AF.Ln, scale=1.0 / 128, bias=EPS),
                             reads=[ssps], writes=[rs])
                        c.op(c.ACT, lambda: nc.scalar.activation(out=rs[:, :], in_=rs[:, :], func=AF.Exp, scale=-0.5), writes=[rs])
                        c.op(c.DVE, lambda: nc.vector.scalar_tensor_tensor(out=ob[:, :], in0=qf[:, :], scalar=g[:, 0:1], in1=rs[:, :],
                                                                           op0=ALU.mult, op1=ALU.mult),
                             reads=[qf, rs, g], writes=[ob])
                        dst = qT if isq else kT
                        r0 = (nt - 8) * 128 if isq else (nt - 12) * 128
                        c.dma(c.ST, dst[r0:r0 + 128, tsl], ob[:, :], reads=[ob])
                    pending.append(tail)
            for i4 in range(4):
                if stop <= 7 or stop in (65, 71, 72, 73):
                    continue
                pm = mmps[mi % 4]
                mi += 1
                for kt in range(8):
                    c.op(c.PE, lambda kt=kt, i4=i4, pm=pm: nc.tensor.matmul(pm[:, :], lhsT=hT[:, kt, i4 * 128:(i4 + 1) * 128],
                                                                            rhs=W[:, kt, 2048:2560], start=(kt == 0), stop=(kt == 7)),
                         reads=[W, hT], writes=[pm])
                while pending:
                    pending.pop(0)()
                ob = outb[bi_ % 4]
                bi_ += 1
                c.op(c.ACT, lambda ob=ob, pm=pm: nc.scalar.copy(out=ob[:, :], in_=pm[:, :]), reads=[pm], writes=[ob])
                r0 = ch * 512 + i4 * 128
                c.dma(c.ST, vS[r0:r0 + 128, :], ob[:, :], reads=[ob])
        c.barrier()

    if do_lru:
        with ExitStack() as eb:
            lru_phase(c, eb, nc, xaT, sgaT, yT, convw, convb, wr, wi, br, bi, lam, S)
            c.barrier()
        if after_lru is not None:
            after_lru()

    if do_attn:
        with ExitStack() as ec:
            attn_phase(c, ec, nc, qT, kT, vS, sgbT, yT, nqg)
            c.barrier()
    c.barrier()


def build_l1(nqg=16, nchunks=16, do_attn=True, do_lru=True, dbg=False, store_q=None, stop=99):
    nc = bass.Bass("TRN2", target_bir_lowering=False)
    A = declare(nc, L1_IN, "ExternalInput")
    A.update(declare(nc, L1_SCR, "ExternalOutput" if dbg else "Internal"))
    yT = nc.dram_tensor("yT", [1024, S], BF16, kind="ExternalOutput").ap()
    with ExitStack() as es:
        c = Ctx(nc, es)
        if store_q == "SP":
            c.ST = c.SP
        ones, ident = make_consts(c)
        l1_body(c, nc, ones, ident, A, (lambda k: yT[k * 128:(k + 1) * 128, :]), nqg, nchunks, do_attn, do_lru, stop)
        c.barrier()
    return nc


def lru_phase(c, eb, nc, xaT, sgaT, yT, convw, convb, wr, wi, br, bi, lam, T_total):
    TC = 2048
    cw = c.sb([128, 4, 4], F32, "cw", eb)
    for k in range(4):
        c.dma(c.SP, cw[:, k, :], convw[k, :].rearrange("(ct p) -> p ct", p=128), writes=[cw], allow_slow_non_contiguous=True)
    cb = load_cols(c, convb, 4, "cb", eb)
    brc = load_cols(c, br, 4, "brc", eb)
    bic = load_cols(c, bi, 4, "bic", eb)
    lmc = load_cols(c, lam, 4, "lmc", eb)
    c8 = c.sb([128, 4], F32, "c8", eb)
    c16 = c.sb([128, 4], F32, "c16", eb)
    c.op(c.ACT, lambda: nc.scalar.activation(out=c8[:, :], in_=lmc[:, :], func=AF.Exp, scale=-1.0), reads=[lmc], writes=[c8])
    c.op(c.ACT, lambda: nc.scalar.activation(out=c8[:, :], in_=c8[:, :], func=AF.Ln, bias=1.0), writes=[c8])
    c.op(c.DVE, lambda: nc.vector.tensor_scalar(out=c16[:, :], in0=c8[:, :], scalar1=-16.0, scalar2=None, op0=ALU.mult), reads=[c8], writes=[c16])
    c.op(c.DVE, lambda: nc.vector.tensor_scalar(out=c8[:, :], in0=c8[:, :], scalar1=-8.0, scalar2=None, op0=ALU.mult), writes=[c8])
    wst = c.sb([128, 8, 128], F32, "wrst", eb)
    wrb = c.sb([128, 8, 128], BF16, "wrb", eb)
    c.dma(c.SP, wst[:, 0:4, :], wr.rearrange("h i j -> i h j"), writes=[wst])
    c.dma(c.SP, wst[:, 4:8, :], wi.rearrange("h i j -> i h j"), writes=[wst])
    c.op(c.DVE, lambda: nc.vector.tensor_copy(out=wrb[:, :, :], in_=wst[:, :, :]), reads=[wst], writes=[wrb])
    xas = [c.sb([128, TC + 3], F32, "xa", eb) for _ in range(2)]
    sgs = [c.sb([128, TC], F32, "sga", eb) for _ in range(2)]
    xcs = [c.sb([128, TC], F32, "xc", eb) for _ in range(2)]
    xcbs = [c.sb([128, TC], BF16, "xcb", eb) for _ in range(2)]
    rts = [c.sb([128, TC], F32, "rt", eb) for _ in range(2)]
    its = [c.sb([128, TC], F32, "it", eb) for _ in range(2)]
    ats = [c.sb([128, TC], F32, "at", eb) for _ in range(2)]
    bts = [c.sb([128, TC], F32, "bt", eb) for _ in range(2)]
    hts = [c.sb([128, TC], F32, "ht", eb) for _ in range(2)]
    yb = [c.sb([128, TC], BF16, "yab", eb) for _ in range(2)]
    gps = [c.ps("gps", eb) for _ in range(4)]
    gcnt = [0]
    items = [(ct, tc) for ct in range(4) for tc in range(T_total // TC)]

    def bufs_of(n):
        return (xas[n % 2], sgs[n % 2], hts[n % 2], yb[n % 2], xcs[n % 2], xcbs[n % 2], rts[n % 2], its[n % 2], ats[n % 2], bts[n % 2])

    def stage_a(n):
        ct, tc = items[n]
        xa, sg, ht, yo, xc, xcb, rt, it, at, bt = bufs_of(n)
        t0 = tc * TC
        if tc == 0:
            c.op(c.POOL, lambda: nc.gpsimd.memset(xa[:, 0:3], 0.0), writes=[xa])
            c.dma(c.SP, xa[:, 3:TC + 3], xaT[ct * 128:(ct + 1) * 128, 0:TC], writes=[xa])
        else:
            c.dma(c.SP, xa[:, :], xaT[ct * 128:(ct + 1) * 128, t0 - 3:t0 + TC], writes=[xa])
        c.dma(c.SP, sg[:, :], sgaT[ct * 128:(ct + 1) * 128, t0:t0 + TC], writes=[sg])
        c.op(c.DVE, lambda: nc.vector.tensor_scalar(out=xc[:, :], in0=xa[:, 3:TC + 3], scalar1=cw[:, 3, ct:ct + 1], scalar2=cb[:, ct:ct + 1],
                                                    op0=ALU.mult, op1=ALU.add), reads=[xa, cw, cb], writes=[xc])
        for k in range(3):
            c.op(c.DVE, lambda k=k: nc.vector.scalar_tensor_tensor(out=xc[:, :], in0=xa[:, k:k + TC], scalar=cw[:, k, ct:ct + 1], in1=xc[:, :],
                                                                   op0=ALU.mult, op1=ALU.add), reads=[xa, cw], writes=[xc])

    def stage_a2(n):
        ct, tc = items[n]
        xa, sg, ht, yo, xc, xcb, rt, it, at, bt = bufs_of(n)
        c.op(c.ACT, lambda: nc.scalar.copy(out=xcb[:, :], in_=xc[:, :]), reads=[xc], writes=[xcb])
        for j in range(TC // 512):
            sl = slice(j * 512, (j + 1) * 512)
            for which in range(2):
                gp = gps[gcnt[0] % 4]
                gcnt[0] += 1
                c.op(c.PE, lambda gp=gp, which=which, sl=sl: nc.tensor.matmul(gp[:, :], lhsT=wrb[:, which * 4 + ct, :], rhs=xcb[:, sl], start=True, stop=True),
                     reads=[wrb, xcb], writes=[gp])
                dst = rt if which == 0 else it
                bb = brc if which == 0 else bic
                c.op(c.ACT, lambda gp=gp, dst=dst, bb=bb, sl=sl: nc.scalar.activation(out=dst[:, sl], in_=gp[:, :], func=AF.Sigmoid, bias=bb[:, ct:ct + 1]),
                     reads=[gp, bb], writes=[dst])

    def stage_b(n):
        ct, tc = items[n]
        xa, sg, ht, yo, xc, xcb, rt, it, at, bt = bufs_of(n)
        t0 = tc * TC
        c.op(c.ACT, lambda: nc.scalar.activation(out=at[:, :], in_=rt[:, :], func=AF.Exp, scale=c8[:, ct:ct + 1]), reads=[rt, c8], writes=[at])
        c.op(c.ACT, lambda: nc.scalar.activation(out=bt[:, :], in_=rt[:, :], func=AF.Exp, scale=c16[:, ct:ct + 1]), reads=[rt, c16], writes=[bt])
        c.op(c.ACT, lambda: nc.scalar.activation(out=bt[:, :], in_=bt[:, :], func=AF.Sqrt, scale=-1.0, bias=1.0), writes=[bt])
        c.op(c.POOL, lambda: nc.gpsimd.tensor_tensor(out=it[:, :], in0=it[:, :], in1=xc[:, :], op=ALU.mult), reads=[xc], writes=[it])

    def stage_b2(n):
        ct, tc = items[n]
        xa, sg, ht, yo, xc, xcb, rt, it, at, bt = bufs_of(n)
        t0 = tc * TC
        c.op(c.DVE, lambda: nc.vector.tensor_tensor(out=bt[:, :], in0=bt[:, :], in1=it[:, :], op=ALU.mult), reads=[it], writes=[bt])
        if tc == 0:
            c.op(c.DVE, lambda: nc.vector.tensor_tensor_scan(out=ht[:, :], data0=at[:, :], data1=bt[:, :], initial=0.0, op0=ALU.mult, op1=ALU.add),
                 reads=[at, bt], writes=[ht])
        else:
            ph = hts[(n - 1) % 2]
            c.op(c.DVE, lambda: nc.vector.tensor_tensor_scan(out=ht[:, :], data0=at[:, :], data1=bt[:, :], initial=ph[:, TC - 1:TC],
                                                             op0=ALU.mult, op1=ALU.add),
                 reads=[at, bt, ph], writes=[ht])
        c.op(c.POOL, lambda: nc.gpsimd.tensor_tensor(out=yo[:, :], in0=ht[:, :], in1=sg[:, :], op=ALU.mult), reads=[ht, sg], writes=[yo])
        c.dma(c.ST, yT(ct)[:, t0:t0 + TC], yo[:, :], reads=[yo])

    stage_a(0)
    stage_a2(0)
    for n in range(len(items)):
        more = n + 1 < len(items)
        if more:
            stage_a(n + 1)
        stage_b(n)
        if more:
            stage_a2(n + 1)
        stage_b2(n)


def attn_phase(c, ec, nc, qT, kT, vS, sgbT, yT, nqg):
    NS = 3
    ones = c.sb([128, 512], F32, "aones", ec)
    c.op(c.POOL, lambda: nc.gpsimd.memset(ones[:, :], 1.0), writes=[ones])
    uin = c.sb([128, 128], BF16, "uin", ec)
    ucm = c.sb([128, 128], BF16, "ucm", ec)
    c.op(c.POOL, lambda: nc.gpsimd.affine_select(out=uin[:, :], in_=ones[:, 0:128], pattern=[[-1, 128]], compare_op=ALU.is_ge, fill=0.0,
                                                 base=0, channel_multiplier=1), reads=[ones], writes=[uin])
    c.op(c.POOL, lambda: nc.gpsimd.affine_select(out=ucm[:, :], in_=ones[:, 0:128], pattern=[[1, 128]], compare_op=ALU.is_gt, fill=0.0,
                                                 base=0, channel_multiplier=-1), reads=[ones], writes=[ucm])
    masks = []
    for jj in range(4):
        m = c.sb([128, 512], F32, "mask", ec)
        c.op(c.POOL, lambda m=m, jj=jj: nc.gpsimd.affine_select(out=m[:, :], in_=ones[:, :], pattern=[[1, 512]], compare_op=ALU.is_gt, fill=0.0,
                                                                base=-jj * 128, channel_multiplier=-1), reads=[ones], writes=[m])
        masks.append(m)
    KT = [c.sb([128, S], BF16, "KT", ec) for _ in range(2)]
    QT = [c.sb([128, S], BF16, "QT", ec) for _ in range(2)]
    VV = [c.sb([128, 64, 128], BF16, "VV", ec) for _ in range(2)]
    zps2 = [c.ps("zps", ec) for _ in range(2)]
    zcnt = [0]
    cps = [c.ps("cps", ec) for _ in range(NS)]
    ops_ = [c.ps("ops", ec) for _ in range(NS)]
    NB = 4
    ebuf = [[c.sb([128, 512], F32, "e", ec) for _ in range(NB)] for _ in range(NS)]
    spb = [[c.sb([128, 512], BF16, "sp", ec) for _ in range(NB)] for _ in range(NS)]
    exb = [[c.sb([128, 512], F32, "ex", ec) for _ in range(NB)] for _ in range(NS)]
    wb = [[c.sb([128, 512], BF16, "w", ec) for _ in range(NB)] for _ in range(NS)]
    sgq = [c.sb([128, 512], F32, "sgq", ec) for _ in range(NS)]
    yob = [c.sb([128, 512], BF16, "yo", ec) for _ in range(NS)]
    cnt = [0] * NS

    def stream_rounds(s, h, hb, qg, tiles):
        K_, Q_, V_ = KT[hb], QT[hb], VV[hb]
        cp, op_ = cps[s], ops_[s]
        qsl = slice(qg * 512, (qg + 1) * 512)
        n = len(tiles)
        base = cnt[s]
        cnt[s] += n

        def bufs(i):
            j = (base + i) % NB
            return ebuf[s][j], spb[s][j], exb[s][j], wb[s][j]

        def stage1(i):
            kb = tiles[i]
            e, sp, ex, w = bufs(i)
            zp = zps2[zcnt[0] % 2]
            zcnt[0] += 1
            c.op(c.PE, lambda: nc.tensor.matmul(zp[:, :], lhsT=K_[:, kb * 128:(kb + 1) * 128], rhs=Q_[:, qsl], start=True, stop=True),
                 reads=[K_, Q_], writes=[zp])
            c.op(c.ACT, lambda: nc.scalar.activation(out=e[:, :], in_=zp[:, :], func=AF.Exp), reads=[zp], writes=[e])
            jj = kb - 4 * qg
            if jj >= 0:
                spf = ex
                c.op(c.ACT, lambda: nc.scalar.activation(out=spf[:, :], in_=e[:, :], func=AF.Ln, bias=1.0), reads=[e], writes=[spf])
                c.op(c.DVE, lambda: nc.vector.tensor_tensor(out=sp[:, :], in0=spf[:, :], in1=masks[jj][:, :], op=ALU.mult), reads=[spf, masks[jj]], writes=[sp])
            else:
                c.op(c.ACT, lambda: nc.scalar.activation(out=sp[:, :], in_=e[:, :], func=AF.Ln, bias=1.0), reads=[e], writes=[sp])

        def mid(i):
            kb = tiles[i]
            e, sp, ex, w = bufs(i)
            jj = kb - 4 * qg
            c.op(c.PE, lambda: nc.tensor.matmul(cp[:, :], lhsT=uin[:, :], rhs=sp[:, :], start=(i == 0), stop=False, skip_group_check=True),
                 reads=[uin, sp], writes=[cp])
            c.op(c.ACT, lambda: nc.scalar.activation(out=ex[:, :], in_=cp[:, :], func=AF.Exp, scale=-1.0), reads=[cp], writes=[ex])
            if jj >= 0:
                c.op(c.DVE, lambda: nc.vector.tensor_tensor(out=ex[:, :], in0=ex[:, :], in1=masks[jj][:, :], op=ALU.mult), reads=[masks[jj]], writes=[ex])
            c.op(c.DVE, lambda: nc.vector.tensor_tensor(out=w[:, :], in0=e[:, :], in1=ex[:, :], op=ALU.mult), reads=[e, ex], writes=[w])

        def tail(i):
            kb = tiles[i]
            e, sp, ex, w = bufs(i)
            last = (i == n - 1)
            c.op(c.PE, lambda: nc.tensor.matmul(cp[:, :], lhsT=ucm[:, :], rhs=sp[:, :], start=False, stop=last, skip_group_check=True),
                 reads=[ucm, sp], writes=[cp])
            c.op(c.PE, lambda: nc.tensor.matmul(op_[:, :], lhsT=V_[:, kb, :], rhs=w[:, :], start=(i == 0), stop=last),
                 reads=[V_, w], writes=[op_])

        stage1(0)
        yield
        for r in range(n + 1):
            if r >= 1:
                tail(r - 1)
            if r + 1 < n:
                stage1(r + 1)
            if r < n:
                mid(r)
            yield

    def finish(s, h, qg):
        qsl = slice(qg * 512, (qg + 1) * 512)
        sg = sgq[s]
        yo = yob[s]
        c.dma(c.SP, sg[:, :], sgbT[h * 128:(h + 1) * 128, qsl], writes=[sg])
        c.op(c.DVE, lambda: nc.vector.tensor_tensor(out=yo[:, :], in0=ops_[s][:, :], in1=sg[:, :], op=ALU.mult), reads=[ops_[s], sg], writes=[yo])
        c.dma(c.ST, yT(4 + h)[:, qsl], yo[:, :], reads=[yo])

    def load_head(h):
        hb = h % 2
        c.dma(c.SP, KT[hb][:, :], kT[h * 128:(h + 1) * 128, :], writes=[KT[hb]])
        c.dma(c.SP, QT[hb][:, :], qT[h * 128:(h + 1) * 128, :], writes=[QT[hb]])
        vv = vS.rearrange("(kb p) c -> p kb c", p=128)
        for part in range(4):
            c.dma(c.SP, VV[hb][:, part * 16:(part + 1) * 16, :], vv[:, part * 16:(part + 1) * 16, h * 128:(h + 1) * 128], writes=[VV[hb]])

    jobs = [(h, qg) for h in range(4) for qg in range(nqg - 1, -1, -1)]
    loaded = set()
    active = [None] * NS
    ji = 0
    while True:
        busy = False
        for s_ in range(NS):
            if active[s_] is None and ji < len(jobs):
                h, qg = jobs[ji]
                ji += 1
                if h not in loaded:
                    load_head(h)
                    loaded.add(h)
                active[s_] = (stream_rounds(s_, h, h % 2, qg, list(range(4 * qg + 3, -1, -1))), h, qg)
            if active[s_] is not None:
                busy = True
                gen, h, qg = active[s_]
                try:
                    next(gen)
                except StopIteration:
                    finish(s_, h, qg)
                    active[s_] = None
        if not busy and ji >= len(jobs):
            break


_CACHE = {}


def _get(name, fn, *a, **k):
    if name not in _CACHE:
        _CACHE[name] = fn(*a, **k)
    return _CACHE[name]


def l1_inputs(inp, b, hf):
    f = lambda a: np.ascontiguousarray(a, dtype=np.float32)
    w = inp["w_in_even"][0]
    cs = slice(hf * 512, (hf + 1) * 512)
    cols = np.concatenate([w[:, 0:1024][:, cs], w[:, 1024:2048][:, cs], w[:, 2048:3072][:, cs], w[:, 3072:4096][:, cs],
                           w[:, 4096:5120][:, cs], w[:, 5120:6144][:, cs]], axis=1)
    hs = slice(hf * 4, (hf + 1) * 4)
    return {
        "x": f(inp["x"][b]), "cvec": f(inp["c"][b]), "adaw": f(inp["ada_w"][0][:, :2048]), "adab": f(inp["ada_b"][0][:2048]),
        "ng": f(inp["norm_g"][0]), "win": f(cols), "convw": f(inp["conv_w"][0][:, cs]), "convb": f(inp["conv_b"][0][cs]),
        "wr": f(inp["lru_wr"][0][hs]), "wi": f(inp["lru_wi"][0][hs]), "br": f(inp["lru_br"][0][cs]), "bi": f(inp["lru_bi"][0][cs]),
        "lam": f(inp["lru_lambda"][0][cs]), "qg": f(inp["q_norm_g"][0]), "kg": f(inp["k_norm_g"][0]),
    }


def _run(nc, maps):
    res = run_bass_kernel_spmd(nc, maps, core_ids=list(range(len(maps))))
    return res.results


def kernel_unfused(**inputs):
    inp = {k: np.asarray(v) for k, v in inputs.items()}
    cores = [(b, h) for b in range(4) for h in range(2)]
    r1 = _run(_get("l1", build_l1), [l1_inputs(inp, b, h) for b, h in cores])
    yfull = []
    for b in range(4):
        y0 = np.asarray(r1[2 * b]["yT"])
        y1 = np.asarray(r1[2 * b + 1]["yT"])
        yfull.append(np.concatenate([y0[:512], y1[:512], y0[512:], y1[512:]], axis=0))
    r2 = _run(_get("l2", build_l2), [l2_inputs(inp, b, t, yfull[b]) for b, t in cores])
    m3 = []
    for b, h in cores:
        uh = np.concatenate([np.asarray(r2[2 * b + t]["uT"])[h * 512:(h + 1) * 512] for t in range(2)], axis=1)
        m3.append(l3_inputs(inp, h, uh))
    r3 = _run(_get("l3", build_l3), m3)
    m4 = []
    for b, t in cores:
        y5 = np.concatenate([np.asarray(r3[2 * b + h]["y5T"])[:, t * 4096:(t + 1) * 4096] for h in range(2)], axis=0)
        m4.append(l4_inputs(inp, b, y5, np.asarray(r2[2 * b + t]["sgT"]), np.asarray(r2[2 * b + t]["x1o"])))
    r4 = _run(_get("l4", build_l4), m4)
    out = np.empty((4, S, D), np.float32)
    for b, t in cores:
        out[b, t * 4096:(t + 1) * 4096] = np.asarray(r4[2 * b + t]["x2o"])
    return out


def residual_tile(c, nc, ps2, gate_bc, xt, x1, tmp):
    for nh in range(2):
        sl = slice(nh * 512, (nh + 1) * 512)
        c.op(c.DVE, lambda nh=nh, sl=sl: nc.vector.tensor_tensor(out=tmp[:, sl], in0=ps2[nh][:, :], in1=gate_bc[:, sl], op=ALU.mult),
             reads=[ps2[nh], gate_bc], writes=[tmp])
    c.op(c.POOL, lambda: nc.gpsimd.tensor_tensor(out=x1[:, :], in0=tmp[:, :], in1=xt[:, :], op=ALU.add), reads=[tmp, xt], writes=[x1])


def build_l2(ntok=4096):
    nc = bass.Bass("TRN2", target_bir_lowering=False)
    dt = nc.dram_tensor
    yf = dt("yf", [2048, ntok], BF16, kind="ExternalInput").ap()
    x = dt("x", [ntok, D], F32, kind="ExternalInput").ap()
    cvec = dt("cvec", [D], F32, kind="ExternalInput").ap()
    adaw0 = dt("adaw0", [D, 3 * D], F32, kind="ExternalInput").ap()
    adab0 = dt("adab0", [3 * D], F32, kind="ExternalInput").ap()
    adaw1 = dt("adaw1", [D, 3 * D], F32, kind="ExternalInput").ap()
    adab1 = dt("adab1", [3 * D], F32, kind="ExternalInput").ap()
    ng1 = dt("ng1", [D], F32, kind="ExternalInput").ap()
    wout = dt("wout", [2048, D], F32, kind="ExternalInput").ap()
    win1 = dt("win1", [D, 2048], F32, kind="ExternalInput").ap()
    x1o = dt("x1o", [ntok, D], F32, kind="ExternalOutput").ap()
    uT = dt("uT", [D, ntok], BF16, kind="ExternalOutput").ap()
    sgT = dt("sgT", [D, ntok], F32, kind="ExternalOutput").ap()
    with ExitStack() as es:
        c = Ctx(nc, es)
        ones, ident = make_consts(c)
        gate0 = c.sb([128, D], F32, "gate0p")
        gmod = c.sb([128, D], F32, "gmod")
        shift = c.sb([128, D], F32, "shift")
        Wo = c.sb([128, 16, 1024], BF16, "Wo")
        W1 = c.sb([128, 8, 2048], BF16, "W1")
        with ExitStack() as e0:
            stage = [c.sb([128, 8, 512], F32, "wst", e0) for _ in range(2)]
            g0t = ada_mod_bc(c, e0, ones, cvec, adaw0, adab0, 2048, 1024, stage[0], "gate0")
            c.op(c.DVE, lambda: nc.vector.tensor_copy(out=gate0[:, :], in_=g0t[:, :]), reads=[g0t], writes=[gate0])
            c.barrier()
        with ExitStack() as e0:
            stage = [c.sb([128, 8, 512], F32, "wst", e0) for _ in range(2)]
            mod1 = ada_mod_bc(c, e0, ones, cvec, adaw1, adab1, 0, 2048, stage[1], "mod1")
            ngb = c.sb([128, D], F32, "ngb", e0)
            c.dma(c.SP, ngb[:, :], ng1.partition_broadcast(128), writes=[ngb])
            c.op(c.DVE, lambda: nc.vector.scalar_tensor_tensor(out=gmod[:, :], in0=mod1[:, 1024:2048], scalar=1.0, in1=ngb[:, :],
                                                               op0=ALU.add, op1=ALU.mult), reads=[mod1, ngb], writes=[gmod])
            c.op(c.DVE, lambda: nc.vector.tensor_copy(out=shift[:, :], in_=mod1[:, 0:1024]), reads=[mod1], writes=[shift])
            load_w_bf16(c, wout, 16, 1024, stage, "Wo", dst=Wo)
            load_w_bf16(c, win1, 8, 2048, stage, "W1", dst=W1)
            c.barrier()
        ybufs = [c.sb([128, 16, 512], BF16, "ych") for _ in range(2)]
        xbufs = [c.sb([128, D], F32, "xt") for _ in range(2)]
        x1bufs = [c.sb([128, D], F32, "x1") for _ in range(3)]
        tmpb = c.sb([128, D], F32, "tmp")
        hbufs = [c.sb([128, D], F32, "hb") for _ in range(2)]
        small = [(c.sb([128, 1], F32, "ssq"), c.sb([128, 1], F32, "rstd")) for _ in range(3)]
        hTs = [c.sb([128, 8, 512], BF16, "hT") for _ in range(2)]
        trps = [c.ps("trps") for _ in range(2)]
        ops2 = [c.ps("ops2") for _ in range(2)]
        mmps = [c.ps("mmps") for _ in range(3)]
        outf = [c.sb([128, 512], F32, "outf") for _ in range(3)]
        outb = [c.sb([128, 512], BF16, "outb") for _ in range(3)]
        yv = yf.rearrange("(kt p) t -> p kt t", p=128)
        mi = fi = bi_ = 0
        for ch in range(ntok // 512):
            tsl = slice(ch * 512, (ch + 1) * 512)
            ych = ybufs[ch % 2]
            c.dma(c.SP, ych[:, :, :], yv[:, :, tsl], writes=[ych])
            hT = hTs[ch % 2]
            for i4 in range(4):
                i = ch * 4 + i4
                xt = xbufs[i % 2]
                c.dma(c.SP, xt[:, :], x[i * 128:(i + 1) * 128, :], writes=[xt])
                for nh in range(2):
                    for kt in range(16):
                        c.op(c.PE, lambda kt=kt, nh=nh, i4=i4: nc.tensor.matmul(ops2[nh][:, :], lhsT=ych[:, kt, i4 * 128:(i4 + 1) * 128],
                                                                                 rhs=Wo[:, kt, nh * 512:(nh + 1) * 512], start=(kt == 0), stop=(kt == 15)),
                             reads=[ych, Wo], writes=[ops2[nh]])
                x1 = x1bufs[i % 3]
                residual_tile(c, nc, ops2, gate0, xt, x1, tmpb)
                c.dma(c.ST, x1o[i * 128:(i + 1) * 128, :], x1[:, :], reads=[x1])
                norm_transpose_tile(c, None, x1bufs, hbufs, i, gmod, shift, ident, trps, hT, i4 * 128, small)
            for nt in range(16):
                pm = mmps[mi % 3]
                mi += 1
                for kt in range(8):
                    c.op(c.PE, lambda kt=kt, nt=nt, pm=pm: nc.tensor.matmul(pm[:, :], lhsT=W1[:, kt, nt * 128:(nt + 1) * 128], rhs=hT[:, kt, :],
                                                                            start=(kt == 0), stop=(kt == 7)), reads=[W1, hT], writes=[pm])
                if nt < 8:
                    ob = outb[bi_ % 3]
                    bi_ += 1
                    c.op(c.DVE, lambda ob=ob, pm=pm: nc.vector.tensor_copy(out=ob[:, :], in_=pm[:, :]), reads=[pm], writes=[ob])
                    c.dma(c.ST, uT[nt * 128:(nt + 1) * 128, tsl], ob[:, :], reads=[ob])
                else:
                    of = outf[fi % 3]
                    fi += 1
                    c.op(c.ACT, lambda of=of, pm=pm: nc.scalar.activation(out=of[:, :], in_=pm[:, :], func=AF.Silu), reads=[pm], writes=[of])
                    c.dma(c.ST, sgT[(nt - 8) * 128:(nt - 7) * 128, tsl], of[:, :], reads=[of])
        c.barrier()
    return nc


def build_l4(ntok=4096):
    nc = bass.Bass("TRN2", target_bir_lowering=False)
    dt = nc.dram_tensor
    y5 = dt("y5", [D, ntok], BF16, kind="ExternalInput").ap()
    sgT = dt("sgT", [D, ntok], F32, kind="ExternalInput").ap()
    x1 = dt("x1", [ntok, D], F32, kind="ExternalInput").ap()
    cvec = dt("cvec", [D], F32, kind="ExternalInput").ap()
    adaw1 = dt("adaw1", [D, 3 * D], F32, kind="ExternalInput").ap()
    adab1 = dt("adab1", [3 * D], F32, kind="ExternalInput").ap()
    gluw = dt("gluw", [D, D], F32, kind="ExternalInput").ap()
    glub = dt("glub", [D], F32, kind="ExternalInput").ap()
    wout = dt("wout", [D, D], F32, kind="ExternalInput").ap()
    x2o = dt("x2o", [ntok, D], F32, kind="ExternalOutput").ap()
    with ExitStack() as es:
        c = Ctx(nc, es)
        ones, ident = make_consts(c)
        gate1 = c.sb([128, D], F32, "gate1p")
        Wg = c.sb([128, 8, 1024], BF16, "Wg")
        Wo = c.sb([128, 8, 1024], BF16, "Wo")
        with ExitStack() as e0:
            stage = [c.sb([128, 8, 512], F32, "wst", e0) for _ in range(2)]
            g1t = ada_mod_bc(c, e0, ones, cvec, adaw1, adab1, 2048, 1024, stage[0], "gate1")
            c.op(c.DVE, lambda: nc.vector.tensor_copy(out=gate1[:, :], in_=g1t[:, :]), reads=[g1t], writes=[gate1])
            load_w_bf16(c, gluw, 8, 1024, stage, "Wg", dst=Wg)
            load_w_bf16(c, wout, 8, 1024, stage, "Wo", dst=Wo)
            c.barrier()
        gb = load_cols(c, glub, 8, "glub")
        ybufs = [c.sb([128, 8, 512], BF16, "y5c") for _ in range(2)]
        sbufs = [c.sb([128, 8, 512], F32, "sgc") for _ in range(2)]
        ygs = [c.sb([128, 8, 512], BF16, "yg") for _ in range(2)]
        sig = [c.sb([128, 512], F32, "sig") for _ in range(2)]
        tt = [c.sb([128, 512], F32, "tt") for _ in range(2)]
        xbufs = [c.sb([128, D], F32, "xt") for _ in range(2)]
        x2bufs = [c.sb([128, D], F32, "x2") for _ in range(2)]
        tmpb = c.sb([128, D], F32, "tmp")
        zps = [c.ps("zps") for _ in range(3)]
        ops2 = [c.ps("ops2") for _ in range(2)]
        yv = y5.rearrange("(kt p) t -> p kt t", p=128)
        sv = sgT.rearrange("(kt p) t -> p kt t", p=128)
        zi = 0
        for ch in range(ntok // 512):
            tsl = slice(ch * 512, (ch + 1) * 512)
            ych = ybufs[ch % 2]
            sch = sbufs[ch % 2]
            yg = ygs[ch % 2]
            c.dma(c.SP, ych[:, :, :], yv[:, :, tsl], writes=[ych])
            c.dma(c.SP, sch[:, :, :], sv[:, :, tsl], writes=[sch])
            for nt in range(8):
                zp = zps[zi % 3]
                sg_ = sig[zi % 2]
                t_ = tt[zi % 2]
                zi += 1
                for kt in range(8):
                    c.op(c.PE, lambda kt=kt, nt=nt, zp=zp: nc.tensor.matmul(zp[:, :], lhsT=Wg[:, kt, nt * 128:(nt + 1) * 128], rhs=ych[:, kt, :],
                                                                            start=(kt == 0), stop=(kt == 7)), reads=[Wg, ych], writes=[zp])
                c.op(c.ACT, lambda zp=zp, sg_=sg_, nt=nt: nc.scalar.activation(out=sg_[:, :], in_=zp[:, :], func=AF.Sigmoid, bias=gb[:, nt:nt + 1]),
                     reads=[zp, gb], writes=[sg_])
                c.op(c.DVE, lambda sg_=sg_, t_=t_, nt=nt: nc.vector.tensor_tensor(out=t_[:, :], in0=sg_[:, :], in1=ych[:, nt, :], op=ALU.mult),
                     reads=[sg_, ych], writes=[t_])
                c.op(c.POOL, lambda t_=t_, nt=nt: nc.gpsimd.tensor_tensor(out=yg[:, nt, :], in0=t_[:, :], in1=sch[:, nt, :], op=ALU.mult),
                     reads=[t_, sch], writes=[yg])
            for i4 in range(4):
                i = ch * 4 + i4
                xt = xbufs[i % 2]
                c.dma(c.SP, xt[:, :], x1[i * 128:(i + 1) * 128, :], writes=[xt])
                for nh in range(2):
                    for kt in range(8):
                        c.op(c.PE, lambda kt=kt, nh=nh, i4=i4: nc.tensor.matmul(ops2[nh][:, :], lhsT=yg[:, kt, i4 * 128:(i4 + 1) * 128],
                                                                                 rhs=Wo[:, kt, nh * 512:(nh + 1) * 512], start=(kt == 0), stop=(kt == 7)),
                             reads=[yg, Wo], writes=[ops2[nh]])
                x2 = x2bufs[i % 2]
                residual_tile(c, nc, ops2, gate1, xt, x2, tmpb)
                c.dma(c.ST, x2o[i * 128:(i + 1) * 128, :], x2[:, :], reads=[x2])
        c.barrier()
    return nc


S5_IN = lambda G: [("lre", [G, 64], F32), ("lim", [G, 64], F32), ("ldt", [G], F32), ("bre", [G, 64, 16], F32), ("bim", [G, 64, 16], F32),
                   ("cre", [G * 16, 64], F32), ("cim", [G * 16, 64], F32), ("dsk", [G * 16], F32)]


def s5_body(c, nc, ones, ident, A, uT, y5T, T=S, ngrp=32, after_ct=None):
    G = ngrp
    NCT = G // 8
    lre, lim, ldt, bre, bim, cre, cim, dsk = [A[k[0]] for k in S5_IN(G)]
    M = T // 8
    NMC = M // 512
    NL = int(np.log2(M))
    PI = float(np.pi)
    with ExitStack() as es:
        V, P_, A_ = c.DVE, c.POOL, c.ACT

        def vts(out, in0, s1, s2=None, op0=ALU.mult, op1=None, reads=(), writes=()):
            kw = dict(out=out, in0=in0, scalar1=s1, scalar2=s2, op0=op0)
            if op1 is not None:
                kw["op1"] = op1
            return c.op(V, lambda: nc.vector.tensor_scalar(**kw), reads=reads, writes=writes)

        def vtt(out, in0, in1, op, reads=(), writes=()):
            return c.op(V, lambda: nc.vector.tensor_tensor(out=out, in0=in0, in1=in1, op=op), reads=reads, writes=writes)

        def vstt(out, in0, sc, in1, op0, op1, reads=(), writes=()):
            return c.op(V, lambda: nc.vector.scalar_tensor_tensor(out=out, in0=in0, scalar=sc, in1=in1, op0=op0, op1=op1), reads=reads, writes=writes)

        selR = c.sb([128, 1], F32, "selR", es)
        selI = c.sb([128, 1], F32, "selI", es)
        sgn = c.sb([128, 1], F32, "sgn", es)
        c.op(P_, lambda: nc.gpsimd.affine_select(out=selR[:, :], in_=ones[:, 0:1], pattern=[[0, 1]], compare_op=ALU.is_gt, fill=0.0,
                                                 base=64, channel_multiplier=-1), reads=[ones], writes=[selR])
        vts(selI[:, :], selR[:, :], -1.0, 1.0, ALU.mult, ALU.add, reads=[selR], writes=[selI])
        vts(sgn[:, :], selR[:, :], 2.0, -1.0, ALU.mult, ALU.add, reads=[selR], writes=[sgn])
        swp = c.sb([128, 128], F32, "swp", es)
        c.op(P_, lambda: nc.gpsimd.memset(swp[:, :], 0.0), writes=[swp])
        c.op(P_, lambda: nc.gpsimd.tensor_copy(out=swp[0:64, 64:128], in_=ident[0:64, 0:64]), reads=[ident], writes=[swp])
        c.op(P_, lambda: nc.gpsimd.tensor_copy(out=swp[64:128, 0:64], in_=ident[64:128, 64:128]), reads=[ident], writes=[swp])

        def dup_load(ap2d, name):
            t = c.sb([128, G], F32, name, es)
            v = ap2d.rearrange("g p -> p g")
            c.dma(c.SP, t[0:64, :], v, writes=[t], allow_slow_non_contiguous=True)
            c.dma(c.SP, t[64:128, :], v, writes=[t], allow_slow_non_contiguous=True)
            return t
        lr = dup_load(lre, "lr")
        li = dup_load(lim, "li")
        dtb = c.sb([128, G], F32, "dtb", es)
        c.dma(c.SP, dtb[:, :], ldt.partition_broadcast(128), writes=[dtb])
        c.op(A_, lambda: nc.scalar.activation(out=dtb[:, :], in_=dtb[:, :], func=AF.Exp), writes=[dtb])
        dec = c.sb([128, G], F32, "dec", es)
        ang = c.sb([128, G], F32, "ang", es)
        vtt(dec[:, :], lr[:, :], dtb[:, :], ALU.mult, reads=[lr, dtb], writes=[dec])
        c.op(A_, lambda: nc.scalar.activation(out=dec[:, :], in_=dec[:, :], func=AF.Exp), writes=[dec])
        vtt(ang[:, :], li[:, :], dtb[:, :], ALU.mult, reads=[li, dtb], writes=[ang])
        xk = c.sb([128, G], F32, "xk", es)
        xi = c.sb([128, G], mybir.dt.int32, "xi", es)
        vts(xk[:, :], ang[:, :], 1.0 / (2 * PI), reads=[ang], writes=[xk])
        c.op(V, lambda: nc.vector.tensor_copy(out=xi[:, :], in_=xk[:, :]), reads=[xk], writes=[xi])
        c.op(V, lambda: nc.vector.tensor_copy(out=xk[:, :], in_=xi[:, :]), reads=[xi], writes=[xk])
        vstt(ang[:, :], xk[:, :], -2 * PI, ang[:, :], ALU.mult, ALU.add, reads=[xk], writes=[ang])
        sq_ = c.sb([128, G], F32, "sq", es)
        cq_ = c.sb([128, G], F32, "cq", es)
        hpi = c.sb([128, 1], F32, "hpi", es)
        c.op(P_, lambda: nc.gpsimd.memset(hpi[:, :], PI / 2), writes=[hpi])
        c.op(A_, lambda: nc.scalar.activation(out=sq_[:, :], in_=ang[:, :], func=AF.Sin, scale=0.25), reads=[ang], writes=[sq_])
        c.op(A_, lambda: nc.scalar.activation(out=cq_[:, :], in_=ang[:, :], func=AF.Sin, scale=0.25, bias=hpi[:, 0:1]), reads=[ang, hpi], writes=[cq_])
        t1 = c.sb([128, G], F32, "t1", es)
        t2 = c.sb([128, G], F32, "t2", es)
        t3 = c.sb([128, G], F32, "t3", es)
        for _ in range(2):
            vtt(t1[:, :], cq_[:, :], cq_[:, :], ALU.mult, reads=[cq_], writes=[t1])
            vtt(t2[:, :], sq_[:, :], sq_[:, :], ALU.mult, reads=[sq_], writes=[t2])
            vtt(t3[:, :], sq_[:, :], cq_[:, :], ALU.mult, reads=[sq_, cq_], writes=[t3])
            vtt(cq_[:, :], t1[:, :], t2[:, :], ALU.subtract, reads=[t1, t2], writes=[cq_])
            vts(sq_[:, :], t3[:, :], 2.0, reads=[t3], writes=[sq_])
        Ar = [c.sb([128, G], F32, "Ar", es) for _ in range(9)]
        Ai = [c.sb([128, G], F32, "Ai", es) for _ in range(9)]
        c.op(P_, lambda: nc.gpsimd.memset(Ar[0][:, :], 1.0), writes=[Ar[0]])
        c.op(P_, lambda: nc.gpsimd.memset(Ai[0][:, :], 0.0), writes=[Ai[0]])
        vtt(Ar[1][:, :], dec[:, :], cq_[:, :], ALU.mult, reads=[dec, cq_], writes=[Ar[1]])
        vtt(Ai[1][:, :], dec[:, :], sq_[:, :], ALU.mult, reads=[dec, sq_], writes=[Ai[1]])

        def cmul(orr, oi, ar, ai, br_, bi__):
            vtt(t1[:, :], ar[:, :], br_[:, :], ALU.mult, reads=[ar, br_], writes=[t1])
            vtt(t2[:, :], ai[:, :], bi__[:, :], ALU.mult, reads=[ai, bi__], writes=[t2])
            vtt(t3[:, :], ar[:, :], bi__[:, :], ALU.mult, reads=[ar, bi__], writes=[t3])
            vtt(orr[:, :], t1[:, :], t2[:, :], ALU.subtract, reads=[t1, t2], writes=[orr])
            vtt(t1[:, :], ai[:, :], br_[:, :], ALU.mult, reads=[ai, br_], writes=[t1])
            vtt(oi[:, :], t3[:, :], t1[:, :], ALU.add, reads=[t3, t1], writes=[oi])
        for k in range(2, 9):
            cmul(Ar[k], Ai[k], Ar[k - 1], Ai[k - 1], Ar[1], Ai[1])
        Sr = [Ar[8]] + [c.sb([128, G], F32, "Sr", es) for _ in range(NL - 1)]
        Si = [Ai[8]] + [c.sb([128, G], F32, "Si", es) for _ in range(NL - 1)]
        for l in range(1, NL):
            cmul(Sr[l], Si[l], Sr[l - 1], Si[l - 1], Sr[l - 1], Si[l - 1])
        cr_ = c.sb([128, G], F32, "cr", es)
        ci_ = c.sb([128, G], F32, "ci", es)
        den = c.sb([128, G], F32, "den", es)
        nre = c.sb([128, G], F32, "nre", es)
        vtt(t1[:, :], lr[:, :], lr[:, :], ALU.mult, reads=[lr], writes=[t1])
        vtt(t2[:, :], li[:, :], li[:, :], ALU.mult, reads=[li], writes=[t2])
        vtt(den[:, :], t1[:, :], t2[:, :], ALU.add, reads=[t1, t2], writes=[den])
        c.op(V, lambda: nc.vector.reciprocal(out=den[:, :], in_=den[:, :]), writes=[den])
        vts(nre[:, :], Ar[1][:, :], -1.0, None, ALU.add, reads=[Ar[1]], writes=[nre])
        vtt(t1[:, :], nre[:, :], lr[:, :], ALU.mult, reads=[nre, lr], writes=[t1])
        vtt(t2[:, :], Ai[1][:, :], li[:, :], ALU.mult, reads=[Ai[1], li], writes=[t2])
        vtt(t1[:, :], t1[:, :], t2[:, :], ALU.add, reads=[t2], writes=[t1])
        vtt(cr_[:, :], t1[:, :], den[:, :], ALU.mult, reads=[t1, den], writes=[cr_])
        vtt(t1[:, :], Ai[1][:, :], lr[:, :], ALU.mult, reads=[Ai[1], lr], writes=[t1])
        vtt(t2[:, :], nre[:, :], li[:, :], ALU.mult, reads=[nre, li], writes=[t2])
        vtt(t1[:, :], t1[:, :], t2[:, :], ALU.subtract, reads=[t2], writes=[t1])
        vtt(ci_[:, :], t1[:, :], den[:, :], ALU.mult, reads=[t1, den], writes=[ci_])
        al = [c.sb([128, G], F32, "al", es) for _ in range(9)]
        be = [c.sb([128, G], F32, "be", es) for _ in range(9)]
        ga = [c.sb([128, G], F32, "ga", es) for _ in range(8)]
        de = [c.sb([128, G], F32, "de", es) for _ in range(8)]
        for k in range(9):
            vts(t1[:, :], Ai[k][:, :], selI[:, 0:1], reads=[Ai[k], selI], writes=[t1])
            vstt(al[k][:, :], Ar[k][:, :], selR[:, 0:1], t1[:, :], ALU.mult, ALU.subtract, reads=[Ar[k], selR, t1], writes=[al[k]])
            vts(t1[:, :], Ar[k][:, :], selI[:, 0:1], reads=[Ar[k], selI], writes=[t1])
            vstt(be[k][:, :], Ai[k][:, :], selR[:, 0:1], t1[:, :], ALU.mult, ALU.add, reads=[Ai[k], selR, t1], writes=[be[k]])
            if k < 8:
                vts(t1[:, :], Ai[k][:, :], selI[:, 0:1], reads=[Ai[k], selI], writes=[t1])
                vstt(ga[k][:, :], Ar[k][:, :], selR[:, 0:1], t1[:, :], ALU.mult, ALU.add, reads=[Ar[k], selR, t1], writes=[ga[k]])
                vts(t1[:, :], Ai[k][:, :], selR[:, 0:1], reads=[Ai[k], selR], writes=[t1])
                vstt(de[k][:, :], Ar[k][:, :], selI[:, 0:1], t1[:, :], ALU.mult, ALU.subtract, reads=[Ar[k], selI, t1], writes=[de[k]])
        So = [c.sb([128, G], F32, "So", es) for _ in range(NL)]
        for l in range(NL):
            vts(So[l][:, :], Si[l][:, :], sgn[:, 0:1], reads=[Si[l], sgn], writes=[So[l]])

        def dup_load3(ap3, name):
            t = c.sb([128, G, 16], F32, name, es)
            v = ap3.rearrange("g p c -> p g c")
            c.dma(c.SP, t[0:64, :, :], v, writes=[t])
            c.dma(c.SP, t[64:128, :, :], v, writes=[t])
            return t
        Br = dup_load3(bre, "Br")
        Bi = dup_load3(bim, "Bi")
        Bbr = c.sb([128, G, 16], F32, "Bbr", es)
        Bbi = c.sb([128, G, 16], F32, "Bbi", es)
        tb = c.sb([128, 16], F32, "tb", es)
        for g in range(G):
            vts(tb[:, :], Bi[:, g, :], ci_[:, g:g + 1], reads=[Bi, ci_], writes=[tb])
            vstt(Bbr[:, g, :], Br[:, g, :], cr_[:, g:g + 1], tb[:, :], ALU.mult, ALU.subtract, reads=[Br, cr_, tb], writes=[Bbr])
            vts(tb[:, :], Br[:, g, :], ci_[:, g:g + 1], reads=[Br, ci_], writes=[tb])
            vstt(Bbi[:, g, :], Bi[:, g, :], cr_[:, g:g + 1], tb[:, :], ALU.mult, ALU.add, reads=[Bi, cr_, tb], writes=[Bbi])
        Cr = c.sb([128, G, 16], F32, "Cr", es)
        Ci = c.sb([128, G, 16], F32, "Ci", es)
        cin = c.sb([128, 128], F32, "cin", es)
        cps_ = c.ps("cps", es)
        for src, dst in ((cre, Cr), (cim, Ci)):
            for ct in range(NCT):
                c.dma(c.SP, cin[:, 0:64], src[ct * 128:(ct + 1) * 128, :], writes=[cin])
                c.dma(c.SP, cin[:, 64:128], src[ct * 128:(ct + 1) * 128, :], writes=[cin])
                c.op(c.PE, lambda: nc.tensor.transpose(out=cps_[:, 0:128], in_=cin[:, :], identity=ident[:, :]), reads=[cin, ident], writes=[cps_])
                c.op(A_, lambda dst=dst, ct=ct: nc.scalar.copy(out=dst[:, ct * 8:(ct + 1) * 8, :], in_=cps_[:, 0:128].rearrange("p (g c) -> p g c", g=8)),
                     reads=[cps_], writes=[dst])
        dcol = load_cols(c, dsk, NCT, "dcol", es)
        c.barrier()

        def views(t, n1, n2):
            return [[Buf(t.t[:, a * n2 + b, :]) for b in range(n2)] for a in range(n1)]
        g0pad = [c.sb([128, 128], F32, "g0pad", es) for _ in range(8)]
        Gp = [c.sb([128, 128], F32, "Gp", es) for _ in range(8)]
        Qp = [c.sb([128, 128], F32, "Qp", es) for _ in range(8)]
        tbs = [c.sb([128, 16], F32, "tbs", es) for _ in range(8)]
        mtmp = [c.sb([128, 128], F32, "mtmp", es) for _ in range(8)]
        lhs1_t = c.sb([128, 64, 128], BF16, "lhs1", es)
        lhsq_t = c.sb([128, 72, 128], BF16, "lhsq", es)
        ktb_t = c.sb([128, 8, 128], BF16, "ktb", es)
        msc_t = c.sb([128, 8 * NL, 128], BF16, "msc", es)
        lhs1 = views(lhs1_t, 8, 8)
        lhsq = views(lhsq_t, 8, 9)
        ktb = views(ktb_t, 1, 8)[0]
        msc = views(msc_t, 8, NL)
        u_ = c.sb([128, T], BF16, "ub", es)
        yo = c.sb([128, T], BF16, "y5o", es)
        NSS = 3
        Hf = [[c.sb([128, M], F32, "Hf", es) for _ in range(2)] for _ in range(NSS)]
        Hb = [[c.sb([128, M], BF16, "Hb", es) for _ in range(2)] for _ in range(NSS)]
        Hfin_t = c.sb([128, 8, M + 1], BF16, "Hfin", es)
        c.op(P_, lambda: nc.gpsimd.memset(Hfin_t[:, :, 0:1], 0.0), writes=[Hfin_t])
        Hfin = [Buf(Hfin_t.t[:, gi, :]) for gi in range(8)]
        for hb_ in Hfin:
            hb_.wr = Hfin_t.wr
        pb = [c.ps("s5ps", es) for _ in range(7)]
        tps, kps, sps, ops_ = pb[0:2], pb[2], pb[1:7], pb[0:3]
        x2b = [c.sb([128, 512], F32, "x2b", es) for _ in range(3)]
        inb = [c.sb([128, 512], F32, "inb", es) for _ in range(3)]
        sgb = [c.sb([128, 512], F32, "sgb", es) for _ in range(3)]
        for bufz in tuple(Gp) + tuple(Qp) + tuple(g0pad):
            c.op(P_, lambda bufz=bufz: nc.gpsimd.memset(bufz[:, :], 0.0), writes=[bufz])
        ti = 0
        oi = 0
        ni = 0
        uv = u_.t[:, :].rearrange("p (m i) -> p i m", i=8)
        yv = yo.t[:, :].rearrange("p (m i) -> p i m", i=8)
        for ct in range(NCT):
            c.dma(c.SP, u_[:, :], uT[ct * 128:(ct + 1) * 128, :], writes=[u_])
            for k in range(8):
                for gi in range(8):
                    g = ct * 8 + gi
                    cs = slice(gi * 16, gi * 16 + 16)
                    gp = g0pad[gi] if k == 0 else Gp[gi]
                    tb_ = tbs[ni % 8]
                    ni += 1
                    vts(tb_[:, :], Bbi[:, g, :], de[k][:, g:g + 1], reads=[Bbi, de[k]], writes=[tb_])
                    vstt(gp[:, cs], Bbr[:, g, :], ga[k][:, g:g + 1], tb_[:, :], ALU.mult, ALU.add, reads=[Bbr, ga[k], tb_], writes=[gp])
                    tp = tps[ti % 2]
                    ti += 1
                    c.op(c.PE, lambda gp=gp, tp=tp: nc.tensor.transpose(out=tp[:, 0:128], in_=gp[:, :], identity=ident[:, :]), reads=[gp, ident], writes=[tp])
                    c.op(A_, lambda tp=tp, gi=gi, k=k: nc.scalar.copy(out=lhs1[gi][k][:, :], in_=tp[:, 0:128]), reads=[tp], writes=[lhs1[gi][k]])
            for k in range(9):
                for gi in range(8):
                    g = ct * 8 + gi
                    cs = slice(gi * 16, gi * 16 + 16)
                    qf = Qp[gi]
                    tb_ = tbs[ni % 8]
                    ni += 1
                    vts(tb_[:, :], Ci[:, g, :], be[k][:, g:g + 1], reads=[Ci, be[k]], writes=[tb_])
                    vstt(qf[:, cs], Cr[:, g, :], al[k][:, g:g + 1], tb_[:, :], ALU.mult, ALU.subtract, reads=[Cr, al[k], tb_], writes=[qf])
                    c.op(A_, lambda qf=qf, gi=gi, k=k: nc.scalar.copy(out=lhsq[gi][k][:, :], in_=qf[:, :]), reads=[qf], writes=[lhsq[gi][k]])
                    if k < 8:
                        c.op(c.PE, lambda gi=gi, qf=qf: nc.tensor.matmul(kps[:, 0:128], lhsT=g0pad[gi][:, :], rhs=qf[:, :],
                                                                        start=(gi == 0), stop=(gi == 7)), reads=[g0pad[gi], qf], writes=[kps])
                if k < 8:
                    if k == 0:
                        vstt(ktb[0][:, :], ident[:, :], dcol[:, ct:ct + 1], kps[:, 0:128], ALU.mult, ALU.add, reads=[ident, dcol, kps], writes=[ktb[0]])
                    else:
                        c.op(A_, lambda k=k: nc.scalar.copy(out=ktb[k][:, :], in_=kps[:, 0:128]), reads=[kps], writes=[ktb[k]])
            for l in range(NL):
                for gi in range(8):
                    g = ct * 8 + gi
                    mt = mtmp[ni % 8]
                    ni += 1
                    vts(mt[:, :], ident[:, :], Sr[l][:, g:g + 1], reads=[ident, Sr[l]], writes=[mt])
                    vstt(msc[gi][l][:, :], swp[:, :], So[l][:, g:g + 1], mt[:, :], ALU.mult, ALU.add, reads=[swp, So[l], mt], writes=[msc[gi][l]])
            for g2 in range(0, 8, NSS):
                ng_ = min(NSS, 8 - g2)
                cur = [0] * NSS
                for s_ in range(ng_):
                    gi = g2 + s_
                    for mc in range(NMC):
                        sp_ = sps[2 * s_ + mc % 2]
                        for i in range(8):
                            c.op(c.PE, lambda sp_=sp_, i=i, mc=mc, gi=gi: nc.tensor.matmul(sp_[:, :], lhsT=lhs1[gi][7 - i][:, :],
                                                                                           rhs=uv[:, i, mc * 512:(mc + 1) * 512], start=(i == 0), stop=(i == 7)),
                                 reads=[lhs1[gi][7 - i], u_], writes=[sp_])
                        c.op(V, lambda sp_=sp_, mc=mc, s_=s_: nc.vector.tensor_copy(out=Hf[s_][0][:, mc * 512:(mc + 1) * 512], in_=sp_[:, :]), reads=[sp_], writes=[Hf[s_][0]])
                        c.op(A_, lambda mc=mc, s_=s_: nc.scalar.copy(out=Hb[s_][0][:, mc * 512:(mc + 1) * 512], in_=Hf[s_][0][:, mc * 512:(mc + 1) * 512]),
                             reads=[Hf[s_][0]], writes=[Hb[s_][0]])
                for l in range(NL):
                    sh = 1 << l
                    last = (l == NL - 1)
                    for s_ in range(ng_):
                        gi = g2 + s_
                        cu = cur[s_]
                        nx = 1 - cu
                        Hfc, Hfn, Hbc, Hbn = Hf[s_][cu], Hf[s_][nx], Hb[s_][cu], Hb[s_][nx]
                        if last:
                            c.op(A_, lambda Hfc=Hfc, sh=sh, gi=gi: nc.scalar.copy(out=Hfin[gi][:, 1:1 + sh], in_=Hfc[:, 0:sh]), reads=[Hfc], writes=[Hfin[gi]])
                        else:
                            c.op(A_, lambda Hfc=Hfc, Hfn=Hfn, sh=sh: nc.scalar.copy(out=Hfn[:, 0:sh], in_=Hfc[:, 0:sh]), reads=[Hfc], writes=[Hfn])
                            c.op(A_, lambda Hfc=Hfc, Hbn=Hbn, sh=sh: nc.scalar.copy(out=Hbn[:, 0:sh], in_=Hfc[:, 0:sh]), reads=[Hfc], writes=[Hbn])
                        pos = sh
                        while pos < M:
                            n = min(512, M - pos)
                            sp_ = sps[2 * s_ + (pos // 512 + l) % 2]
                            c.op(c.PE, lambda sp_=sp_, Hbc=Hbc, pos=pos, n=n, sh=sh, gi=gi, l=l: nc.tensor.matmul(sp_[:, 0:n], lhsT=msc[gi][l][:, :],
                                                                                                                 rhs=Hbc[:, pos - sh:pos - sh + n], start=True, stop=True),
                                 reads=[msc[gi][l], Hbc], writes=[sp_])
                            if last:
                                c.op(V, lambda sp_=sp_, Hfc=Hfc, pos=pos, n=n, gi=gi: nc.vector.tensor_tensor(out=Hfin[gi][:, 1 + pos:1 + pos + n], in0=sp_[:, 0:n],
                                                                                                           in1=Hfc[:, pos:pos + n], op=ALU.add),
                                     reads=[sp_, Hfc], writes=[Hfin[gi]])
                            else:
                                c.op(V, lambda sp_=sp_, Hfc=Hfc, Hfn=Hfn, pos=pos, n=n: nc.vector.tensor_tensor(out=Hfn[:, pos:pos + n], in0=sp_[:, 0:n],
                                                                                                             in1=Hfc[:, pos:pos + n], op=ALU.add),
                                     reads=[sp_, Hfc], writes=[Hfn])
                                c.op(A_, lambda Hfn=Hfn, Hbn=Hbn, pos=pos, n=n: nc.scalar.copy(out=Hbn[:, pos:pos + n], in_=Hfn[:, pos:pos + n]),
                                     reads=[Hfn], writes=[Hbn])
                            pos += n
                        cur[s_] = nx
            for mc in range(NMC):
                msl = slice(mc * 512, (mc + 1) * 512)
                for j in range(8):
                    op_ = ops_[oi % 3]
                    x2_, in_, sg_ = x2b[oi % 3], inb[oi % 3], sgb[oi % 3]
                    oi += 1
                    nmm = 8 + j + 1
                    n_ = 0
                    for gi in range(8):
                        c.op(c.PE, lambda gi=gi, j=j, op_=op_, msl=msl, n_=n_: nc.tensor.matmul(op_[:, :], lhsT=lhsq[gi][j + 1][:, :], rhs=Hfin[gi][:, msl],
                                                                                               start=(n_ == 0), stop=False), reads=[lhsq[gi][j + 1], Hfin[gi]], writes=[op_])
                        n_ += 1
                    for i in range(j + 1):
                        c.op(c.PE, lambda i=i, j=j, op_=op_, msl=msl, n_=n_, nmm=nmm: nc.tensor.matmul(op_[:, :], lhsT=ktb[j - i][:, :], rhs=uv[:, i, msl],
                                                                                                       start=False, stop=(n_ == nmm - 1)), reads=[ktb[j - i], u_], writes=[op_])
                        n_ += 1
                    c.op(A_, lambda op_=op_, x2_=x2_: nc.scalar.activation(out=x2_[:, :], in_=op_[:, :], func=AF.Square), reads=[op_], writes=[x2_])
                    c.op(P_, lambda x2_=x2_: nc.gpsimd.tensor_scalar(out=x2_[:, :], in0=x2_[:, :], scalar1=0.044715, scalar2=1.0, op0=ALU.mult, op1=ALU.add), writes=[x2_])
                    vtt(in_[:, :], x2_[:, :], op_[:, :], ALU.mult, reads=[x2_, op_], writes=[in_])
                    c.op(A_, lambda in_=in_, sg_=sg_: nc.scalar.activation(out=sg_[:, :], in_=in_[:, :], func=AF.Sigmoid, scale=1.5957691216), reads=[in_], writes=[sg_])
                    vtt(yv[:, j, msl], sg_[:, :], op_[:, :], ALU.mult, reads=[sg_, op_], writes=[yo])
            c.dma(c.ST, y5T(ct)[:, :], yo[:, :], reads=[yo])
            c.barrier()
            if after_ct is not None:
                after_ct(ct)
        c.barrier()


def build_l3(T=S, ngrp=32):
    nc = bass.Bass("TRN2", target_bir_lowering=False)
    A = declare(nc, S5_IN(ngrp), "ExternalInput")
    uT = nc.dram_tensor("uT", [ngrp * 16, T], BF16, kind="ExternalInput").ap()
    y5T = nc.dram_tensor("y5T", [ngrp * 16, T], BF16, kind="ExternalOutput").ap()
    with ExitStack() as es:
        c = Ctx(nc, es)
        ones, ident = make_consts(c)
        s5_body(c, nc, ones, ident, A, uT, (lambda k: y5T[k * 128:(k + 1) * 128, :]), T, ngrp)
        c.barrier()
    return nc


def l2_inputs(inp, b, th, yfull, ntok=4096):
    f = lambda a: np.ascontiguousarray(a, dtype=np.float32)
    ts = slice(th * ntok, (th + 1) * ntok)
    return {"yf": np.ascontiguousarray(yfull[:, ts]), "x": f(inp["x"][b][ts]), "cvec": f(inp["c"][b]),
            "adaw0": f(inp["ada_w"][0]), "adab0": f(inp["ada_b"][0]), "adaw1": f(inp["ada_w"][1]), "adab1": f(inp["ada_b"][1]),
            "ng1": f(inp["norm_g"][1]), "wout": f(inp["w_out_even"][0]), "win1": f(inp["w_in_odd"][0])}


def l3_inputs(inp, hf, u_half, G=32):
    f = lambda a: np.ascontiguousarray(a, dtype=np.float32)
    gs = slice(hf * G, (hf + 1) * G)
    return {"uT": (None if u_half is None else np.ascontiguousarray(u_half)), "lre": f(inp["s5_lambda_re"][0][gs]), "lim": f(inp["s5_lambda_im"][0][gs]),
            "ldt": f(inp["s5_log_dt"][0][gs]), "bre": f(inp["s5_b_re"][0][gs]), "bim": f(inp["s5_b_im"][0][gs]),
            "cre": f(inp["s5_c_re"][0][gs].reshape(G * 16, 64)), "cim": f(inp["s5_c_im"][0][gs].reshape(G * 16, 64)),
            "dsk": f(inp["s5_d"][0][hf * G * 16:(hf + 1) * G * 16])}


def l4_inputs(inp, b, y5full_tok, sgT, x1):
    f = lambda a: np.ascontiguousarray(a, dtype=np.float32)
    return {"y5": np.ascontiguousarray(y5full_tok), "sgT": f(sgT), "x1": f(x1), "cvec": f(inp["c"][b]),
            "adaw1": f(inp["ada_w"][1]), "adab1": f(inp["ada_b"][1]), "gluw": f(inp["glu_w"][0]), "glub": f(inp["glu_b"][0]),
            "wout": f(inp["w_out_odd"][0])}


RG = [[0, 1], [2, 3], [4, 5], [6, 7]]

F_IN = [("xh", [S, 512], F32), ("adaw0g", [D, 512], F32), ("adab0g", [512], F32), ("woutp", [2048, 512], F32),
        ("adaw1", [D, 2048], F32), ("adab1", [2048], F32), ("ng1", [D], F32), ("adaw1g", [D, 512], F32), ("adab1g", [512], F32),
        ("win1h", [D, 1024], F32), ("gluwh", [D, 512], F32), ("glubh", [512], F32), ("wout1h", [D, 512], F32)]


def ag_issue(c, nc, sn, rc, toks=()):
    c.POOL.wait(list(toks))
    nc.gpsimd.collective_compute("AllGather", ALU.bypass, replica_groups=RG, ins=[sn.opt()], outs=[rc.opt()]).then_inc(c.ccs)
    c.ncc += 1


def ag_start(c, nc, snds, rcvs):
    c.barrier()
    for sn, rc in zip(snds, rcvs):
        nc.gpsimd.collective_compute("AllGather", ALU.bypass, replica_groups=RG, ins=[sn.opt()], outs=[rc.opt()]).then_inc(c.ccs)
        c.ncc += 1


def ag_wait(c):
    for q in c.queues:
        q.eng.wait_ge(c.ccs, c.ncc)


def p2_body(c, nc, ones, A, y_rcv, x1_snd, after_setup, x1_rcv=None):
    with ExitStack() as es:
        gate = c.sb([128, 512], F32, "gate0", es)
        Wo = c.sb([128, 16, 512], BF16, "Wo", es)
        with ExitStack() as e0:
            stage = [c.sb([128, 8, 512], F32, "wst", e0) for _ in range(2)]
            g0t = ada_mod_bc(c, e0, ones, A["cvec"], A["adaw0g"], A["adab0g"], 0, 512, stage[0], "g0t")
            c.op(c.DVE, lambda: nc.vector.tensor_copy(out=gate[:, :], in_=g0t[:, :]), reads=[g0t], writes=[gate])
            load_w_bf16(c, A["woutp"], 16, 512, stage, "Wo", dst=Wo)
            c.barrier()
        after_setup()
        ybufs = [c.sb([128, 16, 1024], BF16, "ych", es) for _ in range(2)]
        xb = [c.sb([128, 512], F32, "xt", es) for _ in range(4)]
        tb = [c.sb([128, 512], F32, "tmp", es) for _ in range(2)]
        ob = [c.sb([128, 512], F32, "x1t", es) for _ in range(3)]
        pss = [c.ps("p2ps", es) for _ in range(4)]

        def load_y(c2):
            ych = ybufs[c2 % 2]
            for k in range(8):
                c.dma(c.SP, ych[:, 2 * k:2 * k + 2, :], y_rcv[k].rearrange("(r p) t -> p r t", p=128)[:, :, c2 * 1024:(c2 + 1) * 1024], writes=[ych])
        load_y(0)
        stoks = []
        for i in range(S // 128):
            c2, off = i // 8, (i % 8) * 128
            ych = ybufs[c2 % 2]
            if i % 8 == 0 and c2 + 1 < S // 1024:
                load_y(c2 + 1)
            xt, tm, o, ps = xb[i % 4], tb[i % 2], ob[i % 3], pss[i % 4]
            c.dma(c.SP, xt[:, :], A["xh"][i * 128:(i + 1) * 128, :], writes=[xt])
            for kt in range(16):
                c.op(c.PE, lambda kt=kt, off=off, ps=ps, ych=ych: nc.tensor.matmul(ps[:, :], lhsT=ych[:, kt, off:off + 128], rhs=Wo[:, kt, :],
                                                                                   start=(kt == 0), stop=(kt == 15)), reads=[ych, Wo], writes=[ps])
            c.op(c.DVE, lambda tm=tm, ps=ps: nc.vector.tensor_tensor(out=tm[:, :], in0=ps[:, :], in1=gate[:, :], op=ALU.mult), reads=[ps, gate], writes=[tm])
            c.op(c.DVE, lambda o=o, tm=tm, xt=xt: nc.vector.tensor_tensor(out=o[:, :], in0=tm[:, :], in1=xt[:, :], op=ALU.add), reads=[tm, xt], writes=[o])
            stoks.append(c.dma(c.ACT, x1_snd[i // 8][(i % 8) * 128:(i % 8 + 1) * 128, :], o[:, :], reads=[o]))
            if x1_rcv is not None and i % 8 == 7:
                ag_issue(c, nc, x1_snd[i // 8], x1_rcv[i // 8], stoks)
                stoks = []
        c.barrier()


def p3_body(c, nc, ones, ident, A, x1_rcv, uT, sgT, after_setup):
    with ExitStack() as es:
        gmod = c.sb([128, D], F32, "gmod", es)
        shift = c.sb([128, D], F32, "shift", es)
        W1 = c.sb([128, 8, 1024], BF16, "W1", es)
        with ExitStack() as e0:
            stage = [c.sb([128, 8, 512], F32, "wst", e0) for _ in range(2)]
            mod1 = ada_mod_bc(c, e0, ones, A["cvec"], A["adaw1"], A["adab1"], 0, 2048, stage[1], "mod1")
            ngb = c.sb([128, D], F32, "ngb", e0)
            c.dma(c.SP, ngb[:, :], A["ng1"].partition_broadcast(128), writes=[ngb])
            c.op(c.DVE, lambda: nc.vector.scalar_tensor_tensor(out=gmod[:, :], in0=mod1[:, 1024:2048], scalar=1.0, in1=ngb[:, :],
                                                               op0=ALU.add, op1=ALU.mult), reads=[mod1, ngb], writes=[gmod])
            c.op(c.DVE, lambda: nc.vector.tensor_copy(out=shift[:, :], in_=mod1[:, 0:1024]), reads=[mod1], writes=[shift])
            load_w_bf16(c, A["win1h"], 8, 1024, stage, "W1", dst=W1)
            c.barrier()
        after_setup()
        xbufs = [c.sb([128, D], F32, "x1f", es) for _ in range(4)]

        def load_x1(i):
            xt = xbufs[i % 4]
            for r in range(2):
                c.dma(c.SP, xt[:, r * 512:(r + 1) * 512], x1_rcv[i // 8][r * 1024 + (i % 8) * 128: r * 1024 + (i % 8 + 1) * 128, :], writes=[xt])
        hbufs = [c.sb([128, D], F32, "hb", es) for _ in range(4)]
        small = [(c.sb([128, 1], F32, "ssq", es), c.sb([128, 1], F32, "rstd", es)) for _ in range(4)]
        hTs = [c.sb([128, 8, 512], BF16, "hT", es) for _ in range(2)]
        trps = [c.ps("trps", es) for _ in range(2)]
        mmps = [c.ps("mmps", es) for _ in range(4)]
        outf = [c.sb([128, 512], F32, "outf", es) for _ in range(3)]
        outb = [c.sb([128, 512], BF16, "outb", es) for _ in range(3)]
        mi = fi = bi_ = 0

        def ntile3(i, part):
            if part == 1 and i + 2 < S // 128:
                load_x1(i + 2)
            norm_transpose_tile(c, None, xbufs, hbufs, i, gmod, shift, ident, trps, hTs[(i // 4) % 2], (i % 4) * 128, small, part)
        for ch in range(S // 512):
            tsl = slice(ch * 512, (ch + 1) * 512)
            hT = hTs[ch % 2]
            if ch == 0:
                load_x1(0)
                load_x1(1)
                for i4 in range(4):
                    ntile3(i4, 1)
            for i4 in range(4):
                ntile3(ch * 4 + i4, 2)
            for nt in range(8):
                if nt % 2 == 0 and ch + 1 < S // 512:
                    ntile3((ch + 1) * 4 + nt // 2, 1)
                pm = mmps[mi % 4]
                mi += 1
                for kt in range(8):
                    c.op(c.PE, lambda kt=kt, nt=nt, pm=pm: nc.tensor.matmul(pm[:, :], lhsT=W1[:, kt, nt * 128:(nt + 1) * 128], rhs=hT[:, kt, :],
                                                                            start=(kt == 0), stop=(kt == 7)), reads=[W1, hT], writes=[pm])
                if nt < 4:
                    o = outb[bi_ % 3]
                    bi_ += 1
                    c.op(c.DVE, lambda o=o, pm=pm: nc.vector.tensor_copy(out=o[:, :], in_=pm[:, :]), reads=[pm], writes=[o])
                    c.dma(c.ST, uT[nt * 128:(nt + 1) * 128, tsl], o[:, :], reads=[o])
                else:
                    o = outf[fi % 3]
                    fi += 1
                    c.op(c.ACT, lambda o=o, pm=pm: nc.scalar.activation(out=o[:, :], in_=pm[:, :], func=AF.Silu), reads=[pm], writes=[o])
                    c.dma(c.ST, sgT[(nt - 4) * 128:(nt - 3) * 128, tsl], o[:, :], reads=[o])
        c.barrier()


def p5_body(c, nc, A, y5_rcv, y5_own, sgT, yg_snd, after_setup, yg_rcv=None):
    with ExitStack() as es:
        Wg = c.sb([128, 8, 512], BF16, "Wg", es)
        gb = load_cols(c, A["glubh"], 4, "glub", es)
        with ExitStack() as e0:
            stage = [c.sb([128, 8, 512], F32, "wst", e0) for _ in range(2)]
            load_w_bf16(c, A["gluwh"], 8, 512, stage, "Wg", dst=Wg)
            c.barrier()
        after_setup()
        ybufs = [c.sb([128, 8, 512], BF16, "y5c", es) for _ in range(2)]
        yown = [c.sb([128, 4, 512], BF16, "y5o", es) for _ in range(2)]
        sbufs = [c.sb([128, 4, 512], F32, "sgc", es) for _ in range(2)]
        ygs = [c.sb([128, 4, 512], BF16, "yg", es) for _ in range(2)]
        sig = [c.sb([128, 512], F32, "sig", es) for _ in range(2)]
        tt = [c.sb([128, 512], F32, "tt", es) for _ in range(2)]
        zps = [c.ps("zps", es) for _ in range(3)]
        sv = sgT.rearrange("(k p) t -> p k t", p=128)
        zi = 0
        def load5(ch):
            tsl = slice(ch * 512, (ch + 1) * 512)
            ych, yo_, sch = ybufs[ch % 2], yown[ch % 2], sbufs[ch % 2]
            for k in range(4):
                c.dma(c.SP, ych[:, 2 * k:2 * k + 2, :], y5_rcv[k].rearrange("(r p) t -> p r t", p=128)[:, :, tsl], writes=[ych])
                c.dma(c.SP, yo_[:, k, :], y5_own[k][:, tsl], writes=[yo_])
            c.dma(c.SP, sch[:, :, :], sv[:, :, tsl], writes=[sch])
        load5(0)
        stoks = []
        for ch in range(S // 512):
            tsl = slice(ch * 512, (ch + 1) * 512)
            ych, yo_, sch, yg = ybufs[ch % 2], yown[ch % 2], sbufs[ch % 2], ygs[ch % 2]
            if ch + 1 < S // 512:
                load5(ch + 1)
            for nt in range(4):
                zp, sg_, t_ = zps[zi % 3], sig[zi % 2], tt[zi % 2]
                zi += 1
                for kt in range(8):
                    c.op(c.PE, lambda kt=kt, nt=nt, zp=zp, ych=ych: nc.tensor.matmul(zp[:, :], lhsT=Wg[:, kt, nt * 128:(nt + 1) * 128], rhs=ych[:, kt, :],
                                                                                     start=(kt == 0), stop=(kt == 7)), reads=[Wg, ych], writes=[zp])
                c.op(c.ACT, lambda zp=zp, sg_=sg_, nt=nt: nc.scalar.activation(out=sg_[:, :], in_=zp[:, :], func=AF.Sigmoid, bias=gb[:, nt:nt + 1]),
                     reads=[zp, gb], writes=[sg_])
                c.op(c.DVE, lambda sg_=sg_, t_=t_, nt=nt, yo_=yo_: nc.vector.tensor_tensor(out=t_[:, :], in0=sg_[:, :], in1=yo_[:, nt, :], op=ALU.mult),
                     reads=[sg_, yo_], writes=[t_])
                c.op(c.DVE, lambda t_=t_, nt=nt, yg=yg, sch=sch: nc.vector.tensor_tensor(out=yg[:, nt, :], in0=t_[:, :], in1=sch[:, nt, :], op=ALU.mult),
                     reads=[t_, sch], writes=[yg])
            j_ = ch // 4
            stoks.append(c.dma(c.ACT, yg_snd[j_].rearrange("(k p) t -> p k t", p=128)[:, :, (ch % 4) * 512:(ch % 4 + 1) * 512], yg[:, :, :], reads=[yg]))
            if yg_rcv is not None and ch % 4 == 3:
                ag_issue(c, nc, yg_snd[j_], yg_rcv[j_], stoks)
                stoks = []
        c.barrier()


def p6_body(c, nc, ones, A, yg_rcv, x1_own, out, after_setup):
    with ExitStack() as es:
        gate = c.sb([128, 512], F32, "gate1", es)
        Wo = c.sb([128, 8, 512], BF16, "Wo1", es)
        with ExitStack() as e0:
            stage = [c.sb([128, 8, 512], F32, "wst", e0) for _ in range(2)]
            g1t = ada_mod_bc(c, e0, ones, A["cvec"], A["adaw1g"], A["adab1g"], 0, 512, stage[0], "g1t")
            c.op(c.DVE, lambda: nc.vector.tensor_copy(out=gate[:, :], in_=g1t[:, :]), reads=[g1t], writes=[gate])
            load_w_bf16(c, A["wout1h"], 8, 512, stage, "Wo1", dst=Wo)
            c.barrier()
        after_setup()
        ybufs = [c.sb([128, 8, 512], BF16, "ygc", es) for _ in range(2)]
        xb = [c.sb([128, 512], F32, "xt", es) for _ in range(3)]
        tb = [c.sb([128, 512], F32, "tmp", es) for _ in range(2)]
        ob = [c.sb([128, 512], F32, "x2t", es) for _ in range(3)]
        pss = [c.ps("p6ps", es) for _ in range(3)]
        def load_y(ch):
            ych = ybufs[ch % 2]
            c.dma(c.SP, ych[:, :, :], yg_rcv[ch // 4].rearrange("(k p) t -> p k t", p=128)[:, :, (ch % 4) * 512:(ch % 4 + 1) * 512], writes=[ych])
        load_y(0)
        for ch in range(S // 512):
            ych = ybufs[ch % 2]
            if ch + 1 < S // 512:
                load_y(ch + 1)
            for i4 in range(4):
                i = ch * 4 + i4
                xt, tm, o, ps = xb[i % 3], tb[i % 2], ob[i % 3], pss[i % 3]
                c.dma(c.SP, xt[:, :], x1_own[i // 8][(i % 8) * 128:(i % 8 + 1) * 128, :], writes=[xt])
                for kt in range(8):
                    c.op(c.PE, lambda kt=kt, i4=i4, ps=ps, ych=ych: nc.tensor.matmul(ps[:, :], lhsT=ych[:, kt, i4 * 128:(i4 + 1) * 128], rhs=Wo[:, kt, :],
                                                                                      start=(kt == 0), stop=(kt == 7)), reads=[ych, Wo], writes=[ps])
                c.op(c.DVE, lambda tm=tm, ps=ps: nc.vector.tensor_tensor(out=tm[:, :], in0=ps[:, :], in1=gate[:, :], op=ALU.mult), reads=[ps, gate], writes=[tm])
                c.op(c.POOL, lambda o=o, tm=tm, xt=xt: nc.gpsimd.tensor_tensor(out=o[:, :], in0=tm[:, :], in1=xt[:, :], op=ALU.add), reads=[tm, xt], writes=[o])
                c.dma(c.ST, out[i * 128:(i + 1) * 128, :], o[:, :], reads=[o])
        c.barrier()


def build_fused():
    nc = bass.Bass("TRN2", target_bir_lowering=False)
    A = declare(nc, L1_IN, "ExternalInput")
    A.update(declare(nc, L1_SCR, "Internal"))
    A.update(declare(nc, F_IN, "ExternalInput"))
    A.update(declare(nc, S5_IN(32), "ExternalInput"))
    out = nc.dram_tensor("out", [S, 512], F32, kind="ExternalOutput").ap()
    I = declare(nc, [("uT", [512, S], BF16), ("sgT", [512, S], F32)], "Internal")

    def chunks(name, n, shape, dt_):
        d = declare(nc, [("%s%d" % (name, i), shape, dt_) for i in range(n)], "Internal")
        return [d["%s%d" % (name, i)] for i in range(n)]
    y_snd = chunks("y_snd", 8, [128, S], BF16)
    y_rcv = chunks("y_rcv", 8, [256, S], BF16)
    x1_snd = chunks("x1_snd", 8, [1024, 512], F32)
    x1_rcv = chunks("x1_rcv", 8, [2048, 512], F32)
    y5_snd = chunks("y5_snd", 4, [128, S], BF16)
    y5_rcv = chunks("y5_rcv", 4, [256, S], BF16)
    yg_snd = chunks("yg_snd", 4, [512, 2048], BF16)
    yg_rcv = chunks("yg_rcv", 4, [1024, 2048], BF16)
    with ExitStack() as es:
        c = Ctx(nc, es)
        c.ccs = c.sem("ccs")
        c.ncc = 0
        ones, ident = make_consts(c)
        w = lambda: ag_wait(c)

        def after_lru():
            for k in range(4):
                ag_issue(c, nc, y_snd[k], y_rcv[k])
        l1_body(c, nc, ones, ident, A, (lambda k: y_snd[k]), after_lru=after_lru)
        ag_start(c, nc, y_snd[4:], y_rcv[4:])
        p2_body(c, nc, ones, A, y_rcv, x1_snd, w, x1_rcv)
        p3_body(c, nc, ones, ident, A, x1_rcv, I["uT"], I["sgT"], w)
        s5_body(c, nc, ones, ident, A, I["uT"], (lambda k: y5_snd[k]), after_ct=(lambda ct: ag_issue(c, nc, y5_snd[ct], y5_rcv[ct])))
        p5_body(c, nc, A, y5_rcv, y5_snd, I["sgT"], yg_snd, w, yg_rcv)
        p6_body(c, nc, ones, A, yg_rcv, x1_snd, out, w)
        c.barrier()
    return nc


def fused_inputs(inp, b, h):
    f = lambda a: np.ascontiguousarray(a, dtype=np.float32)
    m = l1_inputs(inp, b, h)
    cs = slice(h * 512, (h + 1) * 512)
    wo = inp["w_out_even"][0]
    perm = np.concatenate([np.arange(128) + ((r * 512 + k * 128) if k < 4 else (1024 + r * 512 + (k - 4) * 128))
                           for k in range(8) for r in range(2)])
    perm4 = np.concatenate([np.arange(128) + r * 512 + k * 128 for k in range(4) for r in range(2)])
    w1 = inp["w_in_odd"][0]
    m.update({
        "xh": f(inp["x"][b][:, cs]), "adaw0g": f(inp["ada_w"][0][:, 2048:3072][:, cs]), "adab0g": f(inp["ada_b"][0][2048:3072][cs]),
        "woutp": f(wo[perm][:, cs]), "adaw1": f(inp["ada_w"][1][:, :2048]), "adab1": f(inp["ada_b"][1][:2048]), "ng1": f(inp["norm_g"][1]),
        "adaw1g": f(inp["ada_w"][1][:, 2048:3072][:, cs]), "adab1g": f(inp["ada_b"][1][2048:3072][cs]),
        "win1h": f(np.concatenate([w1[:, :1024][:, cs], w1[:, 1024:][:, cs]], axis=1)),
        "gluwh": f(inp["glu_w"][0][perm4][:, cs]), "glubh": f(inp["glu_b"][0][cs]), "wout1h": f(inp["w_out_odd"][0][:, cs]),
    })
    s5 = l3_inputs(inp, h, None)
    s5.pop("uT")
    m.update(s5)
    return m


def kernel_fused(**inputs):
    inp = {k: np.asarray(v) for k, v in inputs.items()}
    cores = [(b, h) for b in range(4) for h in range(2)]
    r = _run(_get("fused", build_fused), [fused_inputs(inp, b, h) for b, h in cores])
    out = np.empty((4, S, D), np.float32)
    for b, h in cores:
        out[b, :, h * 512:(h + 1) * 512] = np.asarray(r[2 * b + h]["out"])
    return out


def kernel(**inputs):
    return kernel_fused(**inputs)
```

```python
import numpy as np
import ml_dtypes
from contextlib import ExitStack
import concourse.bass as bass
import concourse.mybir as mybir
from concourse.bass_utils import run_bass_kernel_spmd

F32 = mybir.dt.float32
BF16 = mybir.dt.bfloat16
AF = mybir.ActivationFunctionType
ALU = mybir.AluOpType

S = 8192
D = 1024
EPS = 1e-6


class Src:
    def __init__(self, sem):
        self.sem = sem


class Buf:
    def __init__(self, t):
        self.t = t
        self.wr = None
        self.rd = {}
        self.psum = False

    def __getitem__(self, idx):
        return self.t[idx]


class Q:
    def __init__(self, ctx, eng, name, is_pe=False):
        self.ctx = ctx
        self.eng = eng
        self.src = Src(ctx.sem("q_" + name))
        self.cnt = 0
        self.seen = {}
        self.is_pe = is_pe

    def wait(self, toks):
        for t in toks:
            if t is None:
                continue
            src, val = t
            if self.is_pe and src is self.src:
                continue
            if self.seen.get(src, 0) >= val:
                continue
            self.eng.wait_ge(src.sem, val)
            self.seen[src] = val

    def sig(self, inst):
        self.cnt += 1
        inst.then_inc(self.src.sem, 1)
        return (self.src, self.cnt)


class Ctx:
    def __init__(self, nc, es):
        self.nc = nc
        self.es = es
        self.n = 0
        self.PE = Q(self, nc.tensor, "pe", is_pe=True)
        self.ACT = Q(self, nc.scalar, "act")
        self.DVE = Q(self, nc.vector, "dve")
        self.POOL = Q(self, nc.gpsimd, "pool")
        self.SP = Q(self, nc.sync, "sp")
        self.queues = [self.PE, self.ACT, self.DVE, self.POOL, self.SP]
        self.dsem = {}
        for q, tag in ((self.SP, "s"), (self.POOL, "g"), (self.ACT, "a")):
            self.dsem[q] = [[Src(self.sem("d%s%d" % (tag, i))), 0] for i in range(12 if q is not self.ACT else 6)]
        self.di = {self.SP: 0, self.POOL: 0, self.ACT: 0}
        self.dram_toks = []
        self.ST = self.POOL

    def sem(self, name):
        return self.es.enter_context(self.nc.semaphore(name))

    def uid(self, p):
        self.n += 1
        return "%s_%d" % (p, self.n)

    def sb(self, shape, dt, name="sb", es=None):
        return Buf((es or self.es).enter_context(self.nc.sbuf_tensor(self.uid(name), list(shape), dt)))

    def ps(self, name="ps", es=None, shape=(128, 512), dt=F32):
        b = Buf((es or self.es).enter_context(self.nc.psum_tensor(self.uid(name), list(shape), dt)))
        b.psum = True
        return b

    def op(self, q, fn, reads=(), writes=(), extra=()):
        toks = list(extra)
        for b in reads:
            toks.append(b.wr)
            if b.psum:
                toks.extend(t for t in b.rd.items() if t[0] is not q.src)
        for b in writes:
            toks.append(b.wr)
            toks.extend(b.rd.items())
        q.wait(toks)
        tok = q.sig(fn())
        for b in writes:
            b.wr = tok
            b.rd = {}
        for b in reads:
            if b not in writes:
                b.rd[tok[0]] = max(b.rd.get(tok[0], 0), tok[1])
        return tok

    def dma(self, q, out, in_, reads=(), writes=(), extra=(), **kw):
        pool = self.dsem[q]
        i = self.di[q] % len(pool)
        self.di[q] += 1
        ent = pool[i]
        toks = list(extra) + [(ent[0], ent[1])]
        for b in reads:
            toks.append(b.wr)
        for b in writes:
            toks.append(b.wr)
            toks.extend(b.rd.items())
        q.wait(toks)
        ent[1] += 16
        q.eng.dma_start(out=out, in_=in_, **kw).then_inc(ent[0].sem, 16)
        tok = (ent[0], ent[1])
        for b in writes:
            b.wr = tok
            b.rd = {}
        for b in reads:
            b.rd[tok[0]] = max(b.rd.get(tok[0], 0), tok[1])
        if not writes:
            self.dram_toks.append(tok)
            if len(self.dram_toks) > 64:
                self.dram_toks = self.dram_toks[-64:]
        return tok

    def barrier(self):
        toks = [(q.src, q.cnt) for q in self.queues if q.cnt > 0]
        for q in (self.SP, self.POOL, self.ACT):
            for ent in self.dsem[q]:
                if ent[1] > 0:
                    toks.append((ent[0], ent[1]))
        for q in self.queues:
            q.wait(toks)
        self.dram_toks = []


def make_consts(c):
    nc = c.nc
    ones = c.sb([128, 128], F32, "ones")
    ident = c.sb([128, 128], F32, "ident")
    c.op(c.POOL, lambda: nc.gpsimd.memset(ones[:, :], 1.0), writes=[ones])
    c.op(c.POOL, lambda: nc.gpsimd.affine_select(out=ident[:, :], in_=ones[:, :], pattern=[[-1, 128]],
                                                 compare_op=ALU.is_equal, fill=0.0, base=0, channel_multiplier=1),
         reads=[ones], writes=[ident])
    return ones, ident


def load_cols(c, dram_ap, ncols, name, es=None):
    t = c.sb([128, ncols], F32, name, es)
    c.dma(c.SP, t[:, :], dram_ap.rearrange("(j p) -> p j", p=128), writes=[t], allow_slow_non_contiguous=True)
    return t


def ada_mod_bc(c, es, ones, cvec_ap, adaw_ap, adab_ap, col0, ncols, stage, name, pb=None):
    nc = c.nc
    cc = load_cols(c, cvec_ap, 8, "ccol", es)
    sc = c.sb([128, 8], F32, "scol", es)
    c.op(c.ACT, lambda: nc.scalar.activation(out=sc[:, :], in_=cc[:, :], func=AF.Silu), reads=[cc], writes=[sc])
    scb = c.sb([128, 8, 128], F32, "scb", es)
    for kt in range(8):
        c.op(c.DVE, lambda kt=kt: nc.vector.tensor_scalar(out=scb[:, kt, :], in0=ones[:, :], scalar1=sc[:, kt:kt + 1],
                                                          scalar2=None, op0=ALU.mult), reads=[ones, sc], writes=[scb])
    res = c.sb([128, ncols], F32, name, es)
    brow = c.sb([128, ncols], F32, name + "_b", es)
    c.dma(c.SP, brow[:, :], adab_ap[col0:col0 + ncols].partition_broadcast(128), writes=[brow])
    wv = adaw_ap.rearrange("(kt p) n -> p kt n", p=128)
    if pb is None:
        pb = c.ps("modps", es)
    for j in range(ncols // 512):
        c.dma(c.SP, stage[:, :, :], wv[:, :, col0 + j * 512: col0 + (j + 1) * 512], writes=[stage])
        for kt in range(8):
            c.op(c.PE, lambda kt=kt: nc.tensor.matmul(pb[:, :], lhsT=scb[:, kt, :], rhs=stage[:, kt, :],
                                                      start=(kt == 0), stop=(kt == 7)),
                 reads=[scb, stage], writes=[pb])
        c.op(c.DVE, lambda j=j: nc.vector.tensor_tensor(out=res[:, j * 512:(j + 1) * 512], in0=pb[:, :],
                                                        in1=brow[:, j * 512:(j + 1) * 512], op=ALU.add),
             reads=[pb, brow], writes=[res])
    return res


def load_w_bf16(c, w_ap, ktn, ncols, stage_bufs, name, es=None, dst=None):
    nc = c.nc
    wt = dst if dst is not None else c.sb([128, ktn, ncols], BF16, name, es)
    wv = w_ap.rearrange("(kt p) n -> p kt n", p=128)
    i = 0
    for k0 in range(0, ktn, 8):
        for j in range(ncols // 512):
            st = stage_bufs[i % len(stage_bufs)]
            c.dma(c.SP, st[:, :, :], wv[:, k0:k0 + 8, j * 512:(j + 1) * 512], writes=[st])
            q = c.POOL if i % 2 == 0 else c.DVE
            e = nc.gpsimd if i % 2 == 0 else nc.vector
            c.op(q, lambda e=e, st=st, k0=k0, j=j: e.tensor_copy(out=wt[:, k0:k0 + 8, j * 512:(j + 1) * 512], in_=st[:, :, :]),
                 reads=[st], writes=[wt])
            i += 1
    return wt


def norm_transpose_tile(c, x_ap_rows, xbufs, hbufs, i, gmod_bc, shift_bc, ident, trps, hT, col_off, small, part=0):
    nc = c.nc
    xt = xbufs[i % len(xbufs)]
    hb = hbufs[i % len(hbufs)]
    ssq, rstd = small[i % len(small)]
    if part != 2:
        _norm_part(c, nc, x_ap_rows, xt, hb, ssq, rstd, gmod_bc, shift_bc)
    if part != 1:
        _tr_part(c, nc, hb, ident, trps, hT, col_off)
    return xt


def _norm_part(c, nc, x_ap_rows, xt, hb, ssq, rstd, gmod_bc, shift_bc):
    if x_ap_rows is not None:
        c.dma(c.SP, xt[:, :], x_ap_rows, writes=[xt])
    c.op(c.DVE, lambda: nc.vector.scalar_tensor_tensor(out=hb[:, :], in0=xt[:, :], scalar=1.0, in1=xt[:, :],
                                                       op0=ALU.mult, op1=ALU.mult, accum_out=ssq[:, 0:1]),
         reads=[xt], writes=[hb, ssq])
    c.op(c.ACT, lambda: nc.scalar.activation(out=rstd[:, 0:1], in_=ssq[:, 0:1], func=AF.Sqrt, scale=1.0 / D, bias=EPS),
         reads=[ssq], writes=[rstd])
    c.op(c.DVE, lambda: nc.vector.reciprocal(out=rstd[:, 0:1], in_=rstd[:, 0:1]), reads=[rstd], writes=[rstd])
    c.op(c.DVE, lambda: nc.vector.scalar_tensor_tensor(out=hb[:, :], in0=xt[:, :], scalar=rstd[:, 0:1], in1=gmod_bc[:, :],
                                                       op0=ALU.mult, op1=ALU.mult),
         reads=[xt, rstd, gmod_bc], writes=[hb])
    c.op(c.POOL, lambda: nc.gpsimd.tensor_tensor(out=hb[:, :], in0=hb[:, :], in1=shift_bc[:, :], op=ALU.add),
         reads=[shift_bc], writes=[hb])


def _tr_part(c, nc, hb, ident, trps, hT, col_off):
    for g in range(2):
        tp = trps[g]
        for k4 in range(4):
            kt = g * 4 + k4
            c.op(c.PE, lambda kt=kt, k4=k4, tp=tp: nc.tensor.transpose(out=tp[:, k4 * 128:(k4 + 1) * 128],
                                                                       in_=hb[:, kt * 128:(kt + 1) * 128], identity=ident[:, :]),
                 reads=[hb, ident], writes=[tp])
        c.op(c.ACT, lambda g=g, tp=tp: nc.scalar.copy(out=hT[:, g * 4:(g + 1) * 4, col_off:col_off + 128],
                                                      in_=tp[:, :].rearrange("p (k t) -> p k t", k=4)),
             reads=[tp], writes=[hT])


L1_IN = [("x", [S, D], F32), ("cvec", [D], F32), ("adaw", [D, 2048], F32), ("adab", [2048], F32), ("ng", [D], F32),
         ("win", [D, 3072], F32), ("convw", [4, 512], F32), ("convb", [512], F32), ("wr", [4, 128, 128], F32),
         ("wi", [4, 128, 128], F32), ("br", [512], F32), ("bi", [512], F32), ("lam", [512], F32), ("qg", [128], F32), ("kg", [128], F32)]
L1_SCR = [("xaT", [512, S], F32), ("sgaT", [512, S], F32), ("sgbT", [512, S], F32), ("qT", [512, S], BF16), ("kT", [512, S], BF16),
          ("vS", [S, 512], BF16)]


def declare(nc, specs, kind):
    out = {}
    for name, shape, dt_ in specs:
        if kind == "Internal":
            out[name] = nc.dram_tensor(name, list(shape), dt_).ap()
        else:
            out[name] = nc.dram_tensor(name, list(shape), dt_, kind=kind).ap()
    return out


def l1_body(c, nc, ones, ident, A, yT, nqg=16, nchunks=16, do_attn=True, do_lru=True, stop=99, after_lru=None):
    x, cvec, adaw, adab, ng, win, convw, convb, wr, wi, br, bi, lam, qg_, kg_ = [A[k[0]] for k in L1_IN]
    xaT, sgaT, sgbT, qT, kT, vS = [A[k[0]] for k in L1_SCR]
    onesb = c.sb([128, 128], BF16, "onesb")
    c.op(c.POOL, lambda: nc.gpsimd.tensor_copy(out=onesb[:, :], in_=ones[:, :]), reads=[ones], writes=[onesb])
    with ExitStack() as ea:
        if stop <= 1:
            c.barrier()
            return
        stage = [c.sb([128, 8, 512], F32, "wst", ea) for _ in range(2)]
        ssps_l = [c.ps("ssps", ea) for _ in range(2)]
        modbc = ada_mod_bc(c, ea, ones, cvec, adaw, adab, 0, 2048, stage[0], "modbc", pb=ssps_l[1])
        if stop <= 2:
            c.barrier()
            return
        ngb = c.sb([128, D], F32, "ngb", ea)
        c.dma(c.SP, ngb[:, :], ng.partition_broadcast(128), writes=[ngb])
        gmod = c.sb([128, D], F32, "gmod", ea)
        c.op(c.DVE, lambda: nc.vector.scalar_tensor_tensor(out=gmod[:, :], in0=modbc[:, 1024:2048], scalar=1.0, in1=ngb[:, :],
                                                           op0=ALU.add, op1=ALU.mult), reads=[modbc, ngb], writes=[gmod])
        shift = c.sb([128, D], F32, "shift", ea)
        c.op(c.DVE, lambda: nc.vector.tensor_copy(out=shift[:, :], in_=modbc[:, 0:1024]), reads=[modbc], writes=[shift])
        W = load_w_bf16(c, win, 8, 3072, stage, "W", ea)
        gq = load_cols(c, qg_, 1, "gq", ea)
        gk = load_cols(c, kg_, 1, "gk", ea)
        c.op(c.DVE, lambda: nc.vector.tensor_scalar(out=gq[:, :], in0=gq[:, :], scalar1=128.0 ** -0.5, scalar2=None, op0=ALU.mult),
             writes=[gq])
        if stop <= 3:
            c.barrier()
            return
        xbufs = [c.sb([128, D], F32, "xt", ea) for _ in range(4)]
        hbufs = [c.sb([128, D], F32, "hb", ea) for _ in range(4)]
        small = [(c.sb([128, 1], F32, "ssq", ea), c.sb([128, 1], F32, "rstd", ea)) for _ in range(4)]
        hTs = [c.sb([128, 8, 512], BF16, "hT", ea) for _ in range(2)]
        trps = [c.ps("trps", ea) for _ in range(2)]
        mmps = [c.ps("mmps", ea) for _ in range(4)]
        outf = [c.sb([128, 512], F32, "outf", ea) for _ in range(4)]
        outb = [c.sb([128, 512], BF16, "outb", ea) for _ in range(4)]
        sqb = [c.sb([128, 512], BF16, "sqb", ea) for _ in range(4)]
        rsd = [c.sb([128, 512], F32, "rsd", ea) for _ in range(4)]
        qkf = [c.sb([128, 512], F32, "qkf", ea) for _ in range(4)]
        mi = 0
        fi = 0
        bi_ = 0
        pending = []
        def ntile(i_, part):
            norm_transpose_tile(c, x[i_ * 128:(i_ + 1) * 128, :], xbufs, hbufs, i_, gmod, shift, ident, trps, hTs[(i_ // 4) % 2], (i_ % 4) * 128, small, part)
        for i4 in range(4):
            ntile(i4, 1)
        for ch in range(nchunks):
            hT = hTs[ch % 2]
            for i4 in range(4):
                ntile(ch * 4 + i4, 2)
            pf = [0]

            def hook(ch=ch, pf=pf):
                pf[0] += 1
                if ch + 1 < nchunks and pf[0] in (3, 7, 11, 15):
                    ntile((ch + 1) * 4 + (pf[0] - 3) // 4, 1)
            if stop <= 4:
                c.barrier()
                return
            tsl = slice(ch * 512, (ch + 1) * 512)
            for nt in list(range(0, 16)) + list(range(20, 24)):
                if (stop == 5 and nt >= 4) or (stop == 6 and (nt >= 8)) or (stop in (65, 66) and 8 <= nt < 16):
                    continue
                pm = mmps[mi % 4]
                mi += 1
                for kt in range(8):
                    c.op(c.PE, lambda kt=kt, nt=nt, pm=pm: nc.tensor.matmul(pm[:, :], lhsT=W[:, kt, nt * 128:(nt + 1) * 128],
                                                                            rhs=hT[:, kt, :], start=(kt == 0), stop=(kt == 7)),
                         reads=[W, hT], writes=[pm])
                while pending:
                    pending.pop(0)()
                hook()
                if nt < 4:
                    of = outf[fi % 4]
                    fi += 1
                    c.op(c.DVE, lambda of=of, pm=pm: nc.vector.tensor_copy(out=of[:, :], in_=pm[:, :]), reads=[pm], writes=[of])
                    c.dma(c.ST, xaT[nt * 128:(nt + 1) * 128, tsl], of[:, :], reads=[of])
                elif nt < 8 or nt >= 20:
                    of = outf[fi % 4]
                    fi += 1
                    c.op(c.ACT, lambda of=of, pm=pm: nc.scalar.activation(out=of[:, :], in_=pm[:, :], func=AF.Silu), reads=[pm], writes=[of])
                    dst = sgaT if nt < 8 else sgbT
                    r0 = (nt - 4) * 128 if nt < 8 else (nt - 20) * 128
                    c.dma(c.ST, dst[r0:r0 + 128, tsl], of[:, :], reads=[of])
                else:
                    isq = nt < 12
                    g = gq if isq else gk
                    sq = sqb[bi_ % 4]
                    rs = rsd[bi_ % 4]
                    qf = qkf[bi_ % 4]
                    ob = outb[bi_ % 4]
                    bi_ += 1
                    c.op(c.DVE, lambda qf=qf, pm=pm: nc.vector.tensor_copy(out=qf[:, :], in_=pm[:, :]), reads=[pm], writes=[qf])
                    c.op(c.DVE, lambda sq=sq, qf=qf, pm=pm: nc.vector.tensor_tensor(out=sq[:, :], in0=qf[:, :], in1=pm[:, :], op=ALU.mult), reads=[qf, pm], writes=[sq])

                    def tail(sq=sq, rs=rs, qf=qf, ob=ob, g=g, isq=isq, nt=nt, tsl=tsl, ssps=ssps_l[bi_ % 2]):
                        c.op(c.PE, lambda: nc.tensor.matmul(ssps[:, :], lhsT=onesb[:, :], rhs=sq[:, :], start=True, stop=True),
                             reads=[onesb, sq], writes=[ssps])
                        c.op(c.ACT, lambda: nc.scalar.activation(out=rs[:, :], in_=ssps[:, :], func=AF.Ln, scale=1.0 / 128, bias=EPS),
                             reads=[ssps], writes=[rs])
                        c.op(c.ACT, lambda: nc.scalar.activation(out=rs[:, :], in_=rs[:, :], func=AF.Exp, scale=-0.5), writes=[rs])
                        c.op(c.DVE, lambda: nc.vector.scalar_tensor_tensor(out=ob[:, :], in0=qf[:, :], scalar=g[:, 0:1], in1=rs[:, :],
                                                                           op0=ALU.mult, op1=ALU.mult),
                             reads=[qf, rs, g], writes=[ob])
                        dst = qT if isq else kT
                        r0 = (nt - 8) * 128 if isq else (nt - 12) * 128
                        c.dma(c.ST, dst[r0:r0 + 128, tsl], ob[:, :], reads=[ob])
                    pending.append(tail)
            for i4 in range(4):
                if stop <= 7 or stop in (65, 71, 72, 73):
                    continue
                pm = mmps[mi % 4]
                mi += 1
                for kt in range(8):
                    c.op(c.PE, lambda kt=kt, i4=i4, pm=pm: nc.tensor.matmul(pm[:, :], lhsT=hT[:, kt, i4 * 128:(i4 + 1) * 128],
                                                                            rhs=W[:, kt, 2048:2560], start=(kt == 0), stop=(kt == 7)),
                         reads=[W, hT], writes=[pm])
                while pending:
                    pending.pop(0)()
                ob = outb[bi_ % 4]
                bi_ += 1
                c.op(c.ACT, lambda ob=ob, pm=pm: nc.scalar.copy(out=ob[:, :], in_=pm[:, :]), reads=[pm], writes=[ob])
                r0 = ch * 512 + i4 * 128
                c.dma(c.ST, vS[r0:r0 + 128, :], ob[:, :], reads=[ob])
        c.barrier()

    if do_lru:
        with ExitStack() as eb:
            lru_phase(c, eb, nc, xaT, sgaT, yT, convw, convb, wr, wi, br, bi, lam, S)
            c.barrier()
        if after_lru is not None:
            after_lru()

    if do_attn:
        with ExitStack() as ec:
            attn_phase(c, ec, nc, qT, kT, vS, sgbT, yT, nqg)
            c.barrier()
    c.barrier()


def build_l1(nqg=16, nchunks=16, do_attn=True, do_lru=True, dbg=False, store_q=None, stop=99):
    nc = bass.Bass("TRN2", target_bir_lowering=False)
    A = declare(nc, L1_IN, "ExternalInput")
    A.update(declare(nc, L1_SCR, "ExternalOutput" if dbg else "Internal"))
    yT = nc.dram_tensor("yT", [1024, S], BF16, kind="ExternalOutput").ap()
    with ExitStack() as es:
        c = Ctx(nc, es)
        if store_q == "SP":
            c.ST = c.SP
        ones, ident = make_consts(c)
        l1_body(c, nc, ones, ident, A, (lambda k: yT[k * 128:(k + 1) * 128, :]), nqg, nchunks, do_attn, do_lru, stop)
        c.barrier()
    return nc


def lru_phase(c, eb, nc, xaT, sgaT, yT, convw, convb, wr, wi, br, bi, lam, T_total):
    TC = 2048
    cw = c.sb([128, 4, 4], F32, "cw", eb)
    for k in range(4):
        c.dma(c.SP, cw[:, k, :], convw[k, :].rearrange("(ct p) -> p ct", p=128), writes=[cw], allow_slow_non_contiguous=True)
    cb = load_cols(c, convb, 4, "cb", eb)
    brc = load_cols(c, br, 4, "brc", eb)
    bic = load_cols(c, bi, 4, "bic", eb)
    lmc = load_cols(c, lam, 4, "lmc", eb)
    c8 = c.sb([128, 4], F32, "c8", eb)
    c16 = c.sb([128, 4], F32, "c16", eb)
    c.op(c.ACT, lambda: nc.scalar.activation(out=c8[:, :], in_=lmc[:, :], func=AF.Exp, scale=-1.0), reads=[lmc], writes=[c8])
    c.op(c.ACT, lambda: nc.scalar.activation(out=c8[:, :], in_=c8[:, :], func=AF.Ln, bias=1.0), writes=[c8])
    c.op(c.DVE, lambda: nc.vector.tensor_scalar(out=c16[:, :], in0=c8[:, :], scalar1=-16.0, scalar2=None, op0=ALU.mult), reads=[c8], writes=[c16])
    c.op(c.DVE, lambda: nc.vector.tensor_scalar(out=c8[:, :], in0=c8[:, :], scalar1=-8.0, scalar2=None, op0=ALU.mult), writes=[c8])
    wst = c.sb([128, 8, 128], F32, "wrst", eb)
    wrb = c.sb([128, 8, 128], BF16, "wrb", eb)
    c.dma(c.SP, wst[:, 0:4, :], wr.rearrange("h i j -> i h j"), writes=[wst])
    c.dma(c.SP, wst[:, 4:8, :], wi.rearrange("h i j -> i h j"), writes=[wst])
    c.op(c.DVE, lambda: nc.vector.tensor_copy(out=wrb[:, :, :], in_=wst[:, :, :]), reads=[wst], writes=[wrb])
    xas = [c.sb([128, TC + 3], F32, "xa", eb) for _ in range(2)]
    sgs = [c.sb([128, TC], F32, "sga", eb) for _ in range(2)]
    xcs = [c.sb([128, TC], F32, "xc", eb) for _ in range(2)]
    xcbs = [c.sb([128, TC], BF16, "xcb", eb) for _ in range(2)]
    rts = [c.sb([128, TC], F32, "rt", eb) for _ in range(2)]
    its = [c.sb([128, TC], F32, "it", eb) for _ in range(2)]
    ats = [c.sb([128, TC], F32, "at", eb) for _ in range(2)]
    bts = [c.sb([128, TC], F32, "bt", eb) for _ in range(2)]
    hts = [c.sb([128, TC], F32, "ht", eb) for _ in range(2)]
    yb = [c.sb([128, TC], BF16, "yab", eb) for _ in range(2)]
    gps = [c.ps("gps", eb) for _ in range(4)]
    gcnt = [0]
    items = [(ct, tc) for ct in range(4) for tc in range(T_total // TC)]

    def bufs_of(n):
        return (xas[n % 2], sgs[n % 2], hts[n % 2], yb[n % 2], xcs[n % 2], xcbs[n % 2], rts[n % 2], its[n % 2], ats[n % 2], bts[n % 2])

    def stage_a(n):
        ct, tc = items[n]
        xa, sg, ht, yo, xc, xcb, rt, it, at, bt = bufs_of(n)
        t0 = tc * TC
        if tc == 0:
            c.op(c.POOL, lambda: nc.gpsimd.memset(xa[:, 0:3], 0.0), writes=[xa])
            c.dma(c.SP, xa[:, 3:TC + 3], xaT[ct * 128:(ct + 1) * 128, 0:TC], writes=[xa])
        else:
            c.dma(c.SP, xa[:, :], xaT[ct * 128:(ct + 1) * 128, t0 - 3:t0 + TC], writes=[xa])
        c.dma(c.SP, sg[:, :], sgaT[ct * 128:(ct + 1) * 128, t0:t0 + TC], writes=[sg])
        c.op(c.DVE, lambda: nc.vector.tensor_scalar(out=xc[:, :], in0=xa[:, 3:TC + 3], scalar1=cw[:, 3, ct:ct + 1], scalar2=cb[:, ct:ct + 1],
                                                    op0=ALU.mult, op1=ALU.add), reads=[xa, cw, cb], writes=[xc])
        for k in range(3):
            c.op(c.DVE, lambda k=k: nc.vector.scalar_tensor_tensor(out=xc[:, :], in0=xa[:, k:k + TC], scalar=cw[:, k, ct:ct + 1], in1=xc[:, :],
                                                                   op0=ALU.mult, op1=ALU.add), reads=[xa, cw], writes=[xc])

    def stage_a2(n):
        ct, tc = items[n]
        xa, sg, ht, yo, xc, xcb, rt, it, at, bt = bufs_of(n)
        c.op(c.ACT, lambda: nc.scalar.copy(out=xcb[:, :], in_=xc[:, :]), reads=[xc], writes=[xcb])
        for j in range(TC // 512):
            sl = slice(j * 512, (j + 1) * 512)
            for which in range(2):
                gp = gps[gcnt[0] % 4]
                gcnt[0] += 1
                c.op(c.PE, lambda gp=gp, which=which, sl=sl: nc.tensor.matmul(gp[:, :], lhsT=wrb[:, which * 4 + ct, :], rhs=xcb[:, sl], start=True, stop=True),
                     reads=[wrb, xcb], writes=[gp])
                dst = rt if which == 0 else it
                bb = brc if which == 0 else bic
                c.op(c.ACT, lambda gp=gp, dst=dst, bb=bb, sl=sl: nc.scalar.activation(out=dst[:, sl], in_=gp[:, :], func=AF.Sigmoid, bias=bb[:, ct:ct + 1]),
                     reads=[gp, bb], writes=[dst])

    def stage_b(n):
        ct, tc = items[n]
        xa, sg, ht, yo, xc, xcb, rt, it, at, bt = bufs_of(n)
        t0 = tc * TC
        c.op(c.ACT, lambda: nc.scalar.activation(out=at[:, :], in_=rt[:, :], func=AF.Exp, scale=c8[:, ct:ct + 1]), reads=[rt, c8], writes=[at])
        c.op(c.ACT, lambda: nc.scalar.activation(out=bt[:, :], in_=rt[:, :], func=AF.Exp, scale=c16[:, ct:ct + 1]), reads=[rt, c16], writes=[bt])
        c.op(c.ACT, lambda: nc.scalar.activation(out=bt[:, :], in_=bt[:, :], func=AF.Sqrt, scale=-1.0, bias=1.0), writes=[bt])
        c.op(c.POOL, lambda: nc.gpsimd.tensor_tensor(out=it[:, :], in0=it[:, :], in1=xc[:, :], op=ALU.mult), reads=[xc], writes=[it])

    def stage_b2(n):
        ct, tc = items[n]
        xa, sg, ht, yo, xc, xcb, rt, it, at, bt = bufs_of(n)
        t0 = tc * TC
        c.op(c.DVE, lambda: nc.vector.tensor_tensor(out=bt[:, :], in0=bt[:, :], in1=it[:, :], op=ALU.mult), reads=[it], writes=[bt])
        if tc == 0:
            c.op(c.DVE, lambda: nc.vector.tensor_tensor_scan(out=ht[:, :], data0=at[:, :], data1=bt[:, :], initial=0.0, op0=ALU.mult, op1=ALU.add),
                 reads=[at, bt], writes=[ht])
        else:
            ph = hts[(n - 1) % 2]
            c.op(c.DVE, lambda: nc.vector.tensor_tensor_scan(out=ht[:, :], data0=at[:, :], data1=bt[:, :], initial=ph[:, TC - 1:TC],
                                                             op0=ALU.mult, op1=ALU.add),
                 reads=[at, bt, ph], writes=[ht])
        c.op(c.POOL, lambda: nc.gpsimd.tensor_tensor(out=yo[:, :], in0=ht[:, :], in1=sg[:, :], op=ALU.mult), reads=[ht, sg], writes=[yo])
        c.dma(c.ST, yT(ct)[:, t0:t0 + TC], yo[:, :], reads=[yo])

    stage_a(0)
    stage_a2(0)
    for n in range(len(items)):
        more = n + 1 < len(items)
        if more:
            stage_a(n + 1)
        stage_b(n)
        if more:
            stage_a2(n + 1)
        stage_b2(n)


def attn_phase(c, ec, nc, qT, kT, vS, sgbT, yT, nqg):
    NS = 3
    ones = c.sb([128, 512], F32, "aones", ec)
    c.op(c.POOL, lambda: nc.gpsimd.memset(ones[:, :], 1.0), writes=[ones])
    uin = c.sb([128, 128], BF16, "uin", ec)
    ucm = c.sb([128, 128], BF16, "ucm", ec)
    c.op(c.POOL, lambda: nc.gpsimd.affine_select(out=uin[:, :], in_=ones[:, 0:128], pattern=[[-1, 128]], compare_op=ALU.is_ge, fill=0.0,
                                                 base=0, channel_multiplier=1), reads=[ones], writes=[uin])
    c.op(c.POOL, lambda: nc.gpsimd.affine_select(out=ucm[:, :], in_=ones[:, 0:128], pattern=[[1, 128]], compare_op=ALU.is_gt, fill=0.0,
                                                 base=0, channel_multiplier=-1), reads=[ones], writes=[ucm])
    masks = []
    for jj in range(4):
        m = c.sb([128, 512], F32, "mask", ec)
        c.op(c.POOL, lambda m=m, jj=jj: nc.gpsimd.affine_select(out=m[:, :], in_=ones[:, :], pattern=[[1, 512]], compare_op=ALU.is_gt, fill=0.0,
                                                                base=-jj * 128, channel_multiplier=-1), reads=[ones], writes=[m])
        masks.append(m)
    KT = [c.sb([128, S], BF16, "KT", ec) for _ in range(2)]
    QT = [c.sb([128, S], BF16, "QT", ec) for _ in range(2)]
    VV = [c.sb([128, 64, 128], BF16, "VV", ec) for _ in range(2)]
    zps2 = [c.ps("zps", ec) for _ in range(2)]
    zcnt = [0]
    cps = [c.ps("cps", ec) for _ in range(NS)]
    ops_ = [c.ps("ops", ec) for _ in range(NS)]
    NB = 4
    ebuf = [[c.sb([128, 512], F32, "e", ec) for _ in range(NB)] for _ in range(NS)]
    spb = [[c.sb([128, 512], BF16, "sp", ec) for _ in range(NB)] for _ in range(NS)]
    exb = [[c.sb([128, 512], F32, "ex", ec) for _ in range(NB)] for _ in range(NS)]
    wb = [[c.sb([128, 512], BF16, "w", ec) for _ in range(NB)] for _ in range(NS)]
    sgq = [c.sb([128, 512], F32, "sgq", ec) for _ in range(NS)]
    yob = [c.sb([128, 512], BF16, "yo", ec) for _ in range(NS)]
    cnt = [0] * NS

    def stream_rounds(s, h, hb, qg, tiles):
        K_, Q_, V_ = KT[hb], QT[hb], VV[hb]
        cp, op_ = cps[s], ops_[s]
        qsl = slice(qg * 512, (qg + 1) * 512)
        n = len(tiles)
        base = cnt[s]
        cnt[s] += n

        def bufs(i):
            j = (base + i) % NB
            return ebuf[s][j], spb[s][j], exb[s][j], wb[s][j]

        def stage1(i):
            kb = tiles[i]
            e, sp, ex, w = bufs(i)
            zp = zps2[zcnt[0] % 2]
            zcnt[0] += 1
            c.op(c.PE, lambda: nc.tensor.matmul(zp[:, :], lhsT=K_[:, kb * 128:(kb + 1) * 128], rhs=Q_[:, qsl], start=True, stop=True),
                 reads=[K_, Q_], writes=[zp])
            c.op(c.ACT, lambda: nc.scalar.activation(out=e[:, :], in_=zp[:, :], func=AF.Exp), reads=[zp], writes=[e])
            jj = kb - 4 * qg
            if jj >= 0:
                spf = ex
                c.op(c.ACT, lambda: nc.scalar.activation(out=spf[:, :], in_=e[:, :], func=AF.Ln, bias=1.0), reads=[e], writes=[spf])
                c.op(c.DVE, lambda: nc.vector.tensor_tensor(out=sp[:, :], in0=spf[:, :], in1=masks[jj][:, :], op=ALU.mult), reads=[spf, masks[jj]], writes=[sp])
            else:
                c.op(c.ACT, lambda: nc.scalar.activation(out=sp[:, :], in_=e[:, :], func=AF.Ln, bias=1.0), reads=[e], writes=[sp])

        def mid(i):
            kb = tiles[i]
            e, sp, ex, w = bufs(i)
            jj = kb - 4 * qg
            c.op(c.PE, lambda: nc.tensor.matmul(cp[:, :], lhsT=uin[:, :], rhs=sp[:, :], start=(i == 0), stop=False, skip_group_check=True),
                 reads=[uin, sp], writes=[cp])
            c.op(c.ACT, lambda: nc.scalar.activation(out=ex[:, :], in_=cp[:, :], func=AF.Exp, scale=-1.0), reads=[cp], writes=[ex])
            if jj >= 0:
                c.op(c.DVE, lambda: nc.vector.tensor_tensor(out=ex[:, :], in0=ex[:, :], in1=masks[jj][:, :], op=ALU.mult), reads=[masks[jj]], writes=[ex])
            c.op(c.DVE, lambda: nc.vector.tensor_tensor(out=w[:, :], in0=e[:, :], in1=ex[:, :], op=ALU.mult), reads=[e, ex], writes=[w])

        def tail(i):
            kb = tiles[i]
            e, sp, ex, w = bufs(i)
            last = (i == n - 1)
            c.op(c.PE, lambda: nc.tensor.matmul(cp[:, :], lhsT=ucm[:, :], rhs=sp[:, :], start=False, stop=last, skip_group_check=True),
                 reads=[ucm, sp], writes=[cp])
            c.op(c.PE, lambda: nc.tensor.matmul(op_[:, :], lhsT=V_[:, kb, :], rhs=w[:, :], start=(i == 0), stop=last),
                 reads=[V_, w], writes=[op_])

        stage1(0)
        yield
        for r in range(n + 1):
            if r >= 1:
                tail(r - 1)
            if r + 1 < n:
                stage1(r + 1)
            if r < n:
                mid(r)
            yield

    def finish(s, h, qg):
        qsl = slice(qg * 512, (qg + 1) * 512)
        sg = sgq[s]
        yo = yob[s]
        c.dma(c.SP, sg[:, :], sgbT[h * 128:(h + 1) * 128, qsl], writes=[sg])
        c.op(c.DVE, lambda: nc.vector.tensor_tensor(out=yo[:, :], in0=ops_[s][:, :], in1=sg[:, :], op=ALU.mult), reads=[ops_[s], sg], writes=[yo])
        c.dma(c.ST, yT(4 + h)[:, qsl], yo[:, :], reads=[yo])

    def load_head(h):
        hb = h % 2
        c.dma(c.SP, KT[hb][:, :], kT[h * 128:(h + 1) * 128, :], writes=[KT[hb]])
        c.dma(c.SP, QT[hb][:, :], qT[h * 128:(h + 1) * 128, :], writes=[QT[hb]])
        vv = vS.rearrange("(kb p) c -> p kb c", p=128)
        for part in range(4):
            c.dma(c.SP, VV[hb][:, part * 16:(part + 1) * 16, :], vv[:, part * 16:(part + 1) * 16, h * 128:(h + 1) * 128], writes=[VV[hb]])

    jobs = [(h, qg) for h in range(4) for qg in range(nqg - 1, -1, -1)]
    loaded = set()
    active = [None] * NS
    ji = 0
    while True:
        busy = False
        for s_ in range(NS):
            if active[s_] is None and ji < len(jobs):
                h, qg = jobs[ji]
                ji += 1
                if h not in loaded:
                    load_head(h)
                    loaded.add(h)
                active[s_] = (stream_rounds(s_, h, h % 2, qg, list(range(4 * qg + 3, -1, -1))), h, qg)
            if active[s_] is not None:
                busy = True
                gen, h, qg = active[s_]
                try:
                    next(gen)
                except StopIteration:
                    finish(s_, h, qg)
                    active[s_] = None
        if not busy and ji >= len(jobs):
            break


_CACHE = {}


def _get(name, fn, *a, **k):
    if name not in _CACHE:
        _CACHE[name] = fn(*a, **k)
    return _CACHE[name]


def l1_inputs(inp, b, hf):
    f = lambda a: np.ascontiguousarray(a, dtype=np.float32)
    w = inp["w_in_even"][0]
    cs = slice(hf * 512, (hf + 1) * 512)
    cols = np.concatenate([w[:, 0:1024][:, cs], w[:, 1024:2048][:, cs], w[:, 2048:3072][:, cs], w[:, 3072:4096][:, cs],
                           w[:, 4096:5120][:, cs], w[:, 5120:6144][:, cs]], axis=1)
    hs = slice(hf * 4, (hf + 1) * 4)
    return {
        "x": f(inp["x"][b]), "cvec": f(inp["c"][b]), "adaw": f(inp["ada_w"][0][:, :2048]), "adab": f(inp["ada_b"][0][:2048]),
        "ng": f(inp["norm_g"][0]), "win": f(cols), "convw": f(inp["conv_w"][0][:, cs]), "convb": f(inp["conv_b"][0][cs]),
        "wr": f(inp["lru_wr"][0][hs]), "wi": f(inp["lru_wi"][0][hs]), "br": f(inp["lru_br"][0][cs]), "bi": f(inp["lru_bi"][0][cs]),
        "lam": f(inp["lru_lambda"][0][cs]), "qg": f(inp["q_norm_g"][0]), "kg": f(inp["k_norm_g"][0]),
    }


def _run(nc, maps):
    res = run_bass_kernel_spmd(nc, maps, core_ids=list(range(len(maps))))
    return res.results


def kernel_unfused(**inputs):
    inp = {k: np.asarray(v) for k, v in inputs.items()}
    cores = [(b, h) for b in range(4) for h in range(2)]
    r1 = _run(_get("l1", build_l1), [l1_inputs(inp, b, h) for b, h in cores])
    yfull = []
    for b in range(4):
        y0 = np.asarray(r1[2 * b]["yT"])
        y1 = np.asarray(r1[2 * b + 1]["yT"])
        yfull.append(np.concatenate([y0[:512], y1[:512], y0[512:], y1[512:]], axis=0))
    r2 = _run(_get("l2", build_l2), [l2_inputs(inp, b, t, yfull[b]) for b, t in cores])
    m3 = []
    for b, h in cores:
        uh = np.concatenate([np.asarray(r2[2 * b + t]["uT"])[h * 512:(h + 1) * 512] for t in range(2)], axis=1)
        m3.append(l3_inputs(inp, h, uh))
    r3 = _run(_get("l3", build_l3), m3)
    m4 = []
    for b, t in cores:
        y5 = np.concatenate([np.asarray(r3[2 * b + h]["y5T"])[:, t * 4096:(t + 1) * 4096] for h in range(2)], axis=0)
        m4.append(l4_inputs(inp, b, y5, np.asarray(r2[2 * b + t]["sgT"]), np.asarray(r2[2 * b + t]["x1o"])))
    r4 = _run(_get("l4", build_l4), m4)
    out = np.empty((4, S, D), np.float32)
    for b, t in cores:
        out[b, t * 4096:(t + 1) * 4096] = np.asarray(r4[2 * b + t]["x2o"])
    return out


def residual_tile(c, nc, ps2, gate_bc, xt, x1, tmp):
    for nh in range(2):
        sl = slice(nh * 512, (nh + 1) * 512)
        c.op(c.DVE, lambda nh=nh, sl=sl: nc.vector.tensor_tensor(out=tmp[:, sl], in0=ps2[nh][:, :], in1=gate_bc[:, sl], op=ALU.mult),
             reads=[ps2[nh], gate_bc], writes=[tmp])
    c.op(c.POOL, lambda: nc.gpsimd.tensor_tensor(out=x1[:, :], in0=tmp[:, :], in1=xt[:, :], op=ALU.add), reads=[tmp, xt], writes=[x1])


def build_l2(ntok=4096):
    nc = bass.Bass("TRN2", target_bir_lowering=False)
    dt = nc.dram_tensor
    yf = dt("yf", [2048, ntok], BF16, kind="ExternalInput").ap()
    x = dt("x", [ntok, D], F32, kind="ExternalInput").ap()
    cvec = dt("cvec", [D], F32, kind="ExternalInput").ap()
    adaw0 = dt("adaw0", [D, 3 * D], F32, kind="ExternalInput").ap()
    adab0 = dt("adab0", [3 * D], F32, kind="ExternalInput").ap()
    adaw1 = dt("adaw1", [D, 3 * D], F32, kind="ExternalInput").ap()
    adab1 = dt("adab1", [3 * D], F32, kind="ExternalInput").ap()
    ng1 = dt("ng1", [D], F32, kind="ExternalInput").ap()
    wout = dt("wout", [2048, D], F32, kind="ExternalInput").ap()
    win1 = dt("win1", [D, 2048], F32, kind="ExternalInput").ap()
    x1o = dt("x1o", [ntok, D], F32, kind="ExternalOutput").ap()
    uT = dt("uT", [D, ntok], BF16, kind="ExternalOutput").ap()
    sgT = dt("sgT", [D, ntok], F32, kind="ExternalOutput").ap()
    with ExitStack() as es:
        c = Ctx(nc, es)
        ones, ident = make_consts(c)
        gate0 = c.sb([128, D], F32, "gate0p")
        gmod = c.sb([128, D], F32, "gmod")
        shift = c.sb([128, D], F32, "shift")
        Wo = c.sb([128, 16, 1024], BF16, "Wo")
        W1 = c.sb([128, 8, 2048], BF16, "W1")
        with ExitStack() as e0:
            stage = [c.sb([128, 8, 512], F32, "wst", e0) for _ in range(2)]
            g0t = ada_mod_bc(c, e0, ones, cvec, adaw0, adab0, 2048, 1024, stage[0], "gate0")
            c.op(c.DVE, lambda: nc.vector.tensor_copy(out=gate0[:, :], in_=g0t[:, :]), reads=[g0t], writes=[gate0])
            c.barrier()
        with ExitStack() as e0:
            stage = [c.sb([128, 8, 512], F32, "wst", e0) for _ in range(2)]
            mod1 = ada_mod_bc(c, e0, ones, cvec, adaw1, adab1, 0, 2048, stage[1], "mod1")
            ngb = c.sb([128, D], F32, "ngb", e0)
            c.dma(c.SP, ngb[:, :], ng1.partition_broadcast(128), writes=[ngb])
            c.op(c.DVE, lambda: nc.vector.scalar_tensor_tensor(out=gmod[:, :], in0=mod1[:, 1024:2048], scalar=1.0, in1=ngb[:, :],
                                                               op0=ALU.add, op1=ALU.mult), reads=[mod1, ngb], writes=[gmod])
            c.op(c.DVE, lambda: nc.vector.tensor_copy(out=shift[:, :], in_=mod1[:, 0:1024]), reads=[mod1], writes=[shift])
            load_w_bf16(c, wout, 16, 1024, stage, "Wo", dst=Wo)
            load_w_bf16(c, win1, 8, 2048, stage, "W1", dst=W1)
            c.barrier()
        ybufs = [c.sb([128, 16, 512], BF16, "ych") for _ in range(2)]
        xbufs = [c.sb([128, D], F32, "xt") for _ in range(2)]
        x1bufs = [c.sb([128, D], F32, "x1") for _ in range(3)]
        tmpb = c.sb([128, D], F32, "tmp")
        hbufs = [c.sb([128, D], F32, "hb") for _ in range(2)]
        small = [(c.sb([128, 1], F32, "ssq"), c.sb([128, 1], F32, "rstd")) for _ in range(3)]
        hTs = [c.sb([128, 8, 512], BF16, "hT") for _ in range(2)]
        trps = [c.ps("trps") for _ in range(2)]
        ops2 = [c.ps("ops2") for _ in range(2)]
        mmps = [c.ps("mmps") for _ in range(3)]
        outf = [c.sb([128, 512], F32, "outf") for _ in range(3)]
        outb = [c.sb([128, 512], BF16, "outb") for _ in range(3)]
        yv = yf.rearrange("(kt p) t -> p kt t", p=128)
        mi = fi = bi_ = 0
        for ch in range(ntok // 512):
            tsl = slice(ch * 512, (ch + 1) * 512)
            ych = ybufs[ch % 2]
            c.dma(c.SP, ych[:, :, :], yv[:, :, tsl], writes=[ych])
            hT = hTs[ch % 2]
            for i4 in range(4):
                i = ch * 4 + i4
                xt = xbufs[i % 2]
                c.dma(c.SP, xt[:, :], x[i * 128:(i + 1) * 128, :], writes=[xt])
                for nh in range(2):
                    for kt in range(16):
                        c.op(c.PE, lambda kt=kt, nh=nh, i4=i4: nc.tensor.matmul(ops2[nh][:, :], lhsT=ych[:, kt, i4 * 128:(i4 + 1) * 128],
                                                                                 rhs=Wo[:, kt, nh * 512:(nh + 1) * 512], start=(kt == 0), stop=(kt == 15)),
                             reads=[ych, Wo], writes=[ops2[nh]])
                x1 = x1bufs[i % 3]
                residual_tile(c, nc, ops2, gate0, xt, x1, tmpb)
                c.dma(c.ST, x1o[i * 128:(i + 1) * 128, :], x1[:, :], reads=[x1])
                norm_transpose_tile(c, None, x1bufs, hbufs, i, gmod, shift, ident, trps, hT, i4 * 128, small)
            for nt in range(16):
                pm = mmps[mi % 3]
                mi += 1
                for kt in range(8):
                    c.op(c.PE, lambda kt=kt, nt=nt, pm=pm: nc.tensor.matmul(pm[:, :], lhsT=W1[:, kt, nt * 128:(nt + 1) * 128], rhs=hT[:, kt, :],
                                                                            start=(kt == 0), stop=(kt == 7)), reads=[W1, hT], writes=[pm])
                if nt < 8:
                    ob = outb[bi_ % 3]
                    bi_ += 1
                    c.op(c.DVE, lambda ob=ob, pm=pm: nc.vector.tensor_copy(out=ob[:, :], in_=pm[:, :]), reads=[pm], writes=[ob])
                    c.dma(c.ST, uT[nt * 128:(nt + 1) * 128, tsl], ob[:, :], reads=[ob])
                else:
                    of = outf[fi % 3]
                    fi += 1
                    c.op(c.ACT, lambda of=of, pm=pm: nc.scalar.activation(out=of[:, :], in_=pm[:, :], func=AF.Silu), reads=[pm], writes=[of])
                    c.dma(c.ST, sgT[(nt - 8) * 128:(nt - 7) * 128, tsl], of[:, :], reads=[of])
        c.barrier()
    return nc


def build_l4(ntok=4096):
    nc = bass.Bass("TRN2", target_bir_lowering=False)
    dt = nc.dram_tensor
    y5 = dt("y5", [D, ntok], BF16, kind="ExternalInput").ap()
    sgT = dt("sgT", [D, ntok], F32, kind="ExternalInput").ap()
    x1 = dt("x1", [ntok, D], F32, kind="ExternalInput").ap()
    cvec = dt("cvec", [D], F32, kind="ExternalInput").ap()
    adaw1 = dt("adaw1", [D, 3 * D], F32, kind="ExternalInput").ap()
    adab1 = dt("adab1", [3 * D], F32, kind="ExternalInput").ap()
    gluw = dt("gluw", [D, D], F32, kind="ExternalInput").ap()
    glub = dt("glub", [D], F32, kind="ExternalInput").ap()
    wout = dt("wout", [D, D], F32, kind="ExternalInput").ap()
    x2o = dt("x2o", [ntok, D], F32, kind="ExternalOutput").ap()
    with ExitStack() as es:
        c = Ctx(nc, es)
        ones, ident = make_consts(c)
        gate1 = c.sb([128, D], F32, "gate1p")
        Wg = c.sb([128, 8, 1024], BF16, "Wg")
        Wo = c.sb([128, 8, 1024], BF16, "Wo")
        with ExitStack() as e0:
            stage = [c.sb([128, 8, 512], F32, "wst", e0) for _ in range(2)]
            g1t = ada_mod_bc(c, e0, ones, cvec, adaw1, adab1, 2048, 1024, stage[0], "gate1")
            c.op(c.DVE, lambda: nc.vector.tensor_copy(out=gate1[:, :], in_=g1t[:, :]), reads=[g1t], writes=[gate1])
            load_w_bf16(c, gluw, 8, 1024, stage, "Wg", dst=Wg)
            load_w_bf16(c, wout, 8, 1024, stage, "Wo", dst=Wo)
            c.barrier()
        gb = load_cols(c, glub, 8, "glub")
        ybufs = [c.sb([128, 8, 512], BF16, "y5c") for _ in range(2)]
        sbufs = [c.sb([128, 8, 512], F32, "sgc") for _ in range(2)]
        ygs = [c.sb([128, 8, 512], BF16, "yg") for _ in range(2)]
        sig = [c.sb([128, 512], F32, "sig") for _ in range(2)]
        tt = [c.sb([128, 512], F32, "tt") for _ in range(2)]
        xbufs = [c.sb([128, D], F32, "xt") for _ in range(2)]
        x2bufs = [c.sb([128, D], F32, "x2") for _ in range(2)]
        tmpb = c.sb([128, D], F32, "tmp")
        zps = [c.ps("zps") for _ in range(3)]
        ops2 = [c.ps("ops2") for _ in range(2)]
        yv = y5.rearrange("(kt p) t -> p kt t", p=128)
        sv = sgT.rearrange("(kt p) t -> p kt t", p=128)
        zi = 0
        for ch in range(ntok // 512):
            tsl = slice(ch * 512, (ch + 1) * 512)
            ych = ybufs[ch % 2]
            sch = sbufs[ch % 2]
            yg = ygs[ch % 2]
            c.dma(c.SP, ych[:, :, :], yv[:, :, tsl], writes=[ych])
            c.dma(c.SP, sch[:, :, :], sv[:, :, tsl], writes=[sch])
            for nt in range(8):
                zp = zps[zi % 3]
                sg_ = sig[zi % 2]
                t_ = tt[zi % 2]
                zi += 1
                for kt in range(8):
                    c.op(c.PE, lambda kt=kt, nt=nt, zp=zp: nc.tensor.matmul(zp[:, :], lhsT=Wg[:, kt, nt * 128:(nt + 1) * 128], rhs=ych[:, kt, :],
                                                                            start=(kt == 0), stop=(kt == 7)), reads=[Wg, ych], writes=[zp])
                c.op(c.ACT, lambda zp=zp, sg_=sg_, nt=nt: nc.scalar.activation(out=sg_[:, :], in_=zp[:, :], func=AF.Sigmoid, bias=gb[:, nt:nt + 1]),
                     reads=[zp, gb], writes=[sg_])
                c.op(c.DVE, lambda sg_=sg_, t_=t_, nt=nt: nc.vector.tensor_tensor(out=t_[:, :], in0=sg_[:, :], in1=ych[:, nt, :], op=ALU.mult),
                     reads=[sg_, ych], writes=[t_])
                c.op(c.POOL, lambda t_=t_, nt=nt: nc.gpsimd.tensor_tensor(out=yg[:, nt, :], in0=t_[:, :], in1=sch[:, nt, :], op=ALU.mult),
                     reads=[t_, sch], writes=[yg])
            for i4 in range(4):
                i = ch * 4 + i4
                xt = xbufs[i % 2]
                c.dma(c.SP, xt[:, :], x1[i * 128:(i + 1) * 128, :], writes=[xt])
                for nh in range(2):
                    for kt in range(8):
                        c.op(c.PE, lambda kt=kt, nh=nh, i4=i4: nc.tensor.matmul(ops2[nh][:, :], lhsT=yg[:, kt, i4 * 128:(i4 + 1) * 128],
                                                                                 rhs=Wo[:, kt, nh * 512:(nh + 1) * 512], start=(kt == 0), stop=(kt == 7)),
                             reads=[yg, Wo], writes=[ops2[nh]])
                x2 = x2bufs[i % 2]
                residual_tile(c, nc, ops2, gate1, xt, x2, tmpb)
                c.dma(c.ST, x2o[i * 128:(i + 1) * 128, :], x2[:, :], reads=[x2])
        c.barrier()
    return nc


S5_IN = lambda G: [("lre", [G, 64], F32), ("lim", [G, 64], F32), ("ldt", [G], F32), ("bre", [G, 64, 16], F32), ("bim", [G, 64, 16], F32),
                   ("cre", [G * 16, 64], F32), ("cim", [G * 16, 64], F32), ("dsk", [G * 16], F32)]


def s5_body(c, nc, ones, ident, A, uT, y5T, T=S, ngrp=32, after_ct=None):
    G = ngrp
    NCT = G // 8
    lre, lim, ldt, bre, bim, cre, cim, dsk = [A[k[0]] for k in S5_IN(G)]
    M = T // 8
    NMC = M // 512
    NL = int(np.log2(M))
    PI = float(np.pi)
    with ExitStack() as es:
        V, P_, A_ = c.DVE, c.POOL, c.ACT

        def vts(out, in0, s1, s2=None, op0=ALU.mult, op1=None, reads=(), writes=()):
            kw = dict(out=out, in0=in0, scalar1=s1, scalar2=s2, op0=op0)
            if op1 is not None:
                kw["op1"] = op1
            return c.op(V, lambda: nc.vector.tensor_scalar(**kw), reads=reads, writes=writes)

        def vtt(out, in0, in1, op, reads=(), writes=()):
            return c.op(V, lambda: nc.vector.tensor_tensor(out=out, in0=in0, in1=in1, op=op), reads=reads, writes=writes)

        def vstt(out, in0, sc, in1, op0, op1, reads=(), writes=()):
            return c.op(V, lambda: nc.vector.scalar_tensor_tensor(out=out, in0=in0, scalar=sc, in1=in1, op0=op0, op1=op1), reads=reads, writes=writes)

        selR = c.sb([128, 1], F32, "selR", es)
        selI = c.sb([128, 1], F32, "selI", es)
        sgn = c.sb([128, 1], F32, "sgn", es)
        c.op(P_, lambda: nc.gpsimd.affine_select(out=selR[:, :], in_=ones[:, 0:1], pattern=[[0, 1]], compare_op=ALU.is_gt, fill=0.0,
                                                 base=64, channel_multiplier=-1), reads=[ones], writes=[selR])
        vts(selI[:, :], selR[:, :], -1.0, 1.0, ALU.mult, ALU.add, reads=[selR], writes=[selI])
        vts(sgn[:, :], selR[:, :], 2.0, -1.0, ALU.mult, ALU.add, reads=[selR], writes=[sgn])
        swp = c.sb([128, 128], F32, "swp", es)
        c.op(P_, lambda: nc.gpsimd.memset(swp[:, :], 0.0), writes=[swp])
        c.op(P_, lambda: nc.gpsimd.tensor_copy(out=swp[0:64, 64:128], in_=ident[0:64, 0:64]), reads=[ident], writes=[swp])
        c.op(P_, lambda: nc.gpsimd.tensor_copy(out=swp[64:128, 0:64], in_=ident[64:128, 64:128]), reads=[ident], writes=[swp])

        def dup_load(ap2d, name):
            t = c.sb([128, G], F32, name, es)
            v = ap2d.rearrange("g p -> p g")
            c.dma(c.SP, t[0:64, :], v, writes=[t], allow_slow_non_contiguous=True)
            c.dma(c.SP, t[64:128, :], v, writes=[t], allow_slow_non_contiguous=True)
            return t
        lr = dup_load(lre, "lr")
        li = dup_load(lim, "li")
        dtb = c.sb([128, G], F32, "dtb", es)
        c.dma(c.SP, dtb[:, :], ldt.partition_broadcast(128), writes=[dtb])
        c.op(A_, lambda: nc.scalar.activation(out=dtb[:, :], in_=dtb[:, :], func=AF.Exp), writes=[dtb])
        dec = c.sb([128, G], F32, "dec", es)
        ang = c.sb([128, G], F32, "ang", es)
        vtt(dec[:, :], lr[:, :], dtb[:, :], ALU.mult, reads=[lr, dtb], writes=[dec])
        c.op(A_, lambda: nc.scalar.activation(out=dec[:, :], in_=dec[:, :], func=AF.Exp), writes=[dec])
        vtt(ang[:, :], li[:, :], dtb[:, :], ALU.mult, reads=[li, dtb], writes=[ang])
        xk = c.sb([128, G], F32, "xk", es)
        xi = c.sb([128, G], mybir.dt.int32, "xi", es)
        vts(xk[:, :], ang[:, :], 1.0 / (2 * PI), reads=[ang], writes=[xk])
        c.op(V, lambda: nc.vector.tensor_copy(out=xi[:, :], in_=xk[:, :]), reads=[xk], writes=[xi])
        c.op(V, lambda: nc.vector.tensor_copy(out=xk[:, :], in_=xi[:, :]), reads=[xi], writes=[xk])
        vstt(ang[:, :], xk[:, :], -2 * PI, ang[:, :], ALU.mult, ALU.add, reads=[xk], writes=[ang])
        sq_ = c.sb([128, G], F32, "sq", es)
        cq_ = c.sb([128, G], F32, "cq", es)
        hpi = c.sb([128, 1], F32, "hpi", es)
        c.op(P_, lambda: nc.gpsimd.memset(hpi[:, :], PI / 2), writes=[hpi])
        c.op(A_, lambda: nc.scalar.activation(out=sq_[:, :], in_=ang[:, :], func=AF.Sin, scale=0.25), reads=[ang], writes=[sq_])
        c.op(A_, lambda: nc.scalar.activation(out=cq_[:, :], in_=ang[:, :], func=AF.Sin, scale=0.25, bias=hpi[:, 0:1]), reads=[ang, hpi], writes=[cq_])
        t1 = c.sb([128, G], F32, "t1", es)
        t2 = c.sb([128, G], F32, "t2", es)
        t3 = c.sb([128, G], F32, "t3", es)
        for _ in range(2):
            vtt(t1[:, :], cq_[:, :], cq_[:, :], ALU.mult, reads=[cq_], writes=[t1])
            vtt(t2[:, :], sq_[:, :], sq_[:, :], ALU.mult, reads=[sq_], writes=[t2])
            vtt(t3[:, :], sq_[:, :], cq_[:, :], ALU.mult, reads=[sq_, cq_], writes=[t3])
            vtt(cq_[:, :], t1[:, :], t2[:, :], ALU.subtract, reads=[t1, t2], writes=[cq_])
            vts(sq_[:, :], t3[:, :], 2.0, reads=[t3], writes=[sq_])
        Ar = [c.sb([128, G], F32, "Ar", es) for _ in range(9)]
        Ai = [c.sb([128, G], F32, "Ai", es) for _ in range(9)]
        c.op(P_, lambda: nc.gpsimd.memset(Ar[0][:, :], 1.0), writes=[Ar[0]])
        c.op(P_, lambda: nc.gpsimd.memset(Ai[0][:, :], 0.0), writes=[Ai[0]])
        vtt(Ar[1][:, :], dec[:, :], cq_[:, :], ALU.mult, reads=[dec, cq_], writes=[Ar[1]])
        vtt(Ai[1][:, :], dec[:, :], sq_[:, :], ALU.mult, reads=[dec, sq_], writes=[Ai[1]])

        def cmul(orr, oi, ar, ai, br_, bi__):
            vtt(t1[:, :], ar[:, :], br_[:, :], ALU.mult, reads=[ar, br_], writes=[t1])
            vtt(t2[:, :], ai[:, :], bi__[:, :], ALU.mult, reads=[ai, bi__], writes=[t2])
            vtt(t3[:, :], ar[:, :], bi__[:, :], ALU.mult, reads=[ar, bi__], writes=[t3])
            vtt(orr[:, :], t1[:, :], t2[:, :], ALU.subtract, reads=[t1, t2], writes=[orr])
            vtt(t1[:, :], ai[:, :], br_[:, :], ALU.mult, reads=[ai, br_], writes=[t1])
            vtt(oi[:, :], t3[:, :], t1[:, :], ALU.add, reads=[t3, t1], writes=[oi])
        for k in range(2, 9):
            cmul(Ar[k], Ai[k], Ar[k - 1], Ai[k - 1], Ar[1], Ai[1])
        Sr = [Ar[8]] + [c.sb([128, G], F32, "Sr", es) for _ in range(NL - 1)]
        Si = [Ai[8]] + [c.sb([128, G], F32, "Si", es) for _ in range(NL - 1)]
        for l in range(1, NL):
            cmul(Sr[l], Si[l], Sr[l - 1], Si[l - 1], Sr[l - 1], Si[l - 1])
        cr_ = c.sb([128, G], F32, "cr", es)
        ci_ = c.sb([128, G], F32, "ci", es)
        den = c.sb([128, G], F32, "den", es)
        nre = c.sb([128, G], F32, "nre", es)
        vtt(t1[:, :], lr[:, :], lr[:, :], ALU.mult, reads=[lr], writes=[t1])
        vtt(t2[:, :], li[:, :], li[:, :], ALU.mult, reads=[li], writes=[t2])
        vtt(den[:, :], t1[:, :], t2[:, :], ALU.add, reads=[t1, t2], writes=[den])
        c.op(V, lambda: nc.vector.reciprocal(out=den[:, :], in_=den[:, :]), writes=[den])
        vts(nre[:, :], Ar[1][:, :], -1.0, None, ALU.add, reads=[Ar[1]], writes=[nre])
        vtt(t1[:, :], nre[:, :], lr[:, :], ALU.mult, reads=[nre, lr], writes=[t1])
        vtt(t2[:, :], Ai[1][:, :], li[:, :], ALU.mult, reads=[Ai[1], li], writes=[t2])
        vtt(t1[:, :], t1[:, :], t2[:, :], ALU.add, reads=[t2], writes=[t1])
        vtt(cr_[:, :], t1[:, :], den[:, :], ALU.mult, reads=[t1, den], writes=[cr_])
        vtt(t1[:, :], Ai[1][:, :], lr[:, :], ALU.mult, reads=[Ai[1], lr], writes=[t1])
        vtt(t2[:, :], nre[:, :], li[:, :], ALU.mult, reads=[nre, li], writes=[t2])
        vtt(t1[:, :], t1[:, :], t2[:, :], ALU.subtract, reads=[t2], writes=[t1])
        vtt(ci_[:, :], t1[:, :], den[:, :], ALU.mult, reads=[t1, den], writes=[ci_])
        al = [c.sb([128, G], F32, "al", es) for _ in range(9)]
        be = [c.sb([128, G], F32, "be", es) for _ in range(9)]
        ga = [c.sb([128, G], F32, "ga", es) for _ in range(8)]
        de = [c.sb([128, G], F32, "de", es) for _ in range(8)]
        for k in range(9):
            vts(t1[:, :], Ai[k][:, :], selI[:, 0:1], reads=[Ai[k], selI], writes=[t1])
            vstt(al[k][:, :], Ar[k][:, :], selR[:, 0:1], t1[:, :], ALU.mult, ALU.subtract, reads=[Ar[k], selR, t1], writes=[al[k]])
            vts(t1[:, :], Ar[k][:, :], selI[:, 0:1], reads=[Ar[k], selI], writes=[t1])
            vstt(be[k][:, :], Ai[k][:, :], selR[:, 0:1], t1[:, :], ALU.mult, ALU.add, reads=[Ai[k], selR, t1], writes=[be[k]])
            if k < 8:
                vts(t1[:, :], Ai[k][:, :], selI[:, 0:1], reads=[Ai[k], selI], writes=[t1])
                vstt(ga[k][:, :], Ar[k][:, :], selR[:, 0:1], t1[:, :], ALU.mult, ALU.add, reads=[Ar[k], selR, t1], writes=[ga[k]])
                vts(t1[:, :], Ai[k][:, :], selR[:, 0:1], reads=[Ai[k], selR], writes=[t1])
                vstt(de[k][:, :], Ar[k][:, :], selI[:, 0:1], t1[:, :], ALU.mult, ALU.subtract, reads=[Ar[k], selI, t1], writes=[de[k]])
        So = [c.sb([128, G], F32, "So", es) for _ in range(NL)]
        for l in range(NL):
            vts(So[l][:, :], Si[l][:, :], sgn[:, 0:1], reads=[Si[l], sgn], writes=[So[l]])

        def dup_load3(ap3, name):
            t = c.sb([128, G, 16], F32, name, es)
            v = ap3.rearrange("g p c -> p g c")
            c.dma(c.SP, t[0:64, :, :], v, writes=[t])
            c.dma(c.SP, t[64:128, :, :], v, writes=[t])
            return t
        Br = dup_load3(bre, "Br")
        Bi = dup_load3(bim, "Bi")
        Bbr = c.sb([128, G, 16], F32, "Bbr", es)
        Bbi = c.sb([128, G, 16], F32, "Bbi", es)
        tb = c.sb([128, 16], F32, "tb", es)
        for g in range(G):
            vts(tb[:, :], Bi[:, g, :], ci_[:, g:g + 1], reads=[Bi, ci_], writes=[tb])
            vstt(Bbr[:, g, :], Br[:, g, :], cr_[:, g:g + 1], tb[:, :], ALU.mult, ALU.subtract, reads=[Br, cr_, tb], writes=[Bbr])
            vts(tb[:, :], Br[:, g, :], ci_[:, g:g + 1], reads=[Br, ci_], writes=[tb])
            vstt(Bbi[:, g, :], Bi[:, g, :], cr_[:, g:g + 1], tb[:, :], ALU.mult, ALU.add, reads=[Bi, cr_, tb], writes=[Bbi])
        Cr = c.sb([128, G, 16], F32, "Cr", es)
        Ci = c.sb([128, G, 16], F32, "Ci", es)
        cin = c.sb([128, 128], F32, "cin", es)
        cps_ = c.ps("cps", es)
        for src, dst in ((cre, Cr), (cim, Ci)):
            for ct in range(NCT):
                c.dma(c.SP, cin[:, 0:64], src[ct * 128:(ct + 1) * 128, :], writes=[cin])
                c.dma(c.SP, cin[:, 64:128], src[ct * 128:(ct + 1) * 128, :], writes=[cin])
                c.op(c.PE, lambda: nc.tensor.transpose(out=cps_[:, 0:128], in_=cin[:, :], identity=ident[:, :]), reads=[cin, ident], writes=[cps_])
                c.op(A_, lambda dst=dst, ct=ct: nc.scalar.copy(out=dst[:, ct * 8:(ct + 1) * 8, :], in_=cps_[:, 0:128].rearrange("p (g c) -> p g c", g=8)),
                     reads=[cps_], writes=[dst])
        dcol = load_cols(c, dsk, NCT, "dcol", es)
        c.barrier()

        def views(t, n1, n2):
            return [[Buf(t.t[:, a * n2 + b, :]) for b in range(n2)] for a in range(n1)]
        g0pad = [c.sb([128, 128], F32, "g0pad", es) for _ in range(8)]
        Gp = [c.sb([128, 128], F32, "Gp", es) for _ in range(8)]
        Qp = [c.sb([128, 128], F32, "Qp", es) for _ in range(8)]
        tbs = [c.sb([128, 16], F32, "tbs", es) for _ in range(8)]
        mtmp = [c.sb([128, 128], F32, "mtmp", es) for _ in range(8)]
        lhs1_t = c.sb([128, 64, 128], BF16, "lhs1", es)
        lhsq_t = c.sb([128, 72, 128], BF16, "lhsq", es)
        ktb_t = c.sb([128, 8, 128], BF16, "ktb", es)
        msc_t = c.sb([128, 8 * NL, 128], BF16, "msc", es)
        lhs1 = views(lhs1_t, 8, 8)
        lhsq = views(lhsq_t, 8, 9)
        ktb = views(ktb_t, 1, 8)[0]
        msc = views(msc_t, 8, NL)
        u_ = c.sb([128, T], BF16, "ub", es)
        yo = c.sb([128, T], BF16, "y5o", es)
        NSS = 3
        Hf = [[c.sb([128, M], F32, "Hf", es) for _ in range(2)] for _ in range(NSS)]
        Hb = [[c.sb([128, M], BF16, "Hb", es) for _ in range(2)] for _ in range(NSS)]
        Hfin_t = c.sb([128, 8, M + 1], BF16, "Hfin", es)
        c.op(P_, lambda: nc.gpsimd.memset(Hfin_t[:, :, 0:1], 0.0), writes=[Hfin_t])
        Hfin = [Buf(Hfin_t.t[:, gi, :]) for gi in range(8)]
        for hb_ in Hfin:
            hb_.wr = Hfin_t.wr
        pb = [c.ps("s5ps", es) for _ in range(7)]
        tps, kps, sps, ops_ = pb[0:2], pb[2], pb[1:7], pb[0:3]
        x2b = [c.sb([128, 512], F32, "x2b", es) for _ in range(3)]
        inb = [c.sb([128, 512], F32, "inb", es) for _ in range(3)]
        sgb = [c.sb([128, 512], F32, "sgb", es) for _ in range(3)]
        for bufz in tuple(Gp) + tuple(Qp) + tuple(g0pad):
            c.op(P_, lambda bufz=bufz: nc.gpsimd.memset(bufz[:, :], 0.0), writes=[bufz])
        ti = 0
        oi = 0
        ni = 0
        uv = u_.t[:, :].rearrange("p (m i) -> p i m", i=8)
        yv = yo.t[:, :].rearrange("p (m i) -> p i m", i=8)
        for ct in range(NCT):
            c.dma(c.SP, u_[:, :], uT[ct * 128:(ct + 1) * 128, :], writes=[u_])
            for k in range(8):
                for gi in range(8):
                    g = ct * 8 + gi
                    cs = slice(gi * 16, gi * 16 + 16)
                    gp = g0pad[gi] if k == 0 else Gp[gi]
                    tb_ = tbs[ni % 8]
                    ni += 1
                    vts(tb_[:, :], Bbi[:, g, :], de[k][:, g:g + 1], reads=[Bbi, de[k]], writes=[tb_])
                    vstt(gp[:, cs], Bbr[:, g, :], ga[k][:, g:g + 1], tb_[:, :], ALU.mult, ALU.add, reads=[Bbr, ga[k], tb_], writes=[gp])
                    tp = tps[ti % 2]
                    ti += 1
                    c.op(c.PE, lambda gp=gp, tp=tp: nc.tensor.transpose(out=tp[:, 0:128], in_=gp[:, :], identity=ident[:, :]), reads=[gp, ident], writes=[tp])
                    c.op(A_, lambda tp=tp, gi=gi, k=k: nc.scalar.copy(out=lhs1[gi][k][:, :], in_=tp[:, 0:128]), reads=[tp], writes=[lhs1[gi][k]])
            for k in range(9):
                for gi in range(8):
                    g = ct * 8 + gi
                    cs = slice(gi * 16, gi * 16 + 16)
                    qf = Qp[gi]
                    tb_ = tbs[ni % 8]
                    ni += 1
                    vts(tb_[:, :], Ci[:, g, :], be[k][:, g:g + 1], reads=[Ci, be[k]], writes=[tb_])
                    vstt(qf[:, cs], Cr[:, g, :], al[k][:, g:g + 1], tb_[:, :], ALU.mult, ALU.subtract, reads=[Cr, al[k], tb_], writes=[qf])
                    c.op(A_, lambda qf=qf, gi=gi, k=k: nc.scalar.copy(out=lhsq[gi][k][:, :], in_=qf[:, :]), reads=[qf], writes=[lhsq[gi][k]])
                    if k < 8:
                        c.op(c.PE, lambda gi=gi, qf=qf: nc.tensor.matmul(kps[:, 0:128], lhsT=g0pad[gi][:, :], rhs=qf[:, :],
                                                                        start=(gi == 0), stop=(gi == 7)), reads=[g0pad[gi], qf], writes=[kps])
                if k < 8:
                    if k == 0:
                        vstt(ktb[0][:, :], ident[:, :], dcol[:, ct:ct + 1], kps[:, 0:128], ALU.mult, ALU.add, reads=[ident, dcol, kps], writes=[ktb[0]])
                    else:
                        c.op(A_, lambda k=k: nc.scalar.copy(out=ktb[k][:, :], in_=kps[:, 0:128]), reads=[kps], writes=[ktb[k]])
            for l in range(NL):
                for gi in range(8):
                    g = ct * 8 + gi
                    mt = mtmp[ni % 8]
                    ni += 1
                    vts(mt[:, :], ident[:, :], Sr[l][:, g:g + 1], reads=[ident, Sr[l]], writes=[mt])
                    vstt(msc[gi][l][:, :], swp[:, :], So[l][:, g:g + 1], mt[:, :], ALU.mult, ALU.add, reads=[swp, So[l], mt], writes=[msc[gi][l]])
            for g2 in range(0, 8, NSS):
                ng_ = min(NSS, 8 - g2)
                cur = [0] * NSS
                for s_ in range(ng_):
                    gi = g2 + s_
                    for mc in range(NMC):
                        sp_ = sps[2 * s_ + mc % 2]
                        for i in range(8):
                            c.op(c.PE, lambda sp_=sp_, i=i, mc=mc, gi=gi: nc.tensor.matmul(sp_[:, :], lhsT=lhs1[gi][7 - i][:, :],
                                                                                           rhs=uv[:, i, mc * 512:(mc + 1) * 512], start=(i == 0), stop=(i == 7)),
                                 reads=[lhs1[gi][7 - i], u_], writes=[sp_])
                        c.op(V, lambda sp_=sp_, mc=mc, s_=s_: nc.vector.tensor_copy(out=Hf[s_][0][:, mc * 512:(mc + 1) * 512], in_=sp_[:, :]), reads=[sp_], writes=[Hf[s_][0]])
                        c.op(A_, lambda mc=mc, s_=s_: nc.scalar.copy(out=Hb[s_][0][:, mc * 512:(mc + 1) * 512], in_=Hf[s_][0][:, mc * 512:(mc + 1) * 512]),
                             reads=[Hf[s_][0]], writes=[Hb[s_][0]])
                for l in range(NL):
                    sh = 1 << l
                    last = (l == NL - 1)
                    for s_ in range(ng_):
                        gi = g2 + s_
                        cu = cur[s_]
                        nx = 1 - cu
                        Hfc, Hfn, Hbc, Hbn = Hf[s_][cu], Hf[s_][nx], Hb[s_][cu], Hb[s_][nx]
                        if last:
                            c.op(P_, lambda Hfc=Hfc, sh=sh, gi=gi: nc.gpsimd.tensor_copy(out=Hfin[gi][:, 1:1 + sh], in_=Hfc[:, 0:sh]), reads=[Hfc], writes=[Hfin[gi]])
                        else:
                            c.op(P_, lambda Hfc=Hfc, Hfn=Hfn, sh=sh: nc.gpsimd.tensor_copy(out=Hfn[:, 0:sh], in_=Hfc[:, 0:sh]), reads=[Hfc], writes=[Hfn])
                            c.op(P_, lambda Hfc=Hfc, Hbn=Hbn, sh=sh: nc.gpsimd.tensor_copy(out=Hbn[:, 0:sh], in_=Hfc[:, 0:sh]), reads=[Hfc], writes=[Hbn])
                        pos = sh
                        while pos < M:
                            n = min(512, M - pos)
                            sp_ = sps[2 * s_ + (pos // 512 + l) % 2]
                            c.op(c.PE, lambda sp_=sp_, Hbc=Hbc, pos=pos, n=n, sh=sh, gi=gi, l=l: nc.tensor.matmul(sp_[:, 0:n], lhsT=msc[gi][l][:, :],
                                                                                                                 rhs=Hbc[:, pos - sh:pos - sh + n], start=True, stop=True),
                                 reads=[msc[gi][l], Hbc], writes=[sp_])
                            if last:
                                c.op(V, lambda sp_=sp_, Hfc=Hfc, pos=pos, n=n, gi=gi: nc.vector.tensor_tensor(out=Hfin[gi][:, 1 + pos:1 + pos + n], in0=sp_[:, 0:n],
                                                                                                           in1=Hfc[:, pos:pos + n], op=ALU.add),
                                     reads=[sp_, Hfc], writes=[Hfin[gi]])
                            else:
                                c.op(V, lambda sp_=sp_, Hfc=Hfc, Hfn=Hfn, pos=pos, n=n: nc.vector.tensor_tensor(out=Hfn[:, pos:pos + n], in0=sp_[:, 0:n],
                                                                                                             in1=Hfc[:, pos:pos + n], op=ALU.add),
                                     reads=[sp_, Hfc], writes=[Hfn])
                                c.op(A_, lambda Hfn=Hfn, Hbn=Hbn, pos=pos, n=n: nc.scalar.copy(out=Hbn[:, pos:pos + n], in_=Hfn[:, pos:pos + n]),
                                     reads=[Hfn], writes=[Hbn])
                            pos += n
                        cur[s_] = nx
            for mc in range(NMC):
                msl = slice(mc * 512, (mc + 1) * 512)
                for j in range(8):
                    op_ = ops_[oi % 3]
                    x2_, in_, sg_ = x2b[oi % 3], inb[oi % 3], sgb[oi % 3]
                    oi += 1
                    nmm = 8 + j + 1
                    n_ = 0
                    for gi in range(8):
                        c.op(c.PE, lambda gi=gi, j=j, op_=op_, msl=msl, n_=n_: nc.tensor.matmul(op_[:, :], lhsT=lhsq[gi][j + 1][:, :], rhs=Hfin[gi][:, msl],
                                                                                               start=(n_ == 0), stop=False), reads=[lhsq[gi][j + 1], Hfin[gi]], writes=[op_])
                        n_ += 1
                    for i in range(j + 1):
                        c.op(c.PE, lambda i=i, j=j, op_=op_, msl=msl, n_=n_, nmm=nmm: nc.tensor.matmul(op_[:, :], lhsT=ktb[j - i][:, :], rhs=uv[:, i, msl],
                                                                                                       start=False, stop=(n_ == nmm - 1)), reads=[ktb[j - i], u_], writes=[op_])
                        n_ += 1
                    c.op(A_, lambda op_=op_, x2_=x2_: nc.scalar.activation(out=x2_[:, :], in_=op_[:, :], func=AF.Square), reads=[op_], writes=[x2_])
                    c.op(P_, lambda x2_=x2_: nc.gpsimd.tensor_scalar(out=x2_[:, :], in0=x2_[:, :], scalar1=0.044715, scalar2=1.0, op0=ALU.mult, op1=ALU.add), writes=[x2_])
                    vtt(in_[:, :], x2_[:, :], op_[:, :], ALU.mult, reads=[x2_, op_], writes=[in_])
                    c.op(A_, lambda in_=in_, sg_=sg_: nc.scalar.activation(out=sg_[:, :], in_=in_[:, :], func=AF.Sigmoid, scale=1.5957691216), reads=[in_], writes=[sg_])
                    vtt(yv[:, j, msl], sg_[:, :], op_[:, :], ALU.mult, reads=[sg_, op_], writes=[yo])
            c.dma(c.ST, y5T(ct)[:, :], yo[:, :], reads=[yo])
            c.barrier()
            if after_ct is not None:
                after_ct(ct)
        c.barrier()


def build_l3(T=S, ngrp=32):
    nc = bass.Bass("TRN2", target_bir_lowering=False)
    A = declare(nc, S5_IN(ngrp), "ExternalInput")
    uT = nc.dram_tensor("uT", [ngrp * 16, T], BF16, kind="ExternalInput").ap()
    y5T = nc.dram_tensor("y5T", [ngrp * 16, T], BF16, kind="ExternalOutput").ap()
    with ExitStack() as es:
        c = Ctx(nc, es)
        ones, ident = make_consts(c)
        s5_body(c, nc, ones, ident, A, uT, (lambda k: y5T[k * 128:(k + 1) * 128, :]), T, ngrp)
        c.barrier()
    return nc


def l2_inputs(inp, b, th, yfull, ntok=4096):
    f = lambda a: np.ascontiguousarray(a, dtype=np.float32)
    ts = slice(th * ntok, (th + 1) * ntok)
    return {"yf": np.ascontiguousarray(yfull[:, ts]), "x": f(inp["x"][b][ts]), "cvec": f(inp["c"][b]),
            "adaw0": f(inp["ada_w"][0]), "adab0": f(inp["ada_b"][0]), "adaw1": f(inp["ada_w"][1]), "adab1": f(inp["ada_b"][1]),
            "ng1": f(inp["norm_g"][1]), "wout": f(inp["w_out_even"][0]), "win1": f(inp["w_in_odd"][0])}


def l3_inputs(inp, hf, u_half, G=32):
    f = lambda a: np.ascontiguousarray(a, dtype=np.float32)
    gs = slice(hf * G, (hf + 1) * G)
    return {"uT": (None if u_half is None else np.ascontiguousarray(u_half)), "lre": f(inp["s5_lambda_re"][0][gs]), "lim": f(inp["s5_lambda_im"][0][gs]),
            "ldt": f(inp["s5_log_dt"][0][gs]), "bre": f(inp["s5_b_re"][0][gs]), "bim": f(inp["s5_b_im"][0][gs]),
            "cre": f(inp["s5_c_re"][0][gs].reshape(G * 16, 64)), "cim": f(inp["s5_c_im"][0][gs].reshape(G * 16, 64)),
            "dsk": f(inp["s5_d"][0][hf * G * 16:(hf + 1) * G * 16])}


def l4_inputs(inp, b, y5full_tok, sgT, x1):
    f = lambda a: np.ascontiguousarray(a, dtype=np.float32)
    return {"y5": np.ascontiguousarray(y5full_tok), "sgT": f(sgT), "x1": f(x1), "cvec": f(inp["c"][b]),
            "adaw1": f(inp["ada_w"][1]), "adab1": f(inp["ada_b"][1]), "gluw": f(inp["glu_w"][0]), "glub": f(inp["glu_b"][0]),
            "wout": f(inp["w_out_odd"][0])}


RG = [[0, 1], [2, 3], [4, 5], [6, 7]]

F_IN = [("xh", [S, 512], F32), ("adaw0g", [D, 512], F32), ("adab0g", [512], F32), ("woutp", [2048, 512], F32),
        ("adaw1", [D, 2048], F32), ("adab1", [2048], F32), ("ng1", [D], F32), ("adaw1g", [D, 512], F32), ("adab1g", [512], F32),
        ("win1h", [D, 1024], F32), ("gluwh", [D, 512], F32), ("glubh", [512], F32), ("wout1h", [D, 512], F32)]


def ag_issue(c, nc, sn, rc, toks=()):
    c.POOL.wait(list(toks))
    nc.gpsimd.collective_compute("AllGather", ALU.bypass, replica_groups=RG, ins=[sn.opt()], outs=[rc.opt()]).then_inc(c.ccs)
    c.ncc += 1


def ag_start(c, nc, snds, rcvs):
    c.barrier()
    for sn, rc in zip(snds, rcvs):
        nc.gpsimd.collective_compute("AllGather", ALU.bypass, replica_groups=RG, ins=[sn.opt()], outs=[rc.opt()]).then_inc(c.ccs)
        c.ncc += 1


def ag_wait(c):
    for q in c.queues:
        q.eng.wait_ge(c.ccs, c.ncc)


def p2_body(c, nc, ones, A, y_rcv, x1_snd, after_setup, x1_rcv=None):
    with ExitStack() as es:
        gate = c.sb([128, 512], F32, "gate0", es)
        Wo = c.sb([128, 16, 512], BF16, "Wo", es)
        with ExitStack() as e0:
            stage = [c.sb([128, 8, 512], F32, "wst", e0) for _ in range(2)]
            g0t = ada_mod_bc(c, e0, ones, A["cvec"], A["adaw0g"], A["adab0g"], 0, 512, stage[0], "g0t")
            c.op(c.DVE, lambda: nc.vector.tensor_copy(out=gate[:, :], in_=g0t[:, :]), reads=[g0t], writes=[gate])
            load_w_bf16(c, A["woutp"], 16, 512, stage, "Wo", dst=Wo)
            c.barrier()
        after_setup()
        ybufs = [c.sb([128, 16, 1024], BF16, "ych", es) for _ in range(2)]
        xb = [c.sb([128, 512], F32, "xt", es) for _ in range(4)]
        tb = [c.sb([128, 512], F32, "tmp", es) for _ in range(2)]
        ob = [c.sb([128, 512], F32, "x1t", es) for _ in range(3)]
        pss = [c.ps("p2ps", es) for _ in range(4)]

        def load_y(c2):
            ych = ybufs[c2 % 2]
            for k in range(8):
                c.dma(c.SP, ych[:, 2 * k:2 * k + 2, :], y_rcv[k].rearrange("(r p) t -> p r t", p=128)[:, :, c2 * 1024:(c2 + 1) * 1024], writes=[ych])
        load_y(0)
        stoks = []
        for i in range(S // 128):
            c2, off = i // 8, (i % 8) * 128
            ych = ybufs[c2 % 2]
            if i % 8 == 0 and c2 + 1 < S // 1024:
                load_y(c2 + 1)
            xt, tm, o, ps = xb[i % 4], tb[i % 2], ob[i % 3], pss[i % 4]
            c.dma(c.SP, xt[:, :], A["xh"][i * 128:(i + 1) * 128, :], writes=[xt])
            for kt in range(16):
                c.op(c.PE, lambda kt=kt, off=off, ps=ps, ych=ych: nc.tensor.matmul(ps[:, :], lhsT=ych[:, kt, off:off + 128], rhs=Wo[:, kt, :],
                                                                                   start=(kt == 0), stop=(kt == 15)), reads=[ych, Wo], writes=[ps])
            c.op(c.DVE, lambda tm=tm, ps=ps: nc.vector.tensor_tensor(out=tm[:, :], in0=ps[:, :], in1=gate[:, :], op=ALU.mult), reads=[ps, gate], writes=[tm])
            c.op(c.DVE, lambda o=o, tm=tm, xt=xt: nc.vector.tensor_tensor(out=o[:, :], in0=tm[:, :], in1=xt[:, :], op=ALU.add), reads=[tm, xt], writes=[o])
            stoks.append(c.dma(c.ACT, x1_snd[i // 8][(i % 8) * 128:(i % 8 + 1) * 128, :], o[:, :], reads=[o]))
            if x1_rcv is not None and i % 8 == 7:
                ag_issue(c, nc, x1_snd[i // 8], x1_rcv[i // 8], stoks)
                stoks = []
        c.barrier()


def p3_body(c, nc, ones, ident, A, x1_rcv, uT, sgT, after_setup):
    with ExitStack() as es:
        gmod = c.sb([128, D], F32, "gmod", es)
        shift = c.sb([128, D], F32, "shift", es)
        W1 = c.sb([128, 8, 1024], BF16, "W1", es)
        with ExitStack() as e0:
            stage = [c.sb([128, 8, 512], F32, "wst", e0) for _ in range(2)]
            mod1 = ada_mod_bc(c, e0, ones, A["cvec"], A["adaw1"], A["adab1"], 0, 2048, stage[1], "mod1")
            ngb = c.sb([128, D], F32, "ngb", e0)
            c.dma(c.SP, ngb[:, :], A["ng1"].partition_broadcast(128), writes=[ngb])
            c.op(c.DVE, lambda: nc.vector.scalar_tensor_tensor(out=gmod[:, :], in0=mod1[:, 1024:2048], scalar=1.0, in1=ngb[:, :],
                                                               op0=ALU.add, op1=ALU.mult), reads=[mod1, ngb], writes=[gmod])
            c.op(c.DVE, lambda: nc.vector.tensor_copy(out=shift[:, :], in_=mod1[:, 0:1024]), reads=[mod1], writes=[shift])
            load_w_bf16(c, A["win1h"], 8, 1024, stage, "W1", dst=W1)
            c.barrier()
        after_setup()
        xbufs = [c.sb([128, D], F32, "x1f", es) for _ in range(4)]

        def load_x1(i):
            xt = xbufs[i % 4]
            for r in range(2):
                c.dma(c.SP, xt[:, r * 512:(r + 1) * 512], x1_rcv[i // 8][r * 1024 + (i % 8) * 128: r * 1024 + (i % 8 + 1) * 128, :], writes=[xt])
        hbufs = [c.sb([128, D], F32, "hb", es) for _ in range(4)]
        small = [(c.sb([128, 1], F32, "ssq", es), c.sb([128, 1], F32, "rstd", es)) for _ in range(4)]
        hTs = [c.sb([128, 8, 512], BF16, "hT", es) for _ in range(2)]
        trps = [c.ps("trps", es) for _ in range(2)]
        mmps = [c.ps("mmps", es) for _ in range(4)]
        outf = [c.sb([128, 512], F32, "outf", es) for _ in range(3)]
        outb = [c.sb([128, 512], BF16, "outb", es) for _ in range(3)]
        mi = fi = bi_ = 0

        def ntile3(i, part):
            if part == 1 and i + 2 < S // 128:
                load_x1(i + 2)
            norm_transpose_tile(c, None, xbufs, hbufs, i, gmod, shift, ident, trps, hTs[(i // 4) % 2], (i % 4) * 128, small, part)
        for ch in range(S // 512):
            tsl = slice(ch * 512, (ch + 1) * 512)
            hT = hTs[ch % 2]
            if ch == 0:
                load_x1(0)
                load_x1(1)
                for i4 in range(4):
                    ntile3(i4, 1)
            for i4 in range(4):
                ntile3(ch * 4 + i4, 2)
            for nt in range(8):
                if nt % 2 == 0 and ch + 1 < S // 512:
                    ntile3((ch + 1) * 4 + nt // 2, 1)
                pm = mmps[mi % 4]
                mi += 1
                for kt in range(8):
                    c.op(c.PE, lambda kt=kt, nt=nt, pm=pm: nc.tensor.matmul(pm[:, :], lhsT=W1[:, kt, nt * 128:(nt + 1) * 128], rhs=hT[:, kt, :],
                                                                            start=(kt == 0), stop=(kt == 7)), reads=[W1, hT], writes=[pm])
                if nt < 4:
                    o = outb[bi_ % 3]
                    bi_ += 1
                    c.op(c.DVE, lambda o=o, pm=pm: nc.vector.tensor_copy(out=o[:, :], in_=pm[:, :]), reads=[pm], writes=[o])
                    c.dma(c.ST, uT[nt * 128:(nt + 1) * 128, tsl], o[:, :], reads=[o])
                else:
                    o = outf[fi % 3]
                    fi += 1
                    c.op(c.ACT, lambda o=o, pm=pm: nc.scalar.activation(out=o[:, :], in_=pm[:, :], func=AF.Silu), reads=[pm], writes=[o])
                    c.dma(c.ST, sgT[(nt - 4) * 128:(nt - 3) * 128, tsl], o[:, :], reads=[o])
        c.barrier()


def p5_body(c, nc, A, y5_rcv, y5_own, sgT, yg_snd, after_setup, yg_rcv=None):
    with ExitStack() as es:
        Wg = c.sb([128, 8, 512], BF16, "Wg", es)
        gb = load_cols(c, A["glubh"], 4, "glub", es)
        with ExitStack() as e0:
            stage = [c.sb([128, 8, 512], F32, "wst", e0) for _ in range(2)]
            load_w_bf16(c, A["gluwh"], 8, 512, stage, "Wg", dst=Wg)
            c.barrier()
        after_setup()
        ybufs = [c.sb([128, 8, 512], BF16, "y5c", es) for _ in range(2)]
        yown = [c.sb([128, 4, 512], BF16, "y5o", es) for _ in range(2)]
        sbufs = [c.sb([128, 4, 512], F32, "sgc", es) for _ in range(2)]
        ygs = [c.sb([128, 4, 512], BF16, "yg", es) for _ in range(2)]
        sig = [c.sb([128, 512], F32, "sig", es) for _ in range(2)]
        tt = [c.sb([128, 512], F32, "tt", es) for _ in range(2)]
        zps = [c.ps("zps", es) for _ in range(3)]
        sv = sgT.rearrange("(k p) t -> p k t", p=128)
        zi = 0
        def load5(ch):
            tsl = slice(ch * 512, (ch + 1) * 512)
            ych, yo_, sch = ybufs[ch % 2], yown[ch % 2], sbufs[ch % 2]
            for k in range(4):
                c.dma(c.SP, ych[:, 2 * k:2 * k + 2, :], y5_rcv[k].rearrange("(r p) t -> p r t", p=128)[:, :, tsl], writes=[ych])
                c.dma(c.SP, yo_[:, k, :], y5_own[k][:, tsl], writes=[yo_])
            c.dma(c.SP, sch[:, :, :], sv[:, :, tsl], writes=[sch])
        load5(0)
        stoks = []
        for ch in range(S // 512):
            tsl = slice(ch * 512, (ch + 1) * 512)
            ych, yo_, sch, yg = ybufs[ch % 2], yown[ch % 2], sbufs[ch % 2], ygs[ch % 2]
            if ch + 1 < S // 512:
                load5(ch + 1)
            for nt in range(4):
                zp, sg_, t_ = zps[zi % 3], sig[zi % 2], tt[zi % 2]
                zi += 1
                for kt in range(8):
                    c.op(c.PE, lambda kt=kt, nt=nt, zp=zp, ych=ych: nc.tensor.matmul(zp[:, :], lhsT=Wg[:, kt, nt * 128:(nt + 1) * 128], rhs=ych[:, kt, :],
                                                                                     start=(kt == 0), stop=(kt == 7)), reads=[Wg, ych], writes=[zp])
                c.op(c.ACT, lambda zp=zp, sg_=sg_, nt=nt: nc.scalar.activation(out=sg_[:, :], in_=zp[:, :], func=AF.Sigmoid, bias=gb[:, nt:nt + 1]),
                     reads=[zp, gb], writes=[sg_])
                c.op(c.DVE, lambda sg_=sg_, t_=t_, nt=nt, yo_=yo_: nc.vector.tensor_tensor(out=t_[:, :], in0=sg_[:, :], in1=yo_[:, nt, :], op=ALU.mult),
                     reads=[sg_, yo_], writes=[t_])
                c.op(c.DVE, lambda t_=t_, nt=nt, yg=yg, sch=sch: nc.vector.tensor_tensor(out=yg[:, nt, :], in0=t_[:, :], in1=sch[:, nt, :], op=ALU.mult),
                     reads=[t_, sch], writes=[yg])
            j_ = ch // 4
            stoks.append(c.dma(c.ST, yg_snd[j_].rearrange("(k p) t -> p k t", p=128)[:, :, (ch % 4) * 512:(ch % 4 + 1) * 512], yg[:, :, :], reads=[yg]))
            if yg_rcv is not None and ch % 4 == 3:
                ag_issue(c, nc, yg_snd[j_], yg_rcv[j_], stoks)
                stoks = []
        c.barrier()


def p6_body(c, nc, ones, A, yg_rcv, x1_own, out, after_setup):
    with ExitStack() as es:
        gate = c.sb([128, 512], F32, "gate1", es)
        Wo = c.sb([128, 8, 512], BF16, "Wo1", es)
        with ExitStack() as e0:
            stage = [c.sb([128, 8, 512], F32, "wst", e0) for _ in range(2)]
            g1t = ada_mod_bc(c, e0, ones, A["cvec"], A["adaw1g"], A["adab1g"], 0, 512, stage[0], "g1t")
            c.op(c.DVE, lambda: nc.vector.tensor_copy(out=gate[:, :], in_=g1t[:, :]), reads=[g1t], writes=[gate])
            load_w_bf16(c, A["wout1h"], 8, 512, stage, "Wo1", dst=Wo)
            c.barrier()
        after_setup()
        ybufs = [c.sb([128, 8, 2048], BF16, "ygc", es) for _ in range(2)]
        xb = [c.sb([128, 512], F32, "xt", es) for _ in range(4)]
        tb = [c.sb([128, 512], F32, "tmp", es) for _ in range(2)]
        ob = [c.sb([128, 512], F32, "x2t", es) for _ in range(3)]
        pss = [c.ps("p6ps", es) for _ in range(4)]

        def load_y(c4):
            ych = ybufs[c4 % 2]
            c.dma(c.SP, ych[:, :, :], yg_rcv[c4].rearrange("(k p) t -> p k t", p=128), writes=[ych])
        load_y(0)
        for i in range(S // 128):
            c4, off = i // 16, (i % 16) * 128
            ych = ybufs[c4 % 2]
            if i % 16 == 0 and c4 + 1 < 4:
                load_y(c4 + 1)
            xt, tm, o, ps = xb[i % 4], tb[i % 2], ob[i % 3], pss[i % 4]
            c.dma(c.SP, xt[:, :], x1_own[i // 8][(i % 8) * 128:(i % 8 + 1) * 128, :], writes=[xt])
            for kt in range(8):
                c.op(c.PE, lambda kt=kt, off=off, ps=ps, ych=ych: nc.tensor.matmul(ps[:, :], lhsT=ych[:, kt, off:off + 128], rhs=Wo[:, kt, :],
                                                                                   start=(kt == 0), stop=(kt == 7)), reads=[ych, Wo], writes=[ps])
            c.op(c.DVE, lambda tm=tm, ps=ps: nc.vector.tensor_tensor(out=tm[:, :], in0=ps[:, :], in1=gate[:, :], op=ALU.mult), reads=[ps, gate], writes=[tm])
            c.op(c.POOL, lambda o=o, tm=tm, xt=xt: nc.gpsimd.tensor_tensor(out=o[:, :], in0=tm[:, :], in1=xt[:, :], op=ALU.add), reads=[tm, xt], writes=[o])
            c.dma(c.ST, out[i * 128:(i + 1) * 128, :], o[:, :], reads=[o])
        c.barrier()


def build_fused():
    nc = bass.Bass("TRN2", target_bir_lowering=False)
    A = declare(nc, L1_IN, "ExternalInput")
    A.update(declare(nc, L1_SCR, "Internal"))
    A.update(declare(nc, F_IN, "ExternalInput"))
    A.update(declare(nc, S5_IN(32), "ExternalInput"))
    out = nc.dram_tensor("out", [S, 512], F32, kind="ExternalOutput").ap()
    I = declare(nc, [("uT", [512, S], BF16), ("sgT", [512, S], F32)], "Internal")

    def chunks(name, n, shape, dt_):
        d = declare(nc, [("%s%d" % (name, i), shape, dt_) for i in range(n)], "Internal")
        return [d["%s%d" % (name, i)] for i in range(n)]
    y_snd = chunks("y_snd", 8, [128, S], BF16)
    y_rcv = chunks("y_rcv", 8, [256, S], BF16)
    x1_snd = chunks("x1_snd", 8, [1024, 512], F32)
    x1_rcv = chunks("x1_rcv", 8, [2048, 512], F32)
    y5_snd = chunks("y5_snd", 4, [128, S], BF16)
    y5_rcv = chunks("y5_rcv", 4, [256, S], BF16)
    yg_snd = chunks("yg_snd", 4, [512, 2048], BF16)
    yg_rcv = chunks("yg_rcv", 4, [1024, 2048], BF16)
    with ExitStack() as es:
        c = Ctx(nc, es)
        c.ccs = c.sem("ccs")
        c.ncc = 0
        ones, ident = make_consts(c)
        w = lambda: ag_wait(c)

        def after_lru():
            for k in range(4):
                ag_issue(c, nc, y_snd[k], y_rcv[k])
        l1_body(c, nc, ones, ident, A, (lambda k: y_snd[k]), after_lru=after_lru)
        ag_start(c, nc, y_snd[4:], y_rcv[4:])
        p2_body(c, nc, ones, A, y_rcv, x1_snd, w, x1_rcv)
        p3_body(c, nc, ones, ident, A, x1_rcv, I["uT"], I["sgT"], w)
        s5_body(c, nc, ones, ident, A, I["uT"], (lambda k: y5_snd[k]), after_ct=(lambda ct: ag_issue(c, nc, y5_snd[ct], y5_rcv[ct])))
        p5_body(c, nc, A, y5_rcv, y5_snd, I["sgT"], yg_snd, w, yg_rcv)
        p6_body(c, nc, ones, A, yg_rcv, x1_snd, out, w)
        c.barrier()
    return nc


def fused_inputs(inp, b, h):
    f = lambda a: np.ascontiguousarray(a, dtype=np.float32)
    m = l1_inputs(inp, b, h)
    cs = slice(h * 512, (h + 1) * 512)
    wo = inp["w_out_even"][0]
    perm = np.concatenate([np.arange(128) + ((r * 512 + k * 128) if k < 4 else (1024 + r * 512 + (k - 4) * 128))
                           for k in range(8) for r in range(2)])
    perm4 = np.concatenate([np.arange(128) + r * 512 + k * 128 for k in range(4) for r in range(2)])
    w1 = inp["w_in_odd"][0]
    m.update({
        "xh": f(inp["x"][b][:, cs]), "adaw0g": f(inp["ada_w"][0][:, 2048:3072][:, cs]), "adab0g": f(inp["ada_b"][0][2048:3072][cs]),
        "woutp": f(wo[perm][:, cs]), "adaw1": f(inp["ada_w"][1][:, :2048]), "adab1": f(inp["ada_b"][1][:2048]), "ng1": f(inp["norm_g"][1]),
        "adaw1g": f(inp["ada_w"][1][:, 2048:3072][:, cs]), "adab1g": f(inp["ada_b"][1][2048:3072][cs]),
        "win1h": f(np.concatenate([w1[:, :1024][:, cs], w1[:, 1024:][:, cs]], axis=1)),
        "gluwh": f(inp["glu_w"][0][perm4][:, cs]), "glubh": f(inp["glu_b"][0][cs]), "wout1h": f(inp["w_out_odd"][0][:, cs]),
    })
    s5 = l3_inputs(inp, h, None)
    s5.pop("uT")
    m.update(s5)
    return m


def kernel_fused(**inputs):
    inp = {k: np.asarray(v) for k, v in inputs.items()}
    cores = [(b, h) for b in range(4) for h in range(2)]
    r = _run(_get("fused", build_fused), [fused_inputs(inp, b, h) for b, h in cores])
    out = np.empty((4, S, D), np.float32)
    for b, h in cores:
        out[b, :, h * 512:(h + 1) * 512] = np.asarray(r[2 * b + h]["out"])
    return out


def kernel(**inputs):
    return kernel_fused(**inputs)
```
